# Optimizing a Trainium2 kernel written in Bass

```python
import jax
import jax.numpy as jnp
from jax import lax
import numpy as np


D_MODEL = 1024
BATCH = 1
SEQ = 16384
DEPTH = 1
DEC_BATCH = 32
DEC_SEQ = 32
PAST_LEN = 2048

CHUNK = 64
BAND_CHUNKS = 8
BAND_PAST = BAND_CHUNKS * CHUNK
BAND = BAND_PAST + CHUNK
N_HEADS_A = 8
HEAD_DIM_A = 64
D_A = N_HEADS_A * HEAD_DIM_A
REL_CLIP = 128
N_HEADS_B = 4
DK_HEAD_B = 128
DV_HEAD_B = 256
DK_B = N_HEADS_B * DK_HEAD_B
DV_B = N_HEADS_B * DV_HEAD_B
GATE_RANK = 16
GATE_TEMP = 16.0
D_FF = ((8 * D_MODEL // 3 + 255) // 256) * 256
EPS = 1e-6
SPLIT_WIDTHS = (D_A, D_A, D_A, DK_B, DK_B, DV_B, DV_B, GATE_RANK, D_MODEL, D_MODEL)
D_IN = sum(SPLIT_WIDTHS)

kernel_name = 'streaming_hybrid_bandattn_gla_step'


def _rmsnorm(x, g):
    xf = x.astype(jnp.float32)
    xf = xf * lax.rsqrt(jnp.mean(xf * xf, axis=-1, keepdims=True) + EPS)
    return (xf * g.astype(jnp.float32)).astype(x.dtype)


def _rel_bias(table, dist):
    return table[:, jnp.clip(dist, -REL_CLIP, REL_CLIP) + REL_CLIP]


def _band_attention_prompt(q, k, v, rel_table):
    B, L, H, dh = q.shape
    nc = L // CHUNK
    qc = q.reshape(B, nc, CHUNK, H, dh)
    pad = jnp.zeros((B, BAND_CHUNKS, CHUNK, H, dh), k.dtype)
    kc = jnp.concatenate([pad, k.reshape(B, nc, CHUNK, H, dh)], axis=1)
    vc = jnp.concatenate([pad, v.reshape(B, nc, CHUNK, H, dh)], axis=1)
    k_band = jnp.concatenate([kc[:, o:o + nc] for o in range(BAND_CHUNKS + 1)], axis=2)
    v_band = jnp.concatenate([vc[:, o:o + nc] for o in range(BAND_CHUNKS + 1)], axis=2)
    s = jnp.einsum('bcqhd,bckhd->bchqk', qc, k_band).astype(jnp.float32) * (HEAD_DIM_A ** -0.5)
    qi = jnp.arange(CHUNK)[:, None]
    kk = jnp.arange(BAND)[None, :]
    bias = _rel_bias(rel_table, qi + BAND_PAST - kk).astype(jnp.float32)
    valid = kk[None] >= (BAND_PAST - jnp.arange(nc) * CHUNK)[:, None, None]
    s = jnp.where(valid[None, :, None], s + bias[None, None], -jnp.inf)
    p = jax.nn.softmax(s, axis=-1).astype(v.dtype)
    o = jnp.einsum('bchqk,bckhd->bcqhd', p, v_band)
    return o.reshape(B, L, H * dh)


def _band_attention_sample(q, k_new, v_new, k_cache, v_cache, rel_table):
    Bd, S, H, dh = q.shape
    W = k_cache.shape[1]
    k_all = jnp.concatenate([k_cache, k_new], axis=1)
    v_all = jnp.concatenate([v_cache, v_new], axis=1)
    s = jnp.einsum('bqhd,bkhd->bhqk', q, k_all).astype(jnp.float32) * (HEAD_DIM_A ** -0.5)
    dist = jnp.arange(S)[:, None] + W - jnp.arange(W + S)[None, :]
    s = s + _rel_bias(rel_table, dist).astype(jnp.float32)[None]
    p = jax.nn.softmax(s, axis=-1).astype(v_all.dtype)
    o = jnp.einsum('bhqk,bkhd->bqhd', p, v_all)
    return o.reshape(Bd, S, H * dh)


def _gla(q, k, v, log_a, state0, chunk):
    B, L, H, dk = q.shape
    dv = v.shape[-1]
    nc = L // chunk

    def to_chunks(t):
        return jnp.moveaxis(t.astype(jnp.float32).reshape(B, nc, chunk, H, t.shape[-1]), 1, 0)

    causal = jnp.tril(jnp.ones((chunk, chunk), bool))[None, :, :, None, None]

    def step(S, inp):
        qc, kc, vc, lac = inp
        b = jnp.cumsum(lac, axis=1)
        decay = jnp.exp(jnp.where(causal, b[:, :, None] - b[:, None, :], -jnp.inf))
        attn = jnp.einsum('bthd,bshd,btshd->bhts', qc, kc, decay)
        o = (jnp.einsum('bhts,bshv->bthv', attn, vc)
             + jnp.einsum('bthd,bhdv->bthv', qc * jnp.exp(b), S))
        b_last = b[:, -1]
        S = (jnp.exp(b_last)[..., None] * S
             + jnp.einsum('bshd,bshv->bhdv', kc * jnp.exp(b_last[:, None] - b), vc))
        return S, o

    S, o = lax.scan(step, state0.astype(jnp.float32),
                    (to_chunks(q), to_chunks(k), to_chunks(v), to_chunks(log_a)))
    o = jnp.moveaxis(o, 0, 1).reshape(B, L, H, dv)
    return o, S


def _layer(x, gla_state0, cache_k, cache_v, norm_mix_pre, norm_mix_post, norm_ffn_pre, norm_ffn_post,
           w_in, w_decay_up, b_decay, rel_bias, gla_norm, w_proj_a, w_proj_b, w_out,
           w_ffn_gate, w_ffn_up, w_ffn_down):
    B, L, _ = x.shape
    h = _rmsnorm(x, norm_mix_pre)
    proj = h @ w_in
    split_points = np.cumsum(SPLIT_WIDTHS)[:-1].tolist()
    qa, ka, va, qb, kb, vb, rb, dlr, ga, gb = jnp.split(proj, split_points, axis=-1)
    qa = qa.reshape(B, L, N_HEADS_A, HEAD_DIM_A)
    ka = ka.reshape(B, L, N_HEADS_A, HEAD_DIM_A)
    va = va.reshape(B, L, N_HEADS_A, HEAD_DIM_A)
    if cache_k is None:
        oa = _band_attention_prompt(qa, ka, va, rel_bias)
        keep = min(BAND_PAST, L)
        k_rows, v_rows = ka[:, L - keep:], va[:, L - keep:]
    else:
        oa = _band_attention_sample(qa, ka, va, cache_k, cache_v, rel_bias)
        k_rows, v_rows = ka, va
    log_a = jax.nn.log_sigmoid((dlr @ w_decay_up + b_decay).astype(jnp.float32)) / GATE_TEMP
    chunk = CHUNK if L % CHUNK == 0 else L
    ob, S = _gla((qb * (DK_HEAD_B ** -0.5)).reshape(B, L, N_HEADS_B, DK_HEAD_B),
                 kb.reshape(B, L, N_HEADS_B, DK_HEAD_B),
                 vb.reshape(B, L, N_HEADS_B, DV_HEAD_B),
                 log_a.reshape(B, L, N_HEADS_B, DK_HEAD_B),
                 gla_state0, chunk)
    ob = _rmsnorm(ob, gla_norm).astype(x.dtype).reshape(B, L, DV_B) * jax.nn.silu(rb)
    mix = jax.nn.sigmoid(ga) * (oa @ w_proj_a) + jax.nn.sigmoid(gb) * (ob @ w_proj_b)
    x = x + _rmsnorm(mix @ w_out, norm_mix_post)
    h = _rmsnorm(x, norm_ffn_pre)
    f = (jax.nn.silu(h @ w_ffn_gate) * (h @ w_ffn_up)) @ w_ffn_down
    x = x + _rmsnorm(f, norm_ffn_post)
    return x, k_rows, v_rows, S


def setup_inputs(seed: int = 0) -> dict:
    key = jax.random.key(seed)
    ks = jax.random.split(key, 24)
    f32 = jnp.float32
    past_win = min(BAND_PAST, PAST_LEN)

    def nrm(k, shape, scale):
        return scale * jax.random.normal(k, shape, f32)

    return {
        'x_prompt': nrm(ks[0], (BATCH, SEQ, D_MODEL), 1.0),
        'x_sample': nrm(ks[1], (DEC_BATCH, DEC_SEQ, D_MODEL), 1.0),
        'cache_attn_k': nrm(ks[2], (DEPTH, DEC_BATCH, past_win, N_HEADS_A, HEAD_DIM_A), 1.0),
        'cache_attn_v': nrm(ks[3], (DEPTH, DEC_BATCH, past_win, N_HEADS_A, HEAD_DIM_A), 1.0),
        'state_gla': nrm(ks[4], (DEPTH, DEC_BATCH, N_HEADS_B, DK_HEAD_B, DV_HEAD_B), 0.5),
        'norm_mix_pre': 1.0 + nrm(ks[5], (DEPTH, D_MODEL), 0.05),
        'norm_mix_post': 1.0 + nrm(ks[6], (DEPTH, D_MODEL), 0.05),
        'norm_ffn_pre': 1.0 + nrm(ks[7], (DEPTH, D_MODEL), 0.05),
        'norm_ffn_post': 1.0 + nrm(ks[8], (DEPTH, D_MODEL), 0.05),
        'w_in': nrm(ks[9], (DEPTH, D_MODEL, D_IN), D_MODEL ** -0.5),
        'w_decay_up': nrm(ks[10], (DEPTH, GATE_RANK, DK_B), GATE_RANK ** -0.5),
        'b_decay': nrm(ks[11], (DEPTH, DK_B), 0.1),
        'rel_bias': nrm(ks[12], (DEPTH, N_HEADS_A, 2 * REL_CLIP + 1), 0.1),
        'gla_norm': 1.0 + nrm(ks[13], (DEPTH, DV_HEAD_B), 0.05),
        'w_proj_a': nrm(ks[14], (DEPTH, D_A, D_MODEL), D_A ** -0.5),
        'w_proj_b': nrm(ks[15], (DEPTH, DV_B, D_MODEL), DV_B ** -0.5),
        'w_out': nrm(ks[16], (DEPTH, D_MODEL, D_MODEL), D_MODEL ** -0.5),
        'w_ffn_gate': nrm(ks[17], (DEPTH, D_MODEL, D_FF), D_MODEL ** -0.5),
        'w_ffn_up': nrm(ks[18], (DEPTH, D_MODEL, D_FF), D_MODEL ** -0.5),
        'w_ffn_down': nrm(ks[19], (DEPTH, D_FF, D_MODEL), D_FF ** -0.5),
    }


def reference(x_prompt, x_sample, cache_attn_k, cache_attn_v, state_gla,
              norm_mix_pre, norm_mix_post, norm_ffn_pre, norm_ffn_post,
              w_in, w_decay_up, b_decay, rel_bias, gla_norm, w_proj_a, w_proj_b, w_out,
              w_ffn_gate, w_ffn_up, w_ffn_down):
    yp, ys = x_prompt, x_sample
    kp_l, vp_l, sp_l, ks_l, vs_l, ss_l = [], [], [], [], [], []
    for l in range(DEPTH):
        w = (norm_mix_pre[l], norm_mix_post[l], norm_ffn_pre[l], norm_ffn_post[l],
             w_in[l], w_decay_up[l], b_decay[l], rel_bias[l], gla_norm[l],
             w_proj_a[l], w_proj_b[l], w_out[l], w_ffn_gate[l], w_ffn_up[l], w_ffn_down[l])
        s0 = jnp.zeros((yp.shape[0], N_HEADS_B, DK_HEAD_B, DV_HEAD_B), jnp.float32)
        yp, kp, vp, sp = _layer(yp, s0, None, None, *w)
        ys, kn, vn, sn = _layer(ys, state_gla[l], cache_attn_k[l], cache_attn_v[l], *w)
        kp_l.append(kp)
        vp_l.append(vp)
        sp_l.append(sp.astype(x_prompt.dtype))
        ks_l.append(kn)
        vs_l.append(vn)
        ss_l.append(sn.astype(state_gla.dtype))
    return (yp, ys, jnp.stack(kp_l), jnp.stack(vp_l), jnp.stack(sp_l),
            jnp.stack(ks_l), jnp.stack(vs_l), jnp.stack(ss_l))
```

```python
import contextlib
import numpy as np
import concourse.bass as bass
import concourse.mybir as mybir
from concourse.bass_utils import run_bass_kernel_spmd

F32 = mybir.dt.float32
BF16 = mybir.dt.bfloat16
AF = mybir.ActivationFunctionType
ALU = mybir.AluOpType

ENGS = ("pe", "act", "dve", "pool", "sp")


class Op:
    __slots__ = ("eng", "fn", "deps", "idx", "sig", "seq", "is_dma", "semkey", "sem", "semval", "inc")


class KeyState:
    __slots__ = ("writers", "readers")

    def __init__(self):
        self.writers = {}
        self.readers = {}


class Prog:
    def __init__(self, nc):
        self.nc = nc
        self.ops = []
        self.state = {}
        self.stack = contextlib.ExitStack()
        self.out_dmas = []
        self.last_dma = {}
        self.cc_barrier = None

    def sbuf(self, name, shape, dtype):
        return self.stack.enter_context(self.nc.sbuf_tensor(name, list(shape), dtype))

    def psum(self, name, shape, dtype):
        return self.stack.enter_context(self.nc.psum_tensor(name, list(shape), dtype))

    def add(self, eng, fn, reads=(), writes=(), dma=False, semkey=None, is_out=False, inc=16, barrier=False):
        op = Op()
        op.eng = eng
        op.fn = fn
        op.idx = len(self.ops)
        op.sig = False
        op.seq = None
        op.is_dma = dma
        op.semkey = None
        op.sem = None
        op.semval = None
        op.inc = inc
        deps = {}
        ek = ("dma", op.idx) if dma else eng
        psr = [k for k in reads if k.startswith("ps")]
        if psr:
            reads = [k for k in reads if not k.startswith("ps")]
            writes = list(writes) + [k for k in psr if k not in writes]
        for k in reads:
            st = self.state.get(k)
            if st is None:
                st = self.state[k] = KeyState()
            for d in st.writers.values():
                deps[d.idx] = d
        for k in writes:
            st = self.state.get(k)
            if st is None:
                st = self.state[k] = KeyState()
            for d in st.writers.values():
                deps[d.idx] = d
            for d in st.readers.values():
                deps[d.idx] = d
        for k in reads:
            self.state[k].readers[ek] = op
        for k in writes:
            st = self.state[k]
            st.writers = {ek: op}
            st.readers = {}
        if dma:
            if barrier:
                for d in self.last_dma.values():
                    deps[d.idx] = d
                self.cc_barrier = op
            elif self.cc_barrier is not None:
                deps[self.cc_barrier.idx] = self.cc_barrier
        deps.pop(op.idx, None)
        op.deps = list(deps.values())
        if dma:
            if semkey is None:
                semkey = writes[0] if writes else reads[0]
            op.semkey = semkey
            self.last_dma[semkey] = op
            if is_out:
                self.out_dmas.append(op)
        self.ops.append(op)
        return op

    def finish(self):
        op = self.add("sp", None)
        op.deps = list(self.out_dmas)

    def emit(self):
        nc = self.nc

        def need_wait(op, d):
            if d.is_dma:
                return True
            if d.eng == op.eng:
                if op.is_dma:
                    return True
                if op.eng == "pe":
                    return False
                return True
            return True

        for op in self.ops:
            for d in op.deps:
                if not d.is_dma and need_wait(op, d):
                    d.sig = True
        counters = {e: 0 for e in ENGS}
        for op in self.ops:
            if op.sig and not op.is_dma:
                counters[op.eng] += 1
                op.seq = counters[op.eng]
        semkeys = []
        seen = set()
        for op in self.ops:
            if op.is_dma and op.semkey not in seen:
                seen.add(op.semkey)
                semkeys.append(op.semkey)
        self.n_sems = len(semkeys) + len(ENGS)
        engsem = {e: self.stack.enter_context(nc.semaphore("s_" + e)) for e in ENGS}
        dmasem = {k: self.stack.enter_context(nc.semaphore("d_%d" % i)) for i, k in enumerate(semkeys)}
        dmacnt = {k: 0 for k in semkeys}
        for op in self.ops:
            if op.is_dma:
                dmacnt[op.semkey] += op.inc
                op.sem = dmasem[op.semkey]
                op.semval = dmacnt[op.semkey]
        per_eng = {e: [o for o in self.ops if o.eng == e] for e in ENGS}

        def run(e, engobj):
            waited = {}
            for op in per_eng[e]:
                for d in op.deps:
                    if not need_wait(op, d):
                        continue
                    if d.is_dma:
                        key = ("d", d.semkey)
                        val = d.semval
                        sem = d.sem
                    else:
                        key = ("e", d.eng)
                        val = d.seq
                        sem = engsem[d.eng]
                    if waited.get(key, 0) >= val:
                        continue
                    waited[key] = val
                    engobj.wait_ge(sem, val)
                if op.fn is None:
                    continue
                ins = op.fn(engobj)
                if op.is_dma:
                    if op.inc == 16:
                        ins.then_inc(op.sem, 16)
                    else:
                        ins.then_inc(op.sem)
                elif op.sig:
                    ins.then_inc(engsem[e], 1)

        with nc.Block() as block:
            @block.sync
            def _(eng):
                run("sp", eng)

            @block.tensor
            def _(eng):
                run("pe", eng)

            @block.scalar
            def _(eng):
                run("act", eng)

            @block.vector
            def _(eng):
                run("dve", eng)

            @block.gpsimd
            def _(eng):
                run("pool", eng)

    def close(self):
        self.stack.close()


D = 1024
DIN = 6672
DFF = 2816
NCORES = 8
SEQ = 16384
TOK_CORE = SEQ // NCORES
NT_P = TOK_CORE // 128
G = 2
QA, KA, VA, QB, KB, VB, RB, DLR, GA, GB = 0, 512, 1024, 1536, 2048, 2560, 3584, 4608, 4624, 5648
EPS = 1e-6
NEG = -30000.0


def build_program(debug=False):
    nc = bass.Bass("TRN2", target_bir_lowering=False)
    dbg = {}
    P = Prog(nc)
    A = P.add

    def din(name, shape):
        return nc.dram_tensor(name, list(shape), F32, kind="ExternalInput").ap()

    def dout(name, shape):
        return nc.dram_tensor(name, list(shape), F32, kind="ExternalOutput").ap()

    xp = din("xp", [TOK_CORE, D])
    xh = din("xh", [512, D])
    xpre = din("xpre", [7 * TOK_CORE, D])
    ones_row = din("ones_row", [16, G * 128])
    xs = din("xs", [128, D])
    ck = din("ck", [4, 512, 512])
    cv = din("cv", [4, 512, 512])
    sg = din("sg", [4, 4, 128, 256])
    valid = din("valid", [128, 1])
    w_in = din("w_in", [D, DIN])
    wup = din("wup", [32, 512])
    gpre = din("gpre", [128, 8])
    gffn = din("gffn", [128, 8])
    gpost = din("gpost", [1, D])
    gfpost = din("gfpost", [1, D])
    gnorm = din("gnorm", [1, 256])
    wpa = din("wpa", [512, D])
    wpb = din("wpb", [D, D])
    wout = din("wout", [D, D])
    wg = din("wg", [D, DFF])
    wu = din("wu", [D, DFF])
    wd = din("wd", [DFF, D])
    biasP = din("biasP", [128, 8 * 384])
    mkP = din("mkP", [128, 384])
    biasS = din("biasS", [128, 8 * 64])
    cbias = din("cbias", [128, 8])
    cm64 = din("cm64", [128, 6 * 128 + 512 + 2])
    cm32 = din("cm32", [128, 6 * 128 + 512 + 4])

    yp = dout("yp", [TOK_CORE, D])
    ys = dout("ys", [128, D])
    kp = dout("kp", [512, 512])
    vp = dout("vp", [512, 512])
    spo = dout("spo", [128, 1024])
    ksn = dout("ksn", [128, 512])
    vsn = dout("vsn", [128, 512])
    sso = dout("sso", [4, 128, 1024])

    wsrc = {"w_in": (w_in, D, DIN), "wpa": (wpa, 512, D), "wpb": (wpb, D, D), "wout": (wout, D, D),
            "wg": (wg, D, DFF), "wu": (wu, D, DFF), "wd": (wd, DFF, D)}

    T = G * 128
    xg = P.sbuf("xg", [128, 2 * G, D], F32)
    hb = P.sbuf("hb", [128, D], BF16)
    hb2 = P.sbuf("hb2", [128, D], BF16)
    hT = P.sbuf("hT", [128, 8, T], BF16)
    qTa = P.sbuf("qTa", [128, 4, T], BF16)
    kTa = P.sbuf("kTa", [128, 4, 8 * 128], BF16)
    vA = P.sbuf("vA", [128, 64, 65], BF16)
    qTb = P.sbuf("qTb", [128, 4, T], BF16)
    kTb = P.sbuf("kTb", [128, 4, T], BF16)
    kbt = P.sbuf("kbt", [128, G, 512], BF16)
    vb = P.sbuf("vb", [128, G, 1024], BF16)
    rbs = P.sbuf("rbs", [128, G, 1024], BF16)
    dlrT = P.sbuf("dlrT", [32, T], F32)
    gaT = P.sbuf("gaT", [128, 8, T], BF16)
    gbT = P.sbuf("gbT", [128, 8, T], BF16)
    oaT = P.sbuf("oaT", [128, 4, T], BF16)
    obT = P.sbuf("obT", [128, 8, T], BF16)
    aT = P.sbuf("aT", [128, 22, T], BF16)
    fsb = P.sbuf("fsb", [128, G, D], F32)
    wblk = [P.sbuf("wblk%d" % i, [128, 8, 512], BF16) for i in range(3)]
    ystage = [P.sbuf("yst%d" % i, [128, D], F32) for i in range(2)]
    kvst = [P.sbuf("kvst%d" % i, [128, 512], F32) for i in range(2)]
    ident = P.sbuf("ident", [128, 128], BF16)
    gpre_sb = P.sbuf("gpre_sb", [128, 8], F32)
    gffn_sb = P.sbuf("gffn_sb", [128, 8], F32)
    gpost_sb = P.sbuf("gpost_sb", [128, D], F32)
    gfpost_sb = P.sbuf("gfpost_sb", [128, D], F32)
    gn_sb = P.sbuf("gn_sb", [128, 256], F32)
    wup_sb = P.sbuf("wup_sb", [32, 512], F32)
    biasP_sb = P.sbuf("biasP_sb", [128, 8, 384], BF16)
    biasS_sb = P.sbuf("biasS_sb", [128, 8, 64], BF16)
    cbias_sb = P.sbuf("cbias_sb", [128, 8], F32)
    valid_sb = P.sbuf("valid_sb", [128, 1], F32)
    cm = {64: P.sbuf("cm64_sb", [128, 6 * 128 + 512 + 2], F32), 32: P.sbuf("cm32_sb", [128, 6 * 128 + 512 + 4], F32)}
    ones8 = P.sbuf("ones8", [128, 8], F32)
    ss = P.sbuf("ss", [128, 4], F32)
    ssn = P.sbuf("ssn", [128, 4], F32)
    rstdn = P.sbuf("rstdn", [128, 4], F32)
    rstd = P.sbuf("rstd", [128, 4], F32)
    junk = P.sbuf("junk", [128, D], BF16)
    tmpf = P.sbuf("tmpf", [128, D], F32)
    mkP_sb = tmpf[:, 0:384]
    identf = tmpf[:, 512:640]
    stmp2 = [P.sbuf("stmp%d" % i, [128, 3, 128], F32) for i in range(2)]
    expT2 = [P.sbuf("expT%d" % i, [128, 5, 128], BF16) for i in range(2)]
    rden = P.sbuf("rden", [128, 8], F32)
    oa_tok = P.sbuf("oa_tok", [128, 512], BF16)
    ob_tok = P.sbuf("ob_tok", [128, 1024], BF16)
    la_neg = P.sbuf("la_neg", [128, 512], F32)
    laT_neg = P.sbuf("laT_neg", [128, 512], F32)
    bneg = P.sbuf("bneg", [128, 512], F32)
    eb = P.sbuf("eb", [128, 512], F32)
    enb = P.sbuf("enb", [128, 512], F32)
    ec = P.sbuf("ec", [128, 512], F32)
    qtl = P.sbuf("qtl", [128, 4, 128], BF16)
    ktl = P.sbuf("ktl", [128, 4, 128], BF16)
    qz = [P.sbuf("qz%d" % i, [128, 4, 128], BF16) for i in range(4)]
    khat = [P.sbuf("khat%d" % i, [128, 512], BF16) for i in range(4)]
    attn_sb = P.sbuf("attn_sb", [128, 4, 128], BF16)
    S = P.sbuf("S", [128, 4, 256], F32)
    Sbf = [P.sbuf("Sbf%d" % i, [128, 4, 256], BF16) for i in range(2)]
    vc = P.sbuf("vc", [128, 32, 65], BF16)
    rdens = P.sbuf("rdens", [32, 8], F32)
    stmps2 = [P.sbuf("stmps%d" % i, [128, 2, 32], F32) for i in range(2)]
    expTs2 = [P.sbuf("expTs%d" % i, [128, 5, 32], BF16) for i in range(2)]

    vnew = aT[0:32, 0:9, :].rearrange("p a b -> p (a b)")[:, 0:2080].rearrange("p (n d) -> p n d", d=65)
    oas = aT[0:32, 9:18, :].rearrange("p a b -> p (a b)")[:, 0:2048].rearrange("p (n d) -> p n d", d=512)
    kc_tok = tmpf[:].bitcast(BF16).rearrange("p (t f) -> p t f", t=4)
    Sbs = aT[:, 6:22, :]
    kTc = obT[:].rearrange("p a b -> p (a b)").rearrange("p (a b) -> p a b", a=4)
    wdlr = P.sbuf("wdlr", [128, 8, 16], BF16)
    wup_bf = P.sbuf("wup_bf", [32, 512], BF16)
    dlrTb = P.sbuf("dlrTb", [32, 256], BF16)
    onesb = P.sbuf("onesb", [128, 2], BF16)
    utf_bf = P.sbuf("utf_bf", [128, 128], BF16)
    etile = P.sbuf("etile", [128, 2, 4], F32)
    ssp = P.sbuf("ssp", [128, 2], F32)
    rstdp = P.sbuf("rstdp", [128, 2], F32)
    aTf = aT[:].rearrange("p a b -> p (a b)")
    pp_hb = [aTf[:, i * 1024:(i + 1) * 1024] for i in range(2)]
    pp_hT = [aTf[:, 2048 + i * 1024:2048 + (i + 1) * 1024].rearrange("p (k n) -> p k n", k=8) for i in range(2)]
    obTf = obT[:].rearrange("p a b -> p (a b)")
    pp_kbt = [obTf[:, i * 512:(i + 1) * 512] for i in range(4)]
    gaTf = gaT[:].rearrange("p a b -> p (a b)")
    gbTf = gbT[:].rearrange("p a b -> p (a b)")
    pp_vb = [gaTf[:, 0:1024], gaTf[:, 1024:2048], gbTf[:, 0:1024], gbTf[:, 1024:2048]]
    hTf = hT[:].rearrange("p a b -> p (a b)")
    pp_la = [hTf[:, i * 512:(i + 1) * 512] for i in range(2)]
    pp_khat = [hTf[:, 1024 + i * 512:1024 + (i + 1) * 512] for i in range(2)]
    fsbf = fsb[:].rearrange("p a b -> p (a b)")
    pp_ec = [fsbf[:, i * 512:(i + 1) * 512] for i in range(2)]
    ps = [P.psum("ps%d" % i, [128, 512], F32) for i in range(7)]
    pst = P.psum("pst", [128, 8, 128], BF16)

    def dump(name, ap_sb, shape, dtype, key):
        if not debug:
            return
        t = nc.dram_tensor("dbg_" + name, list(shape), F32, kind="ExternalOutput").ap()
        if dtype == F32:
            A("sp", lambda e: e.dma_start(out=t, in_=ap_sb), reads=[key], dma=True, is_out=True, semkey="dbg_" + name)
        else:
            n = shape[1] * shape[2]
            if n <= 1024:
                stg = fsb[:, 1, 0:n].rearrange("p (a b) -> p a b", a=shape[1])
            else:
                stg = fsb[:].rearrange("p a b -> p (a b)")[:, 0:n].rearrange("p (a b) -> p a b", a=shape[1])
            A("dve", lambda e: e.tensor_copy(out=stg, in_=ap_sb), reads=[key], writes=["fsb0", "fsb1"])
            A("sp", lambda e: e.dma_start(out=t, in_=stg), reads=["fsb0", "fsb1"], dma=True, is_out=True, semkey="dbg_" + name)

    def mask_UT(c):
        return cm[c][:, 0:128]

    def mask_T4(c):
        return cm[c][:, 128:640]

    def mask_restart(c):
        return cm[c][:, 768:1280]

    def rowmask(c, i):
        return cm[c][:, 1280 + i:1281 + i]

    def load(dst, src, key, eng="sp"):
        if eng == "pool":
            A(eng, lambda e: e.dma_start(out=dst, in_=src), writes=["poolq", key], dma=True, semkey="poolq")
        else:
            A(eng, lambda e: e.dma_start(out=dst, in_=src), writes=[key], dma=True)

    load(gpre_sb[:], gpre[:, :], "gpre")
    load(gffn_sb[:], gffn[:, :], "gffn")
    load(gpost_sb[:], gpost.broadcast_to([128, D]), "gpost")
    load(gfpost_sb[:], gfpost.broadcast_to([128, D]), "gfpost")
    load(gn_sb[:], gnorm.broadcast_to([128, 256]), "gn")
    load(wup_sb[:], wup[:, :], "wup")
    load(mkP_sb, mkP[:, :], "tmpf")
    load(cbias_sb[:], cbias[:, :], "cbias")
    load(valid_sb[:], valid[:, :], "valid")
    load(cm[64][:], cm64[:, :], "cm64")
    load(cm[32][:], cm32[:, :], "cm32")
    load(biasP_sb[:], biasP.rearrange("p (h n) -> p h n", h=8), "biasP", eng="pool")
    load(biasS_sb[:], biasS.rearrange("p (h n) -> p h n", h=8), "biasS", eng="pool")
    A("dve", lambda e: e.memset(identf, 1.0), writes=["tmpf"])
    A("pool", lambda e: e.affine_select(out=identf, in_=identf, pattern=[[-1, 128]], compare_op=ALU.is_equal,
                                        fill=0.0, base=0, channel_multiplier=1), reads=["tmpf"], writes=["tmpf"])
    A("dve", lambda e: e.tensor_copy(out=ident[:], in_=identf), reads=["tmpf"], writes=["ident"])
    A("dve", lambda e: e.memset(ones8[:], 1.0), writes=["ones8"])
    for h in range(8):
        A("dve", lambda e, h=h: e.tensor_tensor(out=biasP_sb[:, h, :], in0=biasP_sb[:, h, :], in1=mkP_sb, op=ALU.add),
          reads=["biasP", "tmpf"], writes=["biasP"])
    A("dve", lambda e: e.memset(vA[:, :, 64:65], 1.0), writes=["vA%d" % s for s in range(8)])
    A("dve", lambda e: e.memset(vc[:, :, 64:65], 1.0), writes=["vc"])
    A("dve", lambda e: e.memset(dlrT[:], 0.0), writes=["dlrT"])
    for i in range(4):
        A("dve", lambda e, i=i: e.memset(qz[i][:], 0.0), writes=["qz%d" % i])
    A("dve", lambda e: e.memset(S[:], 0.0), writes=["S", "S0", "S1", "S2", "S3"])
    A("dve", lambda e: e.memset(Sbf[0][:], 0.0), writes=["Sbf0"])

    wctr = [0]
    A("dve", lambda e: e.memset(dlrTb[:], 0.0), writes=["dlrTb"])
    A("pool", lambda e: e.dma_start(out=dlrTb[16:32, :], in_=ones_row[:, :]), writes=["poolq", "dlrTb"], dma=True, semkey="poolq")
    cvctr = [0]
    wblocks = {}

    def register_block(wname, k0, nkc, c0, ncols):
        sig = (wname, k0, nkc, c0, ncols)
        if sig in wblocks:
            return wblocks[sig]
        scr = nc.dram_tensor("scr_%s_%d_%d" % (wname, k0, c0), [128, nkc * ncols], BF16).ap()
        src = wsrc[wname][0].rearrange("(k p) n -> p k n", p=128)[:, k0:k0 + nkc, c0:c0 + ncols]
        q = "cvq%d" % (cvctr[0] % 4)
        cvctr[0] += 1
        op = A("pool", lambda e: e.dma_start(out=scr.rearrange("p (k n) -> p k n", k=nkc), in_=src), writes=[q], dma=True, semkey=q)
        wblocks[sig] = (scr, op)
        return wblocks[sig]

    for c0_ in (QA, KA, VA, QB, KB, VB, VB + 512, RB, RB + 512):
        register_block("w_in", 0, 8, c0_, 512)
    register_block("w_in", 0, 8, DLR, 16)
    for c0_ in (GA, GA + 512, GB, GB + 512):
        register_block("w_in", 0, 8, c0_, 512)
    for half_ in range(2):
        register_block("wpa", 0, 4, half_ * 512, 512)
    for half_ in range(2):
        register_block("wpb", 0, 8, half_ * 512, 512)
    for cb_ in range(2):
        register_block("wout", 0, 8, cb_ * 512, 512)
    for fb4_ in range(0, 22, 4):
        nb_ = min(4, 22 - fb4_)
        register_block("wg", 0, 8, fb4_ * 128, nb_ * 128)
        register_block("wu", 0, 8, fb4_ * 128, nb_ * 128)
    for cb_ in range(2):
        for (k0_, nk_) in ((0, 8), (8, 8), (16, 6)):
            register_block("wd", k0_, nk_, cb_ * 512, 512)

    wctr = [0]
    wblk.append(fsb[:].rearrange("p a b -> p (a b)").bitcast(BF16).rearrange("p (k n) -> p k n", k=8))
    slotkeys = {0: ["w0"], 1: ["w1"], 2: ["w2"], 3: ["w3", "fsb0", "fsb1"]}

    def load_w(wname, k0, nkc, c0, ncols, slot=None):
        if slot is None:
            slot = wctr[0] % 3
            wctr[0] += 1
        scr, cop = register_block(wname, k0, nkc, c0, ncols)
        op = A("sp", lambda e: e.dma_start(out=wblk[slot][:, 0:nkc, 0:ncols], in_=scr.rearrange("p (k n) -> p k n", k=nkc)),
               writes=slotkeys[slot], dma=True, semkey="w%d" % slot)
        if cop.idx not in set(d.idx for d in op.deps):
            op.deps.append(cop)
        return slot

    pctr = [0]

    def acc_bank():
        b = pctr[0] % 2
        pctr[0] += 1
        return b

    def bstyle(slot, nkc, nblk, src, srckey, Tg, evac, mrows=128):
        for ob in range(nblk):
            b = acc_bank()
            for kc in range(nkc):
                A("pe", lambda e, kc=kc, ob=ob, b=b: e.matmul(ps[b][0:mrows, 0:Tg], lhsT=wblk[slot][:, kc, ob * 128:ob * 128 + mrows],
                                                               rhs=src[:, kc, 0:Tg], start=(kc == 0), stop=(kc == nkc - 1)),
                  reads=["w%d" % slot, srckey], writes=["ps%d" % b])
            evac(ps[b], "ps%d" % b, ob)

    def astyle_tile(slot, nkc, ncols, src, srckey, tok0, mtok, evac, kc_off=0):
        b = acc_bank()
        for kc in range(nkc):
            A("pe", lambda e, kc=kc, b=b: e.matmul(ps[b][0:mtok, 0:ncols], lhsT=src[:, kc_off + kc, tok0:tok0 + mtok],
                                                   rhs=wblk[slot][:, kc, 0:ncols], start=(kc == 0), stop=(kc == nkc - 1)),
              reads=["w%d" % slot, srckey], writes=["ps%d" % b])
        evac(ps[b], "ps%d" % b)

    def norm_to_hT(src_ap, srckey, gcol_sb, gkey, tok0):
        A("act", lambda e: e.activation(out=junk[:], in_=src_ap, func=AF.Square, accum_out=ss[:, 0:1]),
          reads=[srckey], writes=["junk", "ss"])
        A("act", lambda e: e.activation(out=rstd[:, 0:1], in_=ss[:, 0:1], func=AF.Ln, scale=1.0 / D, bias=EPS),
          reads=["ss"], writes=["rstd"])
        A("act", lambda e: e.activation(out=rstd[:, 0:1], in_=rstd[:, 0:1], func=AF.Exp, scale=-0.5),
          reads=["rstd"], writes=["rstd"])
        A("dve", lambda e: e.tensor_scalar(out=hb[:], in0=src_ap, scalar1=rstd[:, 0:1], scalar2=None, op0=ALU.mult),
          reads=[srckey, "rstd"], writes=["hb"])
        for kc in range(8):
            A("pe", lambda e, kc=kc: e.transpose(out=pst[:, kc, :], in_=hb[:, kc * 128:(kc + 1) * 128], identity=ident[:]),
              reads=["hb", "ident"], writes=["pst"])
        for kc in range(8):
            A("act", lambda e, kc=kc: e.activation(out=hT[:, kc, tok0:tok0 + 128], in_=pst[:, kc, :], func=AF.Copy,
                                                   scale=gcol_sb[:, kc:kc + 1]),
              reads=["pst", gkey], writes=["hT"])

    def transpose_to(src_tok, srckey, nblk, dst, dstkey, tok0, rows=128):
        for blk in range(nblk):
            A("pe", lambda e, blk=blk: e.transpose(out=pst[:, blk, 0:rows], in_=src_tok[0:rows, blk * 128:(blk + 1) * 128],
                                                   identity=ident[0:rows, 0:rows]),
              reads=[srckey, "ident"], writes=["pst"])
        A("act", lambda e: e.copy(out=dst[:, 0:nblk, tok0:tok0 + rows], in_=pst[:, 0:nblk, 0:rows]),
          reads=["pst"], writes=[dstkey])

    def gla_tile(t, c, state_only, sample=False, bk=None):
        bk = bk or {"la": 2, "laT": 3, "c": 4, "at": 5, "st": 6}
        B_la, B_laT, B_c, B_at, B_st = bk["la"], bk["laT"], bk["c"], bk["at"], bk["st"]
        nch = 128 // c
        tok0 = t * 128
        A("pe", lambda e: e.matmul(ps[B_la][:, :], lhsT=dlrT[0:32, tok0:tok0 + 128], rhs=wup_sb[0:32, :], start=True, stop=True),
          reads=["dlrT", "wup"], writes=["ps%d" % B_la])
        A("act", lambda e: e.activation(out=la_neg[:], in_=ps[B_la][:, :], func=AF.Exp, scale=-1.0), reads=["ps%d" % B_la], writes=["la_neg"])
        A("act", lambda e: e.activation(out=la_neg[:], in_=la_neg[:], func=AF.Ln, bias=1.0), reads=["la_neg"], writes=["la_neg"])
        yield
        A("pe", lambda e: e.matmul(ps[B_c][:, :], lhsT=mask_UT(c), rhs=la_neg[:], start=True, stop=True),
          reads=["la_neg", "cm%d" % c], writes=["ps%d" % B_c])
        A("act", lambda e: e.activation(out=ec[:], in_=ps[B_c][:, :], func=AF.Exp, scale=-1.0 / 16), reads=["ps%d" % B_c], writes=["ec"])
        for i in range(nch):
            A("dve", lambda e, i=i: e.scalar_tensor_tensor(out=khat[i][:], in0=kbt[:, t, :], scalar=rowmask(c, i), in1=ec[:],
                                                           op0=ALU.mult, op1=ALU.mult),
              reads=["kbt", "ec", "cm%d" % c], writes=["khat%d" % i])
        yield
        for h in range(4):
            A("pe", lambda e, h=h: e.matmul(ps[B_laT][:, h * 128:(h + 1) * 128], lhsT=wup_sb[0:32, h * 128:(h + 1) * 128],
                                            rhs=dlrT[0:32, tok0:tok0 + 128], start=True, stop=True),
              reads=["dlrT", "wup"], writes=["ps%d" % B_laT])
        A("act", lambda e: e.activation(out=laT_neg[:], in_=ps[B_laT][:, :], func=AF.Exp, scale=-1.0), reads=["ps%d" % B_laT], writes=["laT_neg"])
        A("act", lambda e: e.activation(out=laT_neg[:], in_=laT_neg[:], func=AF.Ln, bias=1.0), reads=["laT_neg"], writes=["laT_neg"])
        A("dve", lambda e: e.tensor_tensor_scan(out=bneg[:], data0=mask_restart(c), data1=laT_neg[:], initial=0.0,
                                                op0=ALU.mult, op1=ALU.add),
          reads=["laT_neg", "cm%d" % c], writes=["bneg"])
        A("act", lambda e: e.activation(out=eb[:], in_=bneg[:], func=AF.Exp, scale=-1.0 / 16), reads=["bneg"], writes=["eb"])
        yield
        if not state_only:
            A("act", lambda e: e.activation(out=enb[:], in_=bneg[:], func=AF.Exp, scale=1.0 / 16), reads=["bneg"], writes=["enb"])
            A("dve", lambda e: e.tensor_tensor(out=qtl[:], in0=qTb[:, :, tok0:tok0 + 128],
                                               in1=eb[:].rearrange("p (h n) -> p h n", h=4), op=ALU.mult),
              reads=["qTb", "eb"], writes=["qtl"])
            A("dve", lambda e: e.tensor_tensor(out=ktl[:], in0=kTb[:, :, tok0:tok0 + 128],
                                               in1=enb[:].rearrange("p (h n) -> p h n", h=4), op=ALU.mult),
              reads=["kTb", "enb"], writes=["ktl"])
            for i in range(nch):
                A("dve", lambda e, i=i: e.tensor_copy(out=qz[i][:, :, i * c:(i + 1) * c], in_=qtl[:, :, i * c:(i + 1) * c]),
                  reads=["qtl"], writes=["qz%d" % i])
            yield
            for h in range(4):
                A("pe", lambda e, h=h: e.matmul(ps[B_at][:, h * 128:(h + 1) * 128], lhsT=ktl[:, h, :], rhs=qtl[:, h, :], start=True, stop=True),
                  reads=["ktl", "qtl"], writes=["ps%d" % B_at])
            A("dve", lambda e: e.tensor_tensor(out=attn_sb[:].rearrange("p h n -> p (h n)"), in0=ps[B_at][:, :], in1=mask_T4(c), op=ALU.mult),
              reads=["ps%d" % B_at, "cm%d" % c], writes=["attn_sb"])
            yield

        skeys = []
        par = [0]

        def state_update(i):
            cur = par[0]
            skeys.append((Sbf[cur], "Sbf%d" % cur))
            for hp in range(2):
                for hh in range(2):
                    h = hp * 2 + hh
                    A("pe", lambda e, i=i, h=h, hh=hh: e.matmul(ps[B_st][:, hh * 256:(hh + 1) * 256], lhsT=khat[i][:, h * 128:(h + 1) * 128],
                                                                rhs=vb[:, t, h * 256:(h + 1) * 256], start=True, stop=True),
                      reads=["khat%d" % i, "vb"], writes=["ps%d" % B_st])
                for hh in range(2):
                    h = hp * 2 + hh
                    col = h * 128 + (i + 1) * c - 1
                    A("dve", lambda e, h=h, hh=hh, col=col: e.scalar_tensor_tensor(out=S[:, h, :], in0=S[:, h, :], scalar=eb[:, col:col + 1],
                                                                                    in1=ps[B_st][:, hh * 256:(hh + 1) * 256], op0=ALU.mult, op1=ALU.add),
                      reads=["S", "eb", "ps%d" % B_st], writes=["S"])
            if not state_only:
                nxt = 1 - cur
                A("act", lambda e, nxt=nxt: e.copy(out=Sbf[nxt][:], in_=S[:]), reads=["S"], writes=["Sbf%d" % nxt])
                par[0] = nxt

        def sample_states():
            for b in range(4):
                sl = fsb[:, b % 2, :].rearrange("p (h v) -> p h v", h=4)
                slk = "fsb%d" % (b % 2)
                A("sp", lambda e, b=b, sl=sl: e.dma_start(out=sl, in_=sg[b].rearrange("h d v -> d h v")), writes=[slk], dma=True)
                for hp in range(2):
                    for hh in range(2):
                        h = hp * 2 + hh
                        A("pe", lambda e, b=b, h=h, hh=hh: e.matmul(ps[B_st][:, hh * 256:(hh + 1) * 256], lhsT=khat[b][:, h * 128:(h + 1) * 128],
                                                                    rhs=vb[:, t, h * 256:(h + 1) * 256], start=True, stop=True),
                          reads=["khat%d" % b, "vb"], writes=["ps%d" % B_st])
                    for hh in range(2):
                        h = hp * 2 + hh
                        col = h * 128 + (b + 1) * c - 1
                        A("dve", lambda e, h=h, hh=hh, col=col, sl=sl: e.scalar_tensor_tensor(out=sl[:, h, :], in0=sl[:, h, :], scalar=eb[:, col:col + 1],
                                                                                               in1=ps[B_st][:, hh * 256:(hh + 1) * 256], op0=ALU.mult, op1=ALU.add),
                          reads=[slk, "eb", "ps%d" % B_st], writes=[slk])
                A("act", lambda e, b=b: e.dma_start(out=sso[b], in_=fsb[:, b % 2, :]), reads=[slk], dma=True,
                  is_out=True, semkey=slk + "o")

        if state_only:
            for i in range(nch):
                state_update(i)
            return
        if sample:
            sample_states()
        else:
            assert nch == 2
            state_update(0)
            skeys.append((Sbf[par[0]], "Sbf%d" % par[0]))
        yield

        obank = lambda h: B_c if h < 2 else B_st
        for h in range(4):
            hh = h % 2
            ob_ = obank(h)
            A("pe", lambda e, h=h, hh=hh, ob_=ob_: e.matmul(ps[ob_][:, hh * 256:(hh + 1) * 256], lhsT=attn_sb[:, h, :], rhs=vb[:, t, h * 256:(h + 1) * 256],
                                                             start=True, stop=False),
              reads=["attn_sb", "vb"], writes=["ps%d" % ob_])
            for i in range(nch):
                if sample:
                    rhs_ap = Sbs[:, i * 4 + h, :]
                    rk = "aT"
                else:
                    rhs_ap = skeys[i][0][:, h, :]
                    rk = skeys[i][1]
                A("pe", lambda e, h=h, hh=hh, i=i, rhs_ap=rhs_ap, ob_=ob_: e.matmul(ps[ob_][:, hh * 256:(hh + 1) * 256], lhsT=qz[i][:, h, :], rhs=rhs_ap,
                                                                                   start=False, stop=(i == nch - 1)),
                  reads=["qz%d" % i, rk], writes=["ps%d" % ob_])
        yield
        for h in range(4):
            hh = h % 2
            ob_ = obank(h)
            A("act", lambda e, h=h, hh=hh, ob_=ob_: e.activation(out=junk[:, 0:256], in_=ps[ob_][:, hh * 256:(hh + 1) * 256], func=AF.Square,
                                                                 accum_out=ss[:, h:h + 1]),
              reads=["ps%d" % ob_], writes=["junk", "ss"])
        A("act", lambda e: e.activation(out=rstd[:, 0:4], in_=ss[:, 0:4], func=AF.Ln, scale=1.0 / 256, bias=EPS), reads=["ss"], writes=["rstd"])
        A("act", lambda e: e.activation(out=rstd[:, 0:4], in_=rstd[:, 0:4], func=AF.Exp, scale=-0.5), reads=["rstd"], writes=["rstd"])
        for h in range(4):
            hh = h % 2
            ob_ = obank(h)
            A("dve", lambda e, h=h, hh=hh, ob_=ob_: e.scalar_tensor_tensor(out=ob_tok[:, h * 256:(h + 1) * 256], in0=ps[ob_][:, hh * 256:(hh + 1) * 256],
                                                                            scalar=rstd[:, h:h + 1], in1=rbs[:, t, h * 256:(h + 1) * 256],
                                                                            op0=ALU.mult, op1=ALU.mult),
              reads=["ps%d" % ob_, "rstd", "rbs"], writes=["ob_tok"])
        yield
        if not sample:
            skeys.pop()
            state_update(1)
            skeys.pop()
        yield
        transpose_to(ob_tok, "ob_tok", 8, obT, "obT", tok0)

    def attn_prompt_tile(t, a):
        tok0 = t * 128
        jA = {0: 0, 1: 1, 4: 2}
        jB = {2: 0, 3: 1}

        def scores(h):
            par = (h // 2) % 2
            blk, pr = h // 2, (h % 2) * 64
            bA, bB = (2, 3) if (h // 2) % 2 == 0 else (0, 1)
            for d in range(5):
                s = (a - d) % 8
                if d in jA:
                    bank, j, bk = ps[bA], jA[d], "ps%d" % bA
                else:
                    bank, j, bk = ps[bB], jB[d], "ps%d" % bB
                A("pe", lambda e, bank=bank, j=j, s=s: e.matmul(bank[:, j * 128:(j + 1) * 128], lhsT=kTa[pr:pr + 64, blk, s * 128:(s + 1) * 128],
                                                               rhs=qTa[pr:pr + 64, blk, tok0:tok0 + 128], start=True, stop=True),
                  reads=["kT%d" % s, "qTa"], writes=[bk])
            A("dve", lambda e: e.scalar_tensor_tensor(out=stmp2[par][:].rearrange("p a b -> p (a b)"), in0=ps[bA][:, 0:384], scalar=0.125,
                                                      in1=biasP_sb[:, h, :], op0=ALU.mult, op1=ALU.add),
              reads=["ps%d" % bA, "biasP"], writes=["stmp%d" % par])
            A("act", lambda e: e.activation(out=expT2[par][:, 0:3, :].rearrange("p a b -> p (a b)"), in_=stmp2[par][:].rearrange("p a b -> p (a b)"),
                                            func=AF.Exp), reads=["stmp%d" % par], writes=["expT%d" % par])
            A("act", lambda e: e.activation(out=expT2[par][:, 3:5, :].rearrange("p a b -> p (a b)"), in_=ps[bB][:, 0:256], func=AF.Exp,
                                            scale=0.125, bias=cbias_sb[:, h:h + 1]), reads=["ps%d" % bB, "cbias"], writes=["expT%d" % par])

        def pv(h, slot, grp):
            par = (h // 2) % 2
            pb = 5
            order = [(0, 0), (1, 1), (4, 2), (2, 3), (3, 4)]
            for n, (d, j) in enumerate(order):
                s = (a - d) % 8
                A("pe", lambda e, j=j, s=s, n=n: e.matmul(ps[pb][:, slot * 65:(slot + 1) * 65], lhsT=expT2[par][:, j, :],
                                                          rhs=vA[:, s * 8 + h, :], start=(n == 0), stop=(n == 4)),
                  reads=["expT%d" % par, "vA%d" % s], writes=["ps%d" % pb])
            if slot == 3:
                gi = 0 if grp[0] == 0 else 1
                pv3 = ps[pb][:, 0:260].rearrange("p (h n) -> p h n", h=4)
                A("dve", lambda e: e.reciprocal(out=rden[:, gi * 4:gi * 4 + 4], in_=pv3[:, :, 64]), reads=["ps%d" % pb], writes=["rden%d" % gi])
                for k, h2 in enumerate(grp):
                    A("dve", lambda e, h2=h2, k=k: e.tensor_scalar(out=oa_tok[:, h2 * 64:(h2 + 1) * 64], in0=ps[pb][:, k * 65:k * 65 + 64],
                                                                   scalar1=rden[:, gi * 4 + k:gi * 4 + k + 1], scalar2=None, op0=ALU.mult),
                      reads=["ps%d" % pb, "rden%d" % gi], writes=["oa_tok"])

        horder = [0, 2, 4, 6, 1, 3, 5, 7]
        scores(horder[0])
        yield
        for i_, h in enumerate(horder):
            if i_ + 1 < 8:
                scores(horder[i_ + 1])
                yield
            pv(h, i_ % 4, horder[(i_ // 4) * 4:(i_ // 4) * 4 + 4])
            yield
        transpose_to(oa_tok, "oa_tok", 4, oaT, "oaT", tok0)

    def attn_sample():
        vc2 = xg[:, 2:4, :].rearrange("p a b -> p (a b)").bitcast(BF16)[:, 0:2080].rearrange("p (n d) -> p n d", d=65)
        vcb = [vc, vc2]
        vck = [["vc"], ["xg2", "xg3"]]
        A("dve", lambda e: e.memset(vc2[:, :, 64:65], 1.0), writes=["xg2", "xg3"])

        def load_k(b):
            A("pool", lambda e: e.dma_start(out=kc_tok, in_=ck[b].rearrange("(t p) f -> p t f", p=128)), writes=["poolq", "tmpf"], dma=True, semkey="poolq")

        def load_v(b):
            for kt in range(4):
                A("pool", lambda e, kt=kt: e.dma_start(out=vcb[b % 2][:, kt * 8:(kt + 1) * 8, 0:64],
                                                       in_=cv[b][kt * 128:(kt + 1) * 128, :].rearrange("p (h d) -> p h d", h=8)),
                  writes=["poolq"] + vck[b % 2], dma=True, semkey="poolq")

        load_k(0)
        load_v(0)
        for b in range(4):
            vcur = vcb[b % 2]
            vkeys = vck[b % 2]
            for blk in range(4):
                for kt in range(4):
                    A("pe", lambda e, blk=blk, kt=kt: e.transpose(out=pst[:, kt, :], in_=kc_tok[:, kt, blk * 128:(blk + 1) * 128], identity=ident[:]),
                      reads=["tmpf", "ident"], writes=["pst"])
                A("act", lambda e, blk=blk: e.copy(out=kTc[:, blk, :], in_=pst[:, 0:4, :].rearrange("p a b -> p (a b)")),
                  reads=["pst"], writes=["obT"])
            if b + 1 < 4:
                load_k(b + 1)
                load_v(b + 1)
            def s_scores(h, b=b):
                blk, pr = h // 2, (h % 2) * 64
                par = (h // 2) % 2
                bB, bA = (3, 2) if par == 0 else (1, 0)
                for kt in range(3):
                    A("pe", lambda e, kt=kt: e.matmul(ps[bB][:, kt * 32:(kt + 1) * 32], lhsT=kTc[pr:pr + 64, blk, kt * 128:(kt + 1) * 128],
                                                      rhs=qTa[pr:pr + 64, blk, b * 32:(b + 1) * 32], start=True, stop=True),
                      reads=["obT", "qTa"], writes=["ps%d" % bB])
                A("pe", lambda e: e.matmul(ps[bA][:, 0:32], lhsT=kTc[pr:pr + 64, blk, 384:512],
                                           rhs=qTa[pr:pr + 64, blk, b * 32:(b + 1) * 32], start=True, stop=True),
                  reads=["obT", "qTa"], writes=["ps%d" % bA])
                A("pe", lambda e: e.matmul(ps[bA][0:32, 32:64], lhsT=kTa[pr:pr + 64, blk, b * 32:(b + 1) * 32],
                                           rhs=qTa[pr:pr + 64, blk, b * 32:(b + 1) * 32], start=True, stop=True),
                  reads=["kT0", "qTa"], writes=["ps%d" % bA])
                ex, st_ = expTs2[par], stmps2[par]
                A("act", lambda e: e.activation(out=ex[:, 0:3, :].rearrange("p a b -> p (a b)"), in_=ps[bB][:, 0:96], func=AF.Exp,
                                                scale=0.125, bias=cbias_sb[:, h:h + 1]),
                  reads=["ps%d" % bB, "cbias"], writes=["expTs%d" % par])
                A("dve", lambda e: e.scalar_tensor_tensor(out=st_[:, 0, :], in0=ps[bA][:, 0:32], scalar=0.125, in1=biasS_sb[:, h, 0:32],
                                                          op0=ALU.mult, op1=ALU.add),
                  reads=["ps%d" % bA, "biasS"], writes=["stmps%d" % par])
                A("dve", lambda e: e.scalar_tensor_tensor(out=st_[0:32, 1, :], in0=ps[bA][0:32, 32:64], scalar=0.125, in1=biasS_sb[0:32, h, 32:64],
                                                          op0=ALU.mult, op1=ALU.add),
                  reads=["ps%d" % bA, "biasS"], writes=["stmps%d" % par])
                A("act", lambda e: e.activation(out=ex[:, 3, :], in_=st_[:, 0, :], func=AF.Exp), reads=["stmps%d" % par], writes=["expTs%d" % par])
                A("act", lambda e: e.activation(out=ex[0:32, 4, :], in_=st_[0:32, 1, :], func=AF.Exp), reads=["stmps%d" % par], writes=["expTs%d" % par])

            def s_pv(h, slot, grp, b=b, vcur=vcur, vkeys=vkeys):
                par = (h // 2) % 2
                ex = expTs2[par]
                for kt in range(4):
                    A("pe", lambda e, kt=kt: e.matmul(ps[5][0:32, slot * 65:slot * 65 + 65], lhsT=ex[:, kt, :], rhs=vcur[:, kt * 8 + h, :],
                                                      start=(kt == 0), stop=False),
                      reads=["expTs%d" % par] + vkeys, writes=["ps5"])
                A("pe", lambda e: e.matmul(ps[5][0:32, slot * 65:slot * 65 + 65], lhsT=ex[0:32, 4, :], rhs=vnew[0:32, b * 8 + h, :],
                                           start=False, stop=True),
                  reads=["expTs%d" % par, "aT"], writes=["ps5"])
                if slot == 3:
                    pv3 = ps[5][0:32, 0:260].rearrange("p (h n) -> p h n", h=4)
                    A("dve", lambda e: e.reciprocal(out=rdens[:, 0:4], in_=pv3[:, :, 64]), reads=["ps5"], writes=["rdens"])
                    for k, h2 in enumerate(grp):
                        A("dve", lambda e, h2=h2, k=k: e.tensor_scalar(out=oas[:, b, h2 * 64:(h2 + 1) * 64], in0=ps[5][0:32, k * 65:k * 65 + 64],
                                                                       scalar1=rdens[:, k:k + 1], scalar2=None, op0=ALU.mult),
                          reads=["ps5", "rdens"], writes=["aT"])

            horder = [0, 2, 4, 6, 1, 3, 5, 7]
            s_scores(horder[0])
            for i_, h in enumerate(horder):
                if i_ + 1 < 8:
                    s_scores(horder[i_ + 1])
                s_pv(h, i_ % 4, horder[(i_ // 4) * 4:(i_ // 4) * 4 + 4])
        for b in range(4):
            for blk in range(4):
                A("pe", lambda e, b=b, blk=blk: e.transpose(out=pst[:, blk, 0:32], in_=oas[:, b, blk * 128:(blk + 1) * 128], identity=ident[0:32, 0:32]),
                  reads=["aT", "ident"], writes=["pst"])
            A("act", lambda e, b=b: e.copy(out=oaT[:, 0:4, b * 32:(b + 1) * 32], in_=pst[:, 0:4, 0:32]), reads=["pst"], writes=["oaT"])

    def prep_group(xsrc, row0, ntiles, xpar):
        for t in range(ntiles):
            xi = xpar * G + t
            A("sp", lambda e, t=t, xi=xi: e.dma_start(out=xg[:, xi, :], in_=xsrc[row0 + t * 128:row0 + (t + 1) * 128, :]), writes=["xg%d" % xi], dma=True)
            norm_to_hT(xg[:, xi, :], "xg%d" % xi, gpre_sb, "gpre", t * 128)

    def run_group(kind, xsrc, row0, ntiles, a0=None, out_ap=None, kv_out=None, xpar=0, prefetched=False, next_prep=None):
        Tg = ntiles * 128
        c = 32 if kind == "sample" else 64
        if not prefetched:
            prep_group(xsrc, row0, ntiles, xpar)

        def ev_copy(dst, dkey, scale=None, func=AF.Copy, rows=128):
            def f(bank, bkey, ob):
                if scale is None:
                    A("act", lambda e: e.activation(out=dst(ob), in_=bank[0:rows, 0:Tg], func=func), reads=[bkey], writes=[dkey(ob)])
                else:
                    A("act", lambda e: e.activation(out=dst(ob), in_=bank[0:rows, 0:Tg], func=func, scale=scale), reads=[bkey], writes=[dkey(ob)])
            return f

        full = kind in ("prompt", "sample")
        if full:
            s = load_w("w_in", 0, 8, QA, 512)
            bstyle(s, 8, 4, hT, "hT", Tg, ev_copy(lambda ob: qTa[:, ob, 0:Tg], lambda ob: "qTa"))
        if kind != "pre":
            s = load_w("w_in", 0, 8, KA, 512)
            if kind == "sample":
                bstyle(s, 8, 4, hT, "hT", Tg, ev_copy(lambda ob: kTa[:, ob, 0:128], lambda ob: "kT0"))
            else:
                s0 = a0 % 8
                def kdst(ob):
                    return kTa[:, ob, s0 * 128:s0 * 128 + Tg]
                def kev(bank, bkey, ob):
                    A("act", lambda e: e.copy(out=kdst(ob), in_=bank[:, 0:Tg]), reads=[bkey], writes=["kT%d" % ((a0 + i) % 8) for i in range(ntiles)])
                bstyle(s, 8, 4, hT, "hT", Tg, kev)
            if kv_out is not None:
                for t in range(ntiles):
                    def kout(bank, bkey, t=t):
                        st = kvst[t % 2]
                        sk = "kvst%d" % (t % 2)
                        A("act", lambda e: e.copy(out=st[:], in_=bank[:, 0:512]), reads=[bkey], writes=[sk])
                        A("act", lambda e: e.dma_start(out=kv_out[0][kv_out[2] + t * 128:kv_out[2] + (t + 1) * 128, :], in_=st[:]), reads=[sk], dma=True,
                          is_out=True, semkey=sk + "o")
                    astyle_tile(s, 8, 512, hT, "hT", t * 128, 128, kout)
            s = load_w("w_in", 0, 8, VA, 512)
            for t in range(ntiles):
                slot_v = 0 if kind == "sample" else (a0 + t) % 8
                def vev(bank, bkey, t=t, slot_v=slot_v):
                    if kind != "sample":
                        A("act", lambda e: e.copy(out=vA[:, slot_v * 8:(slot_v + 1) * 8, 0:64], in_=bank[:, 0:512].rearrange("p (h d) -> p h d", h=8)),
                          reads=[bkey], writes=["vA%d" % slot_v])
                        if kind == "halo":
                            A("act", lambda e: e.activation(out=vA[:, slot_v * 8:(slot_v + 1) * 8, 64], in_=ones8[:], func=AF.Copy, scale=valid_sb[:, 0:1]),
                              reads=["ones8", "valid"], writes=["vA%d" % slot_v])
                        else:
                            A("act", lambda e: e.copy(out=vA[:, slot_v * 8:(slot_v + 1) * 8, 64], in_=ones8[:]), reads=["ones8"], writes=["vA%d" % slot_v])
                    if kv_out is not None:
                        st = kvst[t % 2]
                        sk = "kvst%d" % (t % 2)
                        A("act", lambda e: e.copy(out=st[:], in_=bank[:, 0:512]), reads=[bkey], writes=[sk])
                        A("act", lambda e: e.dma_start(out=kv_out[1][kv_out[2] + t * 128:kv_out[2] + (t + 1) * 128, :], in_=st[:]), reads=[sk], dma=True,
                          is_out=True, semkey=sk + "o")
                astyle_tile(s, 8, 512, hT, "hT", t * 128, 128, vev)
            if kind == "sample":
                A("dve", lambda e: e.memset(vnew[:, :, 64:65], 1.0), writes=["aT"])
                for b in range(4):
                    def vnev(bank, bkey, b=b):
                        A("act", lambda e: e.copy(out=vnew[:, b * 8:(b + 1) * 8, 0:64], in_=bank[0:32, 0:512].rearrange("p (h d) -> p h d", h=8)),
                          reads=[bkey], writes=["aT"])
                    astyle_tile(s, 8, 512, hT, "hT", b * 32, 32, vnev)
        if kind == "halo":
            return
        if full:
            s = load_w("w_in", 0, 8, QB, 512)
            bstyle(s, 8, 4, hT, "hT", Tg, ev_copy(lambda ob: qTb[:, ob, 0:Tg], lambda ob: "qTb", scale=128.0 ** -0.5))
        s = load_w("w_in", 0, 8, KB, 512)
        if full:
            bstyle(s, 8, 4, hT, "hT", Tg, ev_copy(lambda ob: kTb[:, ob, 0:Tg], lambda ob: "kTb"))
        for t in range(ntiles):
            def kbev(bank, bkey, t=t):
                A("act", lambda e: e.copy(out=kbt[:, t, :], in_=bank[:, 0:512]), reads=[bkey], writes=["kbt"])
            astyle_tile(s, 8, 512, hT, "hT", t * 128, 128, kbev)
        for half in range(2):
            s = load_w("w_in", 0, 8, VB + half * 512, 512)
            for t in range(ntiles):
                def vbev(bank, bkey, t=t, half=half):
                    A("act", lambda e: e.copy(out=vb[:, t, half * 512:(half + 1) * 512], in_=bank[:, 0:512]), reads=[bkey], writes=["vb"])
                astyle_tile(s, 8, 512, hT, "hT", t * 128, 128, vbev)
        if full:
            for half in range(2):
                s = load_w("w_in", 0, 8, RB + half * 512, 512)
                for t in range(ntiles):
                    def rbev(bank, bkey, t=t, half=half):
                        A("act", lambda e: e.activation(out=rbs[:, t, half * 512:(half + 1) * 512], in_=bank[:, 0:512], func=AF.Silu),
                          reads=[bkey], writes=["rbs"])
                        for j in range(2):
                            c0 = half * 512 + j * 256
                            A("dve", lambda e, c0=c0: e.tensor_tensor(out=rbs[:, t, c0:c0 + 256], in0=rbs[:, t, c0:c0 + 256], in1=gn_sb[:], op=ALU.mult),
                              reads=["rbs", "gn"], writes=["rbs"])
                    astyle_tile(s, 8, 512, hT, "hT", t * 128, 128, rbev)
        s = load_w("w_in", 0, 8, DLR, 16)
        def dlev(bank, bkey, ob):
            A("act", lambda e: e.copy(out=dlrT[0:16, 0:Tg], in_=bank[0:16, 0:Tg]), reads=[bkey], writes=["dlrT"])
        bstyle(s, 8, 1, hT, "hT", Tg, dlev, mrows=16)
        if full:
            for half in range(2):
                s = load_w("w_in", 0, 8, GA + half * 512, 512)
                bstyle(s, 8, 4, hT, "hT", Tg, ev_copy(lambda ob, half=half: gaT[:, half * 4 + ob, 0:Tg], lambda ob: "gaT", func=AF.Sigmoid))
            for half in range(2):
                s = load_w("w_in", 0, 8, GB + half * 512, 512)
                bstyle(s, 8, 4, hT, "hT", Tg, ev_copy(lambda ob, half=half: gbT[:, half * 4 + ob, 0:Tg], lambda ob: "gbT", func=AF.Sigmoid))

        if kind == "pre":
            for t in range(ntiles):
                for _ in gla_tile(t, 64, True):
                    pass
            return
        if kind == "prompt":
            ibk = {"la": 4, "laT": 6, "c": 4, "at": 6, "st": 6}
            gens = []
            for t in range(ntiles):
                gens += [attn_prompt_tile(t, a0 + t), gla_tile(t, 64, False, bk=ibk)]
            att = [g_ for i_, g_ in enumerate(gens) if i_ % 2 == 0]
            gl = [g_ for i_, g_ in enumerate(gens) if i_ % 2 == 1]
            while att or gl:
                if att:
                    try:
                        next(att[0])
                    except StopIteration:
                        att.pop(0)
                if gl:
                    try:
                        next(gl[0])
                    except StopIteration:
                        gl.pop(0)
        else:
            attn_sample()
            dump("s_oaT", oaT[:, :, 0:128], [128, 4, 128], BF16, "oaT")
            dump("s_qTa", qTa[:, :, 0:128], [128, 4, 128], BF16, "qTa")
            for b in range(4):
                A("pool", lambda e, b=b: e.dma_start(out=Sbs[:, b * 4:(b + 1) * 4, :], in_=sg[b].rearrange("h d v -> d h v")),
                  writes=["poolq", "aT"], dma=True, semkey="poolq")
            for _ in gla_tile(0, 32, False, sample=True):
                pass
            dump("s_obT", obT[:, :, 0:128], [128, 8, 128], BF16, "obT")

        for half in range(2):
            s = load_w("wpa", 0, 4, half * 512, 512)
            def paev(bank, bkey, ob, half=half):
                A("dve", lambda e: e.tensor_tensor(out=gaT[:, half * 4 + ob, 0:Tg], in0=gaT[:, half * 4 + ob, 0:Tg], in1=bank[:, 0:Tg], op=ALU.mult),
                  reads=[bkey, "gaT"], writes=["gaT"])
            bstyle(s, 4, 4, oaT, "oaT", Tg, paev)
        for half in range(2):
            s = load_w("wpb", 0, 8, half * 512, 512)
            def pbev(bank, bkey, ob, half=half):
                A("dve", lambda e: e.tensor_tensor(out=gbT[:, half * 4 + ob, 0:Tg], in0=gbT[:, half * 4 + ob, 0:Tg], in1=bank[:, 0:Tg], op=ALU.mult),
                  reads=[bkey, "gbT"], writes=["gbT"])
                A("dve", lambda e: e.tensor_tensor(out=gaT[:, half * 4 + ob, 0:Tg], in0=gaT[:, half * 4 + ob, 0:Tg], in1=gbT[:, half * 4 + ob, 0:Tg], op=ALU.add),
                  reads=["gbT", "gaT"], writes=["gaT"])
            bstyle(s, 8, 4, obT, "obT", Tg, pbev)

        def a_into_fsb(wname, nk_total, src, srckey):
            kgs = []
            k = 0
            while k < nk_total:
                kgs.append((k, min(8, nk_total - k)))
                k += 8
            for cb in range(2):
                for gi, (k0, nk) in enumerate(kgs):
                    s = load_w(wname, k0, nk, cb * 512, 512)
                    for t in range(ntiles):
                        def fev(bank, bkey, t=t, cb=cb, gi=gi):
                            if gi == 0:
                                A("act", lambda e: e.copy(out=fsb[:, t, cb * 512:(cb + 1) * 512], in_=bank[:, 0:512]), reads=[bkey], writes=["fsb%d" % t])
                            else:
                                A("dve", lambda e: e.tensor_tensor(out=fsb[:, t, cb * 512:(cb + 1) * 512], in0=fsb[:, t, cb * 512:(cb + 1) * 512],
                                                                   in1=bank[:, 0:512], op=ALU.add), reads=[bkey, "fsb%d" % t], writes=["fsb%d" % t])
                        astyle_tile(s, nk, 512, src, srckey, t * 128, 128, fev, kc_off=k0)

        def norm_residual(t, g_sb, gkey, dst, dkey):
            A("act", lambda e: e.activation(out=junk[:], in_=fsb[:, t, :], func=AF.Square, accum_out=ss[:, 0:1]), reads=["fsb%d" % t], writes=["junk", "ss"])
            A("act", lambda e: e.activation(out=rstd[:, 0:1], in_=ss[:, 0:1], func=AF.Ln, scale=1.0 / D, bias=EPS), reads=["ss"], writes=["rstd"])
            A("act", lambda e: e.activation(out=rstd[:, 0:1], in_=rstd[:, 0:1], func=AF.Exp, scale=-0.5), reads=["rstd"], writes=["rstd"])
            A("dve", lambda e: e.scalar_tensor_tensor(out=tmpf[:], in0=fsb[:, t, :], scalar=rstd[:, 0:1], in1=g_sb[:], op0=ALU.mult, op1=ALU.mult),
              reads=["fsb%d" % t, "rstd", gkey], writes=["tmpf"])
            A("dve", lambda e: e.tensor_tensor(out=dst, in0=tmpf[:], in1=xg[:, xpar * G + t, :], op=ALU.add), reads=["tmpf", "xg%d" % (xpar * G + t)],
              writes=[dkey])

        if kind == "sample":
            dump("s_mixT", gaT[:, :, 0:128], [128, 8, 128], BF16, "gaT")
        a_into_fsb("wout", 8, gaT, "gaT")
        TT = list(range(ntiles))
        xi_ = lambda t: xpar * G + t
        for t in TT:
            A("act", lambda e, t=t: e.activation(out=junk[:], in_=fsb[:, t, :], func=AF.Square, accum_out=ssn[:, t:t + 1]),
              reads=["fsb%d" % t], writes=["junk", "ssn%d" % t])
        for t in TT:
            A("act", lambda e, t=t: e.activation(out=rstdn[:, t:t + 1], in_=ssn[:, t:t + 1], func=AF.Ln, scale=1.0 / D, bias=EPS),
              reads=["ssn%d" % t], writes=["rstdn%d" % t])
        for t in TT:
            A("act", lambda e, t=t: e.activation(out=rstdn[:, t:t + 1], in_=rstdn[:, t:t + 1], func=AF.Exp, scale=-0.5),
              reads=["rstdn%d" % t], writes=["rstdn%d" % t])
        for t in TT:
            A("dve", lambda e, t=t: e.scalar_tensor_tensor(out=fsb[:, t, :], in0=fsb[:, t, :], scalar=rstdn[:, t:t + 1], in1=gpost_sb[:], op0=ALU.mult, op1=ALU.mult),
              reads=["fsb%d" % t, "rstdn%d" % t, "gpost"], writes=["fsb%d" % t])
        for t in TT:
            A("dve", lambda e, t=t: e.tensor_tensor(out=xg[:, xi_(t), :], in0=fsb[:, t, :], in1=xg[:, xi_(t), :], op=ALU.add),
              reads=["fsb%d" % t, "xg%d" % xi_(t)], writes=["xg%d" % xi_(t)])
        for t in TT:
            A("act", lambda e, t=t: e.activation(out=junk[:], in_=xg[:, xi_(t), :], func=AF.Square, accum_out=ssn[:, 2 + t:3 + t]),
              reads=["xg%d" % xi_(t)], writes=["junk", "ssn%d" % (2 + t)])
        for t in TT:
            A("act", lambda e, t=t: e.activation(out=rstdn[:, 2 + t:3 + t], in_=ssn[:, 2 + t:3 + t], func=AF.Ln, scale=1.0 / D, bias=EPS),
              reads=["ssn%d" % (2 + t)], writes=["rstdn%d" % (2 + t)])
        for t in TT:
            A("act", lambda e, t=t: e.activation(out=rstdn[:, 2 + t:3 + t], in_=rstdn[:, 2 + t:3 + t], func=AF.Exp, scale=-0.5),
              reads=["rstdn%d" % (2 + t)], writes=["rstdn%d" % (2 + t)])
        for t in TT:
            hbt = hb if t == 0 else hb2
            A("dve", lambda e, t=t, hbt=hbt: e.tensor_scalar(out=hbt[:], in0=xg[:, xi_(t), :], scalar1=rstdn[:, 2 + t:3 + t], scalar2=None, op0=ALU.mult),
              reads=["xg%d" % xi_(t), "rstdn%d" % (2 + t)], writes=["hb" if t == 0 else "hb2"])
        for t in TT:
            hbt = hb if t == 0 else hb2
            hk = "hb" if t == 0 else "hb2"
            for kc in range(8):
                A("pe", lambda e, kc=kc, hbt=hbt: e.transpose(out=pst[:, kc, :], in_=hbt[:, kc * 128:(kc + 1) * 128], identity=ident[:]),
                  reads=[hk, "ident"], writes=["pst"])
            for kc in range(8):
                A("act", lambda e, kc=kc, t=t: e.activation(out=hT[:, kc, t * 128:(t + 1) * 128], in_=pst[:, kc, :], func=AF.Copy,
                                                            scale=gffn_sb[:, kc:kc + 1]), reads=["pst", "gffn"], writes=["hT"])

        for it_, fb4 in enumerate(range(0, 22, 4)):
            nb = min(4, 22 - fb4)
            sg_ = load_w("wg", 0, 8, fb4 * 128, nb * 128, slot=(2 * it_) % 4)
            su_ = load_w("wu", 0, 8, fb4 * 128, nb * 128, slot=(2 * it_ + 1) % 4)
            for ob in range(nb):
                bg = 0 if ob % 2 == 0 else 2
                bu = 1 if ob % 2 == 0 else 3
                for kc in range(8):
                    A("pe", lambda e, kc=kc, ob=ob, sg_=sg_, bg=bg: e.matmul(ps[bg][:, 0:Tg], lhsT=wblk[sg_][:, kc, ob * 128:(ob + 1) * 128], rhs=hT[:, kc, 0:Tg],
                                                            start=(kc == 0), stop=(kc == 7)), reads=slotkeys[sg_] + ["hT"], writes=["ps%d" % bg])
                for kc in range(8):
                    A("pe", lambda e, kc=kc, ob=ob, su_=su_, bu=bu: e.matmul(ps[bu][:, 0:Tg], lhsT=wblk[su_][:, kc, ob * 128:(ob + 1) * 128], rhs=hT[:, kc, 0:Tg],
                                                            start=(kc == 0), stop=(kc == 7)), reads=slotkeys[su_] + ["hT"], writes=["ps%d" % bu])
                A("act", lambda e, bg=bg: e.activation(out=junk[:, 0:Tg], in_=ps[bg][:, 0:Tg], func=AF.Silu), reads=["ps%d" % bg], writes=["junk"])
                A("dve", lambda e, ob=ob, fb4=fb4, bu=bu: e.tensor_tensor(out=aT[:, fb4 + ob, 0:Tg], in0=junk[:, 0:Tg], in1=ps[bu][:, 0:Tg], op=ALU.mult),
                  reads=["junk", "ps%d" % bu], writes=["aT"])
        if next_prep is not None:
            next_prep()
        a_into_fsb("wd", 22, aT, "aT")
        for t in range(ntiles):
            yst = ystage[t % 2]
            yk = "yst%d" % (t % 2)
            norm_residual(t, gfpost_sb, "gfpost", yst[:, 0:D], yk)
            A("act", lambda e, t=t, yst=yst: e.dma_start(out=out_ap[row0 + t * 128:row0 + (t + 1) * 128, :], in_=yst[:, 0:D]), reads=[yk], dma=True,
              is_out=True, semkey=yk + "o")

    A("sp", lambda e: e.dma_start(out=dlrT[16:32, :], in_=ones_row[:, :]), writes=["dlrT"], dma=True, semkey="dlr1")

    A("dve", lambda e: e.tensor_copy(out=wup_bf[:], in_=wup_sb[:]), reads=["wup"], writes=["wup_bf"])
    A("dve", lambda e: e.tensor_copy(out=utf_bf[:], in_=cm[64][:, 640:768]), reads=["cm64"], writes=["utf_bf"])
    A("dve", lambda e: e.memset(onesb[:], 1.0), writes=["onesb"])
    skb, sv0, sv1 = 0, 1, 2
    wctr[0] = 3

    xgf = xg[:].rearrange("p a b -> p (a b)")
    w_in3 = w_in.rearrange("(k p) n -> p k n", p=128)
    for slot_, c0_ in ((skb, KB), (sv0, VB), (sv1, VB + 512)):
        for half_ in range(2):
            A("sp", lambda e, c0_=c0_, half_=half_: e.dma_start(out=xgf[:, 0:2048].rearrange("p (k n) -> p k n", k=4),
                                                                  in_=w_in3[:, half_ * 4:(half_ + 1) * 4, c0_:c0_ + 512]),
              writes=["xg0", "xg1"], dma=True, semkey="xg0")
            for j in range(4):
                kc = half_ * 4 + j
                if j % 2 == 0:
                    A("act", lambda e, slot_=slot_, kc=kc, j=j: e.activation(out=wblk[slot_][:, kc, :], in_=xgf[:, j * 512:(j + 1) * 512], func=AF.Copy,
                                                                            scale=gpre_sb[:, kc:kc + 1]),
                      reads=["xg0", "xg1", "gpre"], writes=["w%d" % slot_])
                else:
                    A("dve", lambda e, slot_=slot_, kc=kc, j=j: e.tensor_scalar(out=wblk[slot_][:, kc, :], in0=xgf[:, j * 512:(j + 1) * 512],
                                                                                scalar1=gpre_sb[:, kc:kc + 1], scalar2=None, op0=ALU.mult),
                      reads=["xg0", "xg1", "gpre"], writes=["w%d" % slot_])
    A("sp", lambda e: e.dma_start(out=tmpf[:, 0:128].rearrange("p (k n) -> p k n", k=8), in_=w_in3[:, :, DLR:DLR + 16]), writes=["tmpf"], dma=True,
      semkey="tmpfw")
    for kc in range(8):
        A("dve", lambda e, kc=kc: e.tensor_scalar(out=wdlr[:, kc, :], in0=tmpf[:, kc * 16:(kc + 1) * 16], scalar1=gpre_sb[:, kc:kc + 1], scalar2=None,
                                                  op0=ALU.mult), reads=["tmpf", "gpre"], writes=["wdlr"])
    A("dve", lambda e: e.memset(ssp[:], 0.0), reads=["tmpf"], writes=["tmpf0", "tmpf1", "ssp0", "ssp1"])

    def pk(name, i, depth=2):
        return "pp_%s%d" % (name, i % depth)

    def pp_s0(i):
        p = i % 2
        xt = xg[:, p, :]
        A("sp", lambda e: e.dma_start(out=xt, in_=xpre[i * 128:(i + 1) * 128, :]), writes=["xg%d" % p], dma=True)
        A("act", lambda e: e.activation(out=junk[:], in_=xt, func=AF.Square, accum_out=ssp[:, p:p + 1]), reads=["xg%d" % p], writes=["junk", "ssp%d" % p])
        A("act", lambda e: e.activation(out=rstdp[:, p:p + 1], in_=ssp[:, p:p + 1], func=AF.Ln, scale=1.0 / D, bias=EPS),
          reads=["ssp%d" % p], writes=["rstdp%d" % p])
        A("act", lambda e: e.activation(out=rstdp[:, p:p + 1], in_=rstdp[:, p:p + 1], func=AF.Exp, scale=-0.5),
          reads=["rstdp%d" % p], writes=["rstdp%d" % p])
        A("dve", lambda e: e.tensor_scalar(out=pp_hb[p], in0=xt, scalar1=rstdp[:, p:p + 1], scalar2=None, op0=ALU.mult),
          reads=["xg%d" % p, "rstdp%d" % p], writes=[pk("hb", i)])

    def pp_s1(i):
        p = i % 2
        for kc in range(8):
            A("pe", lambda e, kc=kc: e.transpose(out=pst[:, kc, :], in_=pp_hb[p][:, kc * 128:(kc + 1) * 128], identity=ident[:]),
              reads=[pk("hb", i), "ident"], writes=["pst"])
        if p == 0:
            A("dve", lambda e: e.tensor_copy(out=pp_hT[p], in_=pst[:, :, :]), reads=["pst"], writes=[pk("hT", i)])
        else:
            A("act", lambda e: e.copy(out=pp_hT[p], in_=pst[:, :, :]), reads=["pst"], writes=[pk("hT", i)])

    def pp_s2(i):
        p = i % 2
        q4 = i % 4
        for (slot, bank) in ((skb, 0), (sv0, 1), (sv1, 2)):
            for kc in range(8):
                A("pe", lambda e, kc=kc, slot=slot, bank=bank: e.matmul(ps[bank][:, :], lhsT=pp_hT[p][:, kc, :], rhs=wblk[slot][:, kc, :],
                                                                        start=(kc == 0), stop=(kc == 7)),
                  reads=[pk("hT", i), "w%d" % slot], writes=["ps%d" % bank])
        A("act", lambda e: e.copy(out=pp_kbt[q4], in_=ps[0][:, :]), reads=["ps0"], writes=[pk("kbt", i, 4)])
        A("act", lambda e: e.copy(out=pp_vb[q4][:, 0:512], in_=ps[1][:, :]), reads=["ps1"], writes=[pk("vba", i, 4)])
        A("dve", lambda e: e.tensor_copy(out=pp_vb[q4][:, 512:1024], in_=ps[2][:, :]), reads=["ps2"], writes=[pk("vbb", i, 4)])

    def pp_s2b(i):
        p = i % 2
        for kc in range(8):
            A("pe", lambda e, kc=kc: e.matmul(ps[3][0:16, 0:128], lhsT=wdlr[:, kc, :], rhs=pp_hT[p][:, kc, :], start=(kc == 0), stop=(kc == 7)),
              reads=[pk("hT", i), "wdlr"], writes=["ps3"])
        A("act", lambda e: e.copy(out=dlrTb[0:16, p * 128:(p + 1) * 128], in_=ps[3][0:16, 0:128]), reads=["ps3"], writes=[pk("dl", i)])

    def pp_s3(i):
        p = i % 2
        A("pe", lambda e: e.matmul(ps[4][:, :], lhsT=dlrTb[0:32, p * 128:(p + 1) * 128], rhs=wup_bf[0:32, :], start=True, stop=True),
          reads=[pk("dl", i), "dlrTb", "wup_bf"], writes=["ps4"])
        A("act", lambda e: e.activation(out=tmpf[:, p * 512:(p + 1) * 512], in_=ps[4][:, :], func=AF.Exp, scale=-1.0), reads=["ps4"], writes=["tmpf%d" % p])
        A("act", lambda e: e.activation(out=pp_la[p], in_=tmpf[:, p * 512:(p + 1) * 512], func=AF.Ln, bias=1.0), reads=["tmpf%d" % p], writes=[pk("la", i)])

    def pp_s4(i):
        p = i % 2
        q4 = i % 4
        A("pe", lambda e: e.matmul(ps[5][:, :], lhsT=utf_bf[:], rhs=pp_la[p], start=True, stop=True), reads=[pk("la", i), "utf_bf"], writes=["ps5"])
        for h in range(4):
            A("pe", lambda e, h=h: e.matmul(ps[3][:, 256 + h * 2:258 + h * 2], lhsT=pp_la[p][:, h * 128:(h + 1) * 128], rhs=onesb[:, 0:2],
                                            start=True, stop=True), reads=[pk("la", i), "onesb"], writes=["ps3"])
        A("act", lambda e: e.activation(out=pp_ec[p], in_=ps[5][:, :], func=AF.Exp, scale=-1.0 / 16), reads=["ps5"], writes=[pk("ec", i)])
        A("act", lambda e: e.activation(out=etile[:, p, :], in_=ps[3][:, 256:264].rearrange("p (h n) -> p h n", n=2)[:, :, 0], func=AF.Exp,
                                        scale=-1.0 / 16), reads=["ps3"], writes=["etile%d" % p])
        A("dve", lambda e: e.tensor_tensor(out=pp_khat[p], in0=pp_kbt[q4], in1=pp_ec[p], op=ALU.mult), reads=[pk("kbt", i, 4), pk("ec", i)],
          writes=[pk("kh", i)])

    def pp_s5(i):
        p = i % 2
        q4 = i % 4
        for hp in range(2):
            bank = 6 if hp == 0 else 0
            for h in (2 * hp, 2 * hp + 1):
                A("pe", lambda e, h=h, bank=bank: e.matmul(ps[bank][:, (h % 2) * 256:(h % 2 + 1) * 256], lhsT=pp_khat[p][:, h * 128:(h + 1) * 128],
                                                           rhs=pp_vb[q4][:, h * 256:(h + 1) * 256], start=True, stop=True),
                  reads=[pk("kh", i), pk("vba", i, 4), pk("vbb", i, 4)], writes=["ps%d" % bank])
        for hp in range(2):
            bank = 6 if hp == 0 else 0
            for h in (2 * hp, 2 * hp + 1):
                A("dve", lambda e, h=h, bank=bank: e.scalar_tensor_tensor(out=S[:, h, :], in0=S[:, h, :], scalar=etile[:, p, h:h + 1],
                                                                          in1=ps[bank][:, (h % 2) * 256:(h % 2 + 1) * 256], op0=ALU.mult, op1=ALU.add),
                  reads=["S%d" % h, "etile%d" % p, "ps%d" % bank], writes=["S%d" % h])

    NPRE = 7 * NT_P
    for n in range(-2, NPRE + 3):
        for st, off in ((pp_s0, 2), (pp_s1, 1), (pp_s2, 0), (pp_s5, -3), (pp_s2b, 0), (pp_s3, -1), (pp_s4, -2)):
            i = n + off
            if 0 <= i < NPRE:
                st(i)
    A("act", lambda e: e.copy(out=Sbf[0][:], in_=S[:]), reads=["S%d" % h for h in range(4)] + ["S"], writes=["Sbf0"])
    ppkeys = ["S%d" % h for h in range(4)] + ["junk", "tmpf0", "tmpf1", "wdlr", "dlrTb", "wup_bf", "utf_bf", "onesb"]
    for p_ in range(4):
        ppkeys += ["pp_kbt%d" % p_, "pp_vba%d" % p_, "pp_vbb%d" % p_]
    for p_ in range(2):
        ppkeys += ["pp_hb%d" % p_, "pp_hT%d" % p_, "pp_la%d" % p_, "pp_ec%d" % p_, "pp_kh%d" % p_, "pp_dl%d" % p_, "etile%d" % p_, "ssp%d" % p_, "rstdp%d" % p_]
    A("dve", lambda e: e.memset(dlrT[0:16, :], 0.0), reads=ppkeys,
      writes=["dlrT", "aT", "obT", "gaT", "gbT", "hT", "tmpf", "fsb0", "fsb1", "S", "ps6", "pst", "ps0", "ps1", "ps2", "ps3", "ps4", "ps5", "xg0", "xg1"])
    for g in range(4 // G):
        run_group("halo", xh, g * T, G, a0=-4 + g * G)
    run_group("sample", xs, 0, 1, out_ap=ys, kv_out=(ksn, vsn, 0))
    for i in range(2):
        A("dve", lambda e, i=i: e.memset(qz[i][:], 0.0), writes=["qz%d" % i])
    ngroups = NT_P // G
    for g in range(ngroups):
        rbase = (g * G - (NT_P - 4)) * 128
        nxt = None
        if g + 1 < ngroups:
            nxt = (lambda g=g: prep_group(xp, (g + 1) * T, G, (g + 1) % 2))
        run_group("prompt", xp, g * T, G, a0=g * G, out_ap=yp, kv_out=(kp, vp, rbase) if rbase >= 0 else None,
                  xpar=g % 2, prefetched=(g > 0), next_prep=nxt)
    A("act", lambda e: e.dma_start(out=spo[:, :], in_=S[:].rearrange("p h v -> p (h v)")), reads=["S"], dma=True, is_out=True, semkey="spo")
    P.finish()
    P.emit()
    return nc, P


def _const_masks(c):
    n = 128
    s = np.arange(n)[:, None]
    t = np.arange(n)[None, :]
    same = (s // c) == (t // c)
    UT = ((s > t) & same).astype(np.float32)
    maskT = ((s <= t) & same).astype(np.float32)
    restart = np.ones((128, 4, 128), np.float32)
    restart[:, :, ::c] = 0.0
    nch = n // c
    rm = np.zeros((128, nch), np.float32)
    for i in range(nch):
        rm[i * c:(i + 1) * c, i] = 1.0
    out = np.zeros((128, 6 * 128 + 512 + nch), np.float32)
    out[:, 0:128] = UT
    out[:, 128:640] = np.tile(maskT, (1, 4))
    out[:, 640:768] = (s > t).astype(np.float32)
    out[:, 768:1280] = restart.reshape(128, 512)
    out[:, 1280:1280 + nch] = rm
    return out


def _bias_tables(rel):
    p = np.arange(128)[:, None]
    col = np.arange(128)[None, :]
    kc, kl = p // 64, p % 64
    qc, ql = col // 64, col % 64
    idxP = np.zeros((3, 128, 128), np.int64)
    mk = np.zeros((128, 3, 128), np.float32)
    for j, d in enumerate((0, 1, 4)):
        o = 2 * d + qc - kc
        dist = 64 * o + ql - kl
        idxP[j] = np.clip(dist, -128, 128) + 128
        mk[:, j, :] = np.where((o >= 0) & (o <= 8), 0.0, NEG)
    biasP = rel[:, idxP]
    biasP = np.ascontiguousarray(biasP.transpose(2, 0, 1, 3)).reshape(128, 8 * 384)
    q32 = np.arange(32)[None, :]
    d0 = q32 + 512 - (384 + p)
    d1 = q32 - np.minimum(p, 31)
    idxS = np.stack([np.clip(d0, -128, 128) + 128, np.clip(d1, -128, 128) + 128], 0)
    biasS = rel[:, idxS]
    biasS = np.ascontiguousarray(biasS.transpose(2, 0, 1, 3)).reshape(128, 8 * 64)
    cb = np.ascontiguousarray(np.broadcast_to(rel[:, 256][None, :], (128, 8)))
    return biasP.astype(np.float32), mk.reshape(128, 384), biasS.astype(np.float32), cb.astype(np.float32)


_CACHE = {}


def kernel(x_prompt, x_sample, cache_attn_k, cache_attn_v, state_gla,
           norm_mix_pre, norm_mix_post, norm_ffn_pre, norm_ffn_post,
           w_in, w_decay_up, b_decay, rel_bias, gla_norm, w_proj_a, w_proj_b, w_out,
           w_ffn_gate, w_ffn_up, w_ffn_down):
    f = lambda a: np.ascontiguousarray(np.asarray(a, dtype=np.float32))
    x_prompt, x_sample = f(x_prompt), f(x_sample)
    if "nc" not in _CACHE:
        _CACHE["nc"] = build_program()
    nc, P = _CACHE["nc"]
    xpf = x_prompt.reshape(SEQ, D)
    wup = np.zeros((32, 512), np.float32)
    wup[0:16] = f(w_decay_up)[0]
    wup[16] = f(b_decay)[0]
    ones_row = np.zeros((16, G * 128), np.float32)
    ones_row[0] = 1.0
    bP, mkP, bS, cb = _bias_tables(f(rel_bias)[0])
    shared = {
        "w_in": f(w_in)[0], "wup": wup,
        "gpre": np.ascontiguousarray(f(norm_mix_pre)[0].reshape(8, 128).T),
        "gffn": np.ascontiguousarray(f(norm_ffn_pre)[0].reshape(8, 128).T),
        "gpost": f(norm_mix_post)[0].reshape(1, D), "gfpost": f(norm_ffn_post)[0].reshape(1, D),
        "gnorm": f(gla_norm)[0].reshape(1, 256),
        "wpa": f(w_proj_a)[0], "wpb": f(w_proj_b)[0], "wout": f(w_out)[0],
        "wg": f(w_ffn_gate)[0], "wu": f(w_ffn_up)[0], "wd": f(w_ffn_down)[0],
        "biasP": bP, "mkP": mkP, "biasS": bS, "cbias": cb,
        "cm64": _const_masks(64), "cm32": _const_masks(32), "ones_row": ones_row,
    }
    ck = f(cache_attn_k)[0].reshape(32, 512, 512)
    cv = f(cache_attn_v)[0].reshape(32, 512, 512)
    sgl = f(state_gla)[0]
    in_maps = []
    for c in range(NCORES):
        m = dict(shared)
        m["xp"] = xpf[c * TOK_CORE:(c + 1) * TOK_CORE]
        m["xh"] = xpf[c * TOK_CORE - 512:c * TOK_CORE] if c > 0 else np.zeros((512, D), np.float32)
        xpre = np.zeros((7 * TOK_CORE, D), np.float32)
        if c > 0:
            xpre[(7 - c) * TOK_CORE:] = xpf[0:c * TOK_CORE]
        m["xpre"] = xpre
        m["xs"] = x_sample[4 * c:4 * c + 4].reshape(128, D)
        m["ck"] = ck[4 * c:4 * c + 4]
        m["cv"] = cv[4 * c:4 * c + 4]
        m["sg"] = sgl[4 * c:4 * c + 4]
        m["valid"] = np.full((128, 1), 1.0 if c > 0 else 0.0, np.float32)
        in_maps.append(m)
    res = run_bass_kernel_spmd(nc, in_maps, core_ids=list(range(NCORES)))
    R = res.results
    yp = np.concatenate([R[c]["yp"] for c in range(NCORES)], 0).reshape(1, SEQ, D)
    ys = np.concatenate([R[c]["ys"] for c in range(NCORES)], 0).reshape(32, 32, D)
    kpo = R[NCORES - 1]["kp"].reshape(1, 1, 512, 8, 64)
    vpo = R[NCORES - 1]["vp"].reshape(1, 1, 512, 8, 64)
    spo = R[NCORES - 1]["spo"].reshape(128, 4, 256).transpose(1, 0, 2).reshape(1, 1, 4, 128, 256)
    kso = np.concatenate([R[c]["ksn"] for c in range(NCORES)], 0).reshape(1, 32, 32, 8, 64)
    vso = np.concatenate([R[c]["vsn"] for c in range(NCORES)], 0).reshape(1, 32, 32, 8, 64)
    sso = np.concatenate([R[c]["sso"] for c in range(NCORES)], 0).reshape(32, 128, 4, 256).transpose(0, 2, 1, 3).reshape(1, 32, 4, 128, 256)
    return (yp, ys, kpo, vpo, np.ascontiguousarray(spo), kso, vso, np.ascontiguousarray(sso))
```

```python
import contextlib
import numpy as np
import concourse.bass as bass
import concourse.mybir as mybir
from concourse.bass_utils import run_bass_kernel_spmd

F32 = mybir.dt.float32
BF16 = mybir.dt.bfloat16
AF = mybir.ActivationFunctionType
ALU = mybir.AluOpType

ENGS = ("pe", "act", "dve", "pool", "sp")


class Op:
    __slots__ = ("eng", "fn", "deps", "idx", "sig", "seq", "is_dma", "semkey", "sem", "semval", "inc")


class KeyState:
    __slots__ = ("writers", "readers")

    def __init__(self):
        self.writers = {}
        self.readers = {}


class Prog:
    def __init__(self, nc):
        self.nc = nc
        self.ops = []
        self.state = {}
        self.stack = contextlib.ExitStack()
        self.out_dmas = []
        self.last_dma = {}
        self.cc_barrier = None

    def sbuf(self, name, shape, dtype):
        return self.stack.enter_context(self.nc.sbuf_tensor(name, list(shape), dtype))

    def psum(self, name, shape, dtype):
        return self.stack.enter_context(self.nc.psum_tensor(name, list(shape), dtype))

    def add(self, eng, fn, reads=(), writes=(), dma=False, semkey=None, is_out=False, inc=16, barrier=False):
        op = Op()
        op.eng = eng
        op.fn = fn
        op.idx = len(self.ops)
        op.sig = False
        op.seq = None
        op.is_dma = dma
        op.semkey = None
        op.sem = None
        op.semval = None
        op.inc = inc
        deps = {}
        ek = ("dma", op.idx) if dma else eng
        psr = [k for k in reads if k.startswith("ps")]
        if psr:
            reads = [k for k in reads if not k.startswith("ps")]
            writes = list(writes) + [k for k in psr if k not in writes]
        for k in reads:
            st = self.state.get(k)
            if st is None:
                st = self.state[k] = KeyState()
            for d in st.writers.values():
                deps[d.idx] = d
        for k in writes:
            st = self.state.get(k)
            if st is None:
                st = self.state[k] = KeyState()
            for d in st.writers.values():
                deps[d.idx] = d
            for d in st.readers.values():
                deps[d.idx] = d
        for k in reads:
            self.state[k].readers[ek] = op
        for k in writes:
            st = self.state[k]
            st.writers = {ek: op}
            st.readers = {}
        if dma:
            if barrier:
                for d in self.last_dma.values():
                    deps[d.idx] = d
                self.cc_barrier = op
            elif self.cc_barrier is not None:
                deps[self.cc_barrier.idx] = self.cc_barrier
        deps.pop(op.idx, None)
        op.deps = list(deps.values())
        if dma:
            if semkey is None:
                semkey = writes[0] if writes else reads[0]
            op.semkey = semkey
            self.last_dma[semkey] = op
            if is_out:
                self.out_dmas.append(op)
        self.ops.append(op)
        return op

    def finish(self):
        op = self.add("sp", None)
        op.deps = list(self.out_dmas)

    def emit(self):
        nc = self.nc

        def need_wait(op, d):
            if d.is_dma:
                return True
            if d.eng == op.eng:
                if op.is_dma:
                    return True
                if op.eng == "pe":
                    return False
                return True
            return True

        for op in self.ops:
            for d in op.deps:
                if not d.is_dma and need_wait(op, d):
                    d.sig = True
        counters = {e: 0 for e in ENGS}
        for op in self.ops:
            if op.sig and not op.is_dma:
                counters[op.eng] += 1
                op.seq = counters[op.eng]
        semkeys = []
        seen = set()
        for op in self.ops:
            if op.is_dma and op.semkey not in seen:
                seen.add(op.semkey)
                semkeys.append(op.semkey)
        self.n_sems = len(semkeys) + len(ENGS)
        engsem = {e: self.stack.enter_context(nc.semaphore("s_" + e)) for e in ENGS}
        dmasem = {k: self.stack.enter_context(nc.semaphore("d_%d" % i)) for i, k in enumerate(semkeys)}
        dmacnt = {k: 0 for k in semkeys}
        for op in self.ops:
            if op.is_dma:
                dmacnt[op.semkey] += op.inc
                op.sem = dmasem[op.semkey]
                op.semval = dmacnt[op.semkey]
        per_eng = {e: [o for o in self.ops if o.eng == e] for e in ENGS}

        def run(e, engobj):
            waited = {}
            for op in per_eng[e]:
                for d in op.deps:
                    if not need_wait(op, d):
                        continue
                    if d.is_dma:
                        key = ("d", d.semkey)
                        val = d.semval
                        sem = d.sem
                    else:
                        key = ("e", d.eng)
                        val = d.seq
                        sem = engsem[d.eng]
                    if waited.get(key, 0) >= val:
                        continue
                    waited[key] = val
                    engobj.wait_ge(sem, val)
                if op.fn is None:
                    continue
                ins = op.fn(engobj)
                if op.is_dma:
                    if op.inc == 16:
                        ins.then_inc(op.sem, 16)
                    else:
                        ins.then_inc(op.sem)
                elif op.sig:
                    ins.then_inc(engsem[e], 1)

        with nc.Block() as block:
            @block.sync
            def _(eng):
                run("sp", eng)

            @block.tensor
            def _(eng):
                run("pe", eng)

            @block.scalar
            def _(eng):
                run("act", eng)

            @block.vector
            def _(eng):
                run("dve", eng)

            @block.gpsimd
            def _(eng):
                run("pool", eng)

    def close(self):
        self.stack.close()


D = 1024
DIN = 6672
DFF = 2816
NCORES = 8
SEQ = 16384
TOK_CORE = SEQ // NCORES
NT_P = TOK_CORE // 128
G = 2
QA, KA, VA, QB, KB, VB, RB, DLR, GA, GB = 0, 512, 1024, 1536, 2048, 2560, 3584, 4608, 4624, 5648
EPS = 1e-6
NEG = -30000.0


def build_program(debug=False):
    nc = bass.Bass("TRN2", target_bir_lowering=False)
    dbg = {}
    P = Prog(nc)
    A = P.add

    def din(name, shape):
        return nc.dram_tensor(name, list(shape), F32, kind="ExternalInput").ap()

    def dout(name, shape):
        return nc.dram_tensor(name, list(shape), F32, kind="ExternalOutput").ap()

    xp = din("xp", [TOK_CORE, D])
    xh = din("xh", [512, D])
    xpre = din("xpre", [7 * TOK_CORE, D])
    ones_row = din("ones_row", [16, G * 128])
    xs = din("xs", [128, D])
    ck = din("ck", [4, 512, 512])
    cv = din("cv", [4, 512, 512])
    sg = din("sg", [4, 4, 128, 256])
    valid = din("valid", [128, 1])
    w_in = din("w_in", [D, DIN])
    wup = din("wup", [32, 512])
    gpre = din("gpre", [128, 8])
    gffn = din("gffn", [128, 8])
    gpost = din("gpost", [1, D])
    gfpost = din("gfpost", [1, D])
    gnorm = din("gnorm", [1, 256])
    wpa = din("wpa", [512, D])
    wpb = din("wpb", [D, D])
    wout = din("wout", [D, D])
    wg = din("wg", [D, DFF])
    wu = din("wu", [D, DFF])
    wd = din("wd", [DFF, D])
    biasP = din("biasP", [128, 8 * 384])
    mkP = din("mkP", [128, 384])
    biasS = din("biasS", [128, 8 * 64])
    cbias = din("cbias", [128, 8])
    cm64 = din("cm64", [128, 6 * 128 + 512 + 2])
    cm32 = din("cm32", [128, 6 * 128 + 512 + 4])

    yp = dout("yp", [TOK_CORE, D])
    ys = dout("ys", [128, D])
    kp = dout("kp", [512, 512])
    vp = dout("vp", [512, 512])
    spo = dout("spo", [128, 1024])
    ksn = dout("ksn", [128, 512])
    vsn = dout("vsn", [128, 512])
    sso = dout("sso", [4, 128, 1024])

    wsrc = {"w_in": (w_in, D, DIN), "wpa": (wpa, 512, D), "wpb": (wpb, D, D), "wout": (wout, D, D),
            "wg": (wg, D, DFF), "wu": (wu, D, DFF), "wd": (wd, DFF, D)}

    T = G * 128
    xg = P.sbuf("xg", [128, 2 * G, D], F32)
    hb = P.sbuf("hb", [128, D], BF16)
    hb2 = P.sbuf("hb2", [128, D], BF16)
    hT = P.sbuf("hT", [128, 8, T], BF16)
    qTa = P.sbuf("qTa", [128, 4, T], BF16)
    kTa = P.sbuf("kTa", [128, 4, 8 * 128], BF16)
    vA = P.sbuf("vA", [128, 64, 65], BF16)
    qTb = P.sbuf("qTb", [128, 4, T], BF16)
    kTb = P.sbuf("kTb", [128, 4, T], BF16)
    kbt = P.sbuf("kbt", [128, G, 512], BF16)
    vb = P.sbuf("vb", [128, G, 1024], BF16)
    rbs = P.sbuf("rbs", [128, G, 1024], BF16)
    dlrT = P.sbuf("dlrT", [32, T], F32)
    gaT = P.sbuf("gaT", [128, 8, T], BF16)
    gbT = P.sbuf("gbT", [128, 8, T], BF16)
    oaT = P.sbuf("oaT", [128, 4, T], BF16)
    obT = P.sbuf("obT", [128, 8, T], BF16)
    aT = P.sbuf("aT", [128, 22, T], BF16)
    fsb = P.sbuf("fsb", [128, G, D], F32)
    wblk = [P.sbuf("wblk%d" % i, [128, 8, 512], BF16) for i in range(3)]
    ystage = [P.sbuf("yst%d" % i, [128, D], F32) for i in range(2)]
    kvst = [P.sbuf("kvst%d" % i, [128, 512], F32) for i in range(2)]
    ident = P.sbuf("ident", [128, 128], BF16)
    gpre_sb = P.sbuf("gpre_sb", [128, 8], F32)
    gffn_sb = P.sbuf("gffn_sb", [128, 8], F32)
    gpost_sb = P.sbuf("gpost_sb", [128, D], F32)
    gfpost_sb = P.sbuf("gfpost_sb", [128, D], F32)
    gn_sb = P.sbuf("gn_sb", [128, 256], F32)
    wup_sb = P.sbuf("wup_sb", [32, 512], F32)
    biasP_sb = P.sbuf("biasP_sb", [128, 8, 384], BF16)
    biasS_sb = P.sbuf("biasS_sb", [128, 8, 64], BF16)
    cbias_sb = P.sbuf("cbias_sb", [128, 8], F32)
    valid_sb = P.sbuf("valid_sb", [128, 1], F32)
    cm = {64: P.sbuf("cm64_sb", [128, 6 * 128 + 512 + 2], F32), 32: P.sbuf("cm32_sb", [128, 6 * 128 + 512 + 4], F32)}
    ones8 = P.sbuf("ones8", [128, 8], F32)
    ss = P.sbuf("ss", [128, 4], F32)
    ssn = P.sbuf("ssn", [128, 4], F32)
    rstdn = P.sbuf("rstdn", [128, 4], F32)
    rstd = P.sbuf("rstd", [128, 4], F32)
    junk = P.sbuf("junk", [128, D], BF16)
    tmpf = P.sbuf("tmpf", [128, D], F32)
    mkP_sb = tmpf[:, 0:384]
    identf = tmpf[:, 512:640]
    stmp2 = [P.sbuf("stmp%d" % i, [128, 3, 128], F32) for i in range(2)]
    expT2 = [P.sbuf("expT%d" % i, [128, 5, 128], BF16) for i in range(2)]
    rden = P.sbuf("rden", [128, 8], F32)
    oa_tok = P.sbuf("oa_tok", [128, 512], BF16)
    ob_tok = P.sbuf("ob_tok", [128, 1024], BF16)
    la_neg = P.sbuf("la_neg", [128, 512], F32)
    laT_neg = P.sbuf("laT_neg", [128, 512], F32)
    bneg = P.sbuf("bneg", [128, 512], F32)
    eb = P.sbuf("eb", [128, 512], F32)
    enb = P.sbuf("enb", [128, 512], F32)
    ec = P.sbuf("ec", [128, 512], F32)
    qtl = P.sbuf("qtl", [128, 4, 128], BF16)
    ktl = P.sbuf("ktl", [128, 4, 128], BF16)
    qz = [P.sbuf("qz%d" % i, [128, 4, 128], BF16) for i in range(4)]
    khat = [P.sbuf("khat%d" % i, [128, 512], BF16) for i in range(4)]
    attn_sb = P.sbuf("attn_sb", [128, 4, 128], BF16)
    S = P.sbuf("S", [128, 4, 256], F32)
    Sbf = [P.sbuf("Sbf%d" % i, [128, 4, 256], BF16) for i in range(2)]
    vc = P.sbuf("vc", [128, 32, 65], BF16)
    rdens = P.sbuf("rdens", [32, 8], F32)
    stmps2 = [P.sbuf("stmps%d" % i, [128, 2, 32], F32) for i in range(2)]
    expTs2 = [P.sbuf("expTs%d" % i, [128, 5, 32], BF16) for i in range(2)]

    vnew = aT[0:32, 0:9, :].rearrange("p a b -> p (a b)")[:, 0:2080].rearrange("p (n d) -> p n d", d=65)
    oas = aT[0:32, 9:18, :].rearrange("p a b -> p (a b)")[:, 0:2048].rearrange("p (n d) -> p n d", d=512)
    kc_tok = tmpf[:].bitcast(BF16).rearrange("p (t f) -> p t f", t=4)
    Sbs = aT[:, 6:22, :]
    kTc = obT[:].rearrange("p a b -> p (a b)").rearrange("p (a b) -> p a b", a=4)
    wdlr = P.sbuf("wdlr", [128, 8, 16], BF16)
    wup_bf = P.sbuf("wup_bf", [32, 512], BF16)
    dlrTb = P.sbuf("dlrTb", [32, 256], BF16)
    onesb = P.sbuf("onesb", [128, 2], BF16)
    utf_bf = P.sbuf("utf_bf", [128, 128], BF16)
    etile = P.sbuf("etile", [128, 2, 4], F32)
    ssp = P.sbuf("ssp", [128, 2], F32)
    rstdp = P.sbuf("rstdp", [128, 2], F32)
    aTf = aT[:].rearrange("p a b -> p (a b)")
    pp_hb = [aTf[:, i * 1024:(i + 1) * 1024] for i in range(2)]
    pp_hT = [aTf[:, 2048 + i * 1024:2048 + (i + 1) * 1024].rearrange("p (k n) -> p k n", k=8) for i in range(2)]
    obTf = obT[:].rearrange("p a b -> p (a b)")
    pp_kbt = [obTf[:, i * 512:(i + 1) * 512] for i in range(4)]
    gaTf = gaT[:].rearrange("p a b -> p (a b)")
    gbTf = gbT[:].rearrange("p a b -> p (a b)")
    pp_vb = [gaTf[:, 0:1024], gaTf[:, 1024:2048], gbTf[:, 0:1024], gbTf[:, 1024:2048]]
    hTf = hT[:].rearrange("p a b -> p (a b)")
    pp_la = [hTf[:, i * 512:(i + 1) * 512] for i in range(2)]
    pp_khat = [hTf[:, 1024 + i * 512:1024 + (i + 1) * 512] for i in range(2)]
    fsbf = fsb[:].rearrange("p a b -> p (a b)")
    pp_ec = [fsbf[:, i * 512:(i + 1) * 512] for i in range(2)]
    ps = [P.psum("ps%d" % i, [128, 512], F32) for i in range(7)]
    pst = P.psum("pst", [128, 8, 128], BF16)

    def dump(name, ap_sb, shape, dtype, key):
        if not debug:
            return
        t = nc.dram_tensor("dbg_" + name, list(shape), F32, kind="ExternalOutput").ap()
        if dtype == F32:
            A("sp", lambda e: e.dma_start(out=t, in_=ap_sb), reads=[key], dma=True, is_out=True, semkey="dbg_" + name)
        else:
            n = shape[1] * shape[2]
            if n <= 1024:
                stg = fsb[:, 1, 0:n].rearrange("p (a b) -> p a b", a=shape[1])
            else:
                stg = fsb[:].rearrange("p a b -> p (a b)")[:, 0:n].rearrange("p (a b) -> p a b", a=shape[1])
            A("dve", lambda e: e.tensor_copy(out=stg, in_=ap_sb), reads=[key], writes=["fsb0", "fsb1"])
            A("sp", lambda e: e.dma_start(out=t, in_=stg), reads=["fsb0", "fsb1"], dma=True, is_out=True, semkey="dbg_" + name)

    def mask_UT(c):
        return cm[c][:, 0:128]

    def mask_T4(c):
        return cm[c][:, 128:640]

    def mask_restart(c):
        return cm[c][:, 768:1280]

    def rowmask(c, i):
        return cm[c][:, 1280 + i:1281 + i]

    def load(dst, src, key, eng="sp"):
        if eng == "pool":
            A(eng, lambda e: e.dma_start(out=dst, in_=src), writes=["poolq", key], dma=True, semkey="poolq")
        else:
            A(eng, lambda e: e.dma_start(out=dst, in_=src), writes=[key], dma=True)

    load(gpre_sb[:], gpre[:, :], "gpre")
    load(gffn_sb[:], gffn[:, :], "gffn")
    load(gpost_sb[:], gpost.broadcast_to([128, D]), "gpost")
    load(gfpost_sb[:], gfpost.broadcast_to([128, D]), "gfpost")
    load(gn_sb[:], gnorm.broadcast_to([128, 256]), "gn")
    load(wup_sb[:], wup[:, :], "wup")
    load(mkP_sb, mkP[:, :], "tmpf")
    load(cbias_sb[:], cbias[:, :], "cbias")
    load(valid_sb[:], valid[:, :], "valid")
    load(cm[64][:], cm64[:, :], "cm64")
    load(cm[32][:], cm32[:, :], "cm32")
    load(biasP_sb[:], biasP.rearrange("p (h n) -> p h n", h=8), "biasP", eng="pool")
    load(biasS_sb[:], biasS.rearrange("p (h n) -> p h n", h=8), "biasS", eng="pool")
    A("dve", lambda e: e.memset(identf, 1.0), writes=["tmpf"])
    A("pool", lambda e: e.affine_select(out=identf, in_=identf, pattern=[[-1, 128]], compare_op=ALU.is_equal,
                                        fill=0.0, base=0, channel_multiplier=1), reads=["tmpf"], writes=["tmpf"])
    A("dve", lambda e: e.tensor_copy(out=ident[:], in_=identf), reads=["tmpf"], writes=["ident"])
    A("dve", lambda e: e.memset(ones8[:], 1.0), writes=["ones8"])
    for h in range(8):
        A("dve", lambda e, h=h: e.tensor_tensor(out=biasP_sb[:, h, :], in0=biasP_sb[:, h, :], in1=mkP_sb, op=ALU.add),
          reads=["biasP", "tmpf"], writes=["biasP"])
    A("dve", lambda e: e.memset(vA[:, :, 64:65], 1.0), writes=["vA%d" % s for s in range(8)])
    A("dve", lambda e: e.memset(vc[:, :, 64:65], 1.0), writes=["vc"])
    A("dve", lambda e: e.memset(dlrT[:], 0.0), writes=["dlrT"])
    for i in range(4):
        A("dve", lambda e, i=i: e.memset(qz[i][:], 0.0), writes=["qz%d" % i])
    A("dve", lambda e: e.memset(S[:], 0.0), writes=["S", "S0", "S1", "S2", "S3"])
    A("dve", lambda e: e.memset(Sbf[0][:], 0.0), writes=["Sbf0"])

    wctr = [0]
    A("dve", lambda e: e.memset(dlrTb[:], 0.0), writes=["dlrTb"])
    A("pool", lambda e: e.dma_start(out=dlrTb[16:32, :], in_=ones_row[:, :]), writes=["poolq", "dlrTb"], dma=True, semkey="poolq")
    cvctr = [0]
    wblocks = {}

    def register_block(wname, k0, nkc, c0, ncols):
        sig = (wname, k0, nkc, c0, ncols)
        if sig in wblocks:
            return wblocks[sig]
        scr = nc.dram_tensor("scr_%s_%d_%d" % (wname, k0, c0), [128, nkc * ncols], BF16).ap()
        src = wsrc[wname][0].rearrange("(k p) n -> p k n", p=128)[:, k0:k0 + nkc, c0:c0 + ncols]
        q = "cvq%d" % (cvctr[0] % 4)
        cvctr[0] += 1
        op = A("pool", lambda e: e.dma_start(out=scr.rearrange("p (k n) -> p k n", k=nkc), in_=src), writes=[q], dma=True, semkey=q)
        wblocks[sig] = (scr, op)
        return wblocks[sig]

    for c0_ in (QA, KA, VA, QB, KB, VB, VB + 512, RB, RB + 512):
        register_block("w_in", 0, 8, c0_, 512)
    register_block("w_in", 0, 8, DLR, 16)
    for c0_ in (GA, GA + 512, GB, GB + 512):
        register_block("w_in", 0, 8, c0_, 512)
    for half_ in range(2):
        register_block("wpa", 0, 4, half_ * 512, 512)
    for half_ in range(2):
        register_block("wpb", 0, 8, half_ * 512, 512)
    for cb_ in range(2):
        register_block("wout", 0, 8, cb_ * 512, 512)
    for fb4_ in range(0, 22, 4):
        nb_ = min(4, 22 - fb4_)
        register_block("wg", 0, 8, fb4_ * 128, nb_ * 128)
        register_block("wu", 0, 8, fb4_ * 128, nb_ * 128)
    for cb_ in range(2):
        for (k0_, nk_) in ((0, 8), (8, 8), (16, 6)):
            register_block("wd", k0_, nk_, cb_ * 512, 512)

    wctr = [0]
    wblk.append(fsb[:].rearrange("p a b -> p (a b)").bitcast(BF16).rearrange("p (k n) -> p k n", k=8))
    slotkeys = {0: ["w0"], 1: ["w1"], 2: ["w2"], 3: ["w3", "fsb0", "fsb1"]}

    def load_w(wname, k0, nkc, c0, ncols, slot=None):
        if slot is None:
            slot = wctr[0] % 3
            wctr[0] += 1
        scr, cop = register_block(wname, k0, nkc, c0, ncols)
        op = A("sp", lambda e: e.dma_start(out=wblk[slot][:, 0:nkc, 0:ncols], in_=scr.rearrange("p (k n) -> p k n", k=nkc)),
               writes=slotkeys[slot], dma=True, semkey="w%d" % slot)
        if cop.idx not in set(d.idx for d in op.deps):
            op.deps.append(cop)
        return slot

    pctr = [0]

    def acc_bank():
        b = pctr[0] % 2
        pctr[0] += 1
        return b

    def bstyle(slot, nkc, nblk, src, srckey, Tg, evac, mrows=128):
        for ob in range(nblk):
            b = acc_bank()
            for kc in range(nkc):
                A("pe", lambda e, kc=kc, ob=ob, b=b: e.matmul(ps[b][0:mrows, 0:Tg], lhsT=wblk[slot][:, kc, ob * 128:ob * 128 + mrows],
                                                               rhs=src[:, kc, 0:Tg], start=(kc == 0), stop=(kc == nkc - 1)),
                  reads=["w%d" % slot, srckey], writes=["ps%d" % b])
            evac(ps[b], "ps%d" % b, ob)

    def astyle_tile(slot, nkc, ncols, src, srckey, tok0, mtok, evac, kc_off=0):
        b = acc_bank()
        for kc in range(nkc):
            A("pe", lambda e, kc=kc, b=b: e.matmul(ps[b][0:mtok, 0:ncols], lhsT=src[:, kc_off + kc, tok0:tok0 + mtok],
                                                   rhs=wblk[slot][:, kc, 0:ncols], start=(kc == 0), stop=(kc == nkc - 1)),
              reads=["w%d" % slot, srckey], writes=["ps%d" % b])
        evac(ps[b], "ps%d" % b)

    def norm_to_hT(src_ap, srckey, gcol_sb, gkey, tok0):
        A("act", lambda e: e.activation(out=junk[:], in_=src_ap, func=AF.Square, accum_out=ss[:, 0:1]),
          reads=[srckey], writes=["junk", "ss"])
        A("act", lambda e: e.activation(out=rstd[:, 0:1], in_=ss[:, 0:1], func=AF.Ln, scale=1.0 / D, bias=EPS),
          reads=["ss"], writes=["rstd"])
        A("act", lambda e: e.activation(out=rstd[:, 0:1], in_=rstd[:, 0:1], func=AF.Exp, scale=-0.5),
          reads=["rstd"], writes=["rstd"])
        A("dve", lambda e: e.tensor_scalar(out=hb[:], in0=src_ap, scalar1=rstd[:, 0:1], scalar2=None, op0=ALU.mult),
          reads=[srckey, "rstd"], writes=["hb"])
        for kc in range(8):
            A("pe", lambda e, kc=kc: e.transpose(out=pst[:, kc, :], in_=hb[:, kc * 128:(kc + 1) * 128], identity=ident[:]),
              reads=["hb", "ident"], writes=["pst"])
        for kc in range(8):
            A("act", lambda e, kc=kc: e.activation(out=hT[:, kc, tok0:tok0 + 128], in_=pst[:, kc, :], func=AF.Copy,
                                                   scale=gcol_sb[:, kc:kc + 1]),
              reads=["pst", gkey], writes=["hT"])

    def transpose_to(src_tok, srckey, nblk, dst, dstkey, tok0, rows=128):
        for blk in range(nblk):
            A("pe", lambda e, blk=blk: e.transpose(out=pst[:, blk, 0:rows], in_=src_tok[0:rows, blk * 128:(blk + 1) * 128],
                                                   identity=ident[0:rows, 0:rows]),
              reads=[srckey, "ident"], writes=["pst"])
        A("act", lambda e: e.copy(out=dst[:, 0:nblk, tok0:tok0 + rows], in_=pst[:, 0:nblk, 0:rows]),
          reads=["pst"], writes=[dstkey])

    def gla_tile(t, c, state_only, sample=False, bk=None):
        bk = bk or {"la": 2, "laT": 3, "c": 4, "at": 5, "st": 6}
        B_la, B_laT, B_c, B_at, B_st = bk["la"], bk["laT"], bk["c"], bk["at"], bk["st"]
        nch = 128 // c
        tok0 = t * 128
        A("pe", lambda e: e.matmul(ps[B_la][:, :], lhsT=dlrT[0:32, tok0:tok0 + 128], rhs=wup_sb[0:32, :], start=True, stop=True),
          reads=["dlrT", "wup"], writes=["ps%d" % B_la])
        for h in range(4):
            A("pe", lambda e, h=h: e.matmul(ps[B_laT][:, h * 128:(h + 1) * 128], lhsT=wup_sb[0:32, h * 128:(h + 1) * 128],
                                            rhs=dlrT[0:32, tok0:tok0 + 128], start=True, stop=True),
              reads=["dlrT", "wup"], writes=["ps%d" % B_laT])
        A("act", lambda e: e.activation(out=la_neg[:], in_=ps[B_la][:, :], func=AF.Exp, scale=-1.0), reads=["ps%d" % B_la], writes=["la_neg"])
        A("act", lambda e: e.activation(out=laT_neg[:], in_=ps[B_laT][:, :], func=AF.Exp, scale=-1.0), reads=["ps%d" % B_laT], writes=["laT_neg"])
        A("act", lambda e: e.activation(out=la_neg[:], in_=la_neg[:], func=AF.Ln, bias=1.0), reads=["la_neg"], writes=["la_neg"])
        A("act", lambda e: e.activation(out=laT_neg[:], in_=laT_neg[:], func=AF.Ln, bias=1.0), reads=["laT_neg"], writes=["laT_neg"])
        yield
        A("pe", lambda e: e.matmul(ps[B_c][:, :], lhsT=mask_UT(c), rhs=la_neg[:], start=True, stop=True),
          reads=["la_neg", "cm%d" % c], writes=["ps%d" % B_c])
        A("dve", lambda e: e.tensor_tensor_scan(out=bneg[:], data0=mask_restart(c), data1=laT_neg[:], initial=0.0,
                                                op0=ALU.mult, op1=ALU.add),
          reads=["laT_neg", "cm%d" % c], writes=["bneg"])
        A("act", lambda e: e.activation(out=ec[:], in_=ps[B_c][:, :], func=AF.Exp, scale=-1.0 / 16), reads=["ps%d" % B_c], writes=["ec"])
        A("act", lambda e: e.activation(out=eb[:], in_=bneg[:], func=AF.Exp, scale=-1.0 / 16), reads=["bneg"], writes=["eb"])
        for i in range(nch):
            A("dve", lambda e, i=i: e.scalar_tensor_tensor(out=khat[i][:], in0=kbt[:, t, :], scalar=rowmask(c, i), in1=ec[:],
                                                           op0=ALU.mult, op1=ALU.mult),
              reads=["kbt", "ec", "cm%d" % c], writes=["khat%d" % i])
        yield
        if not state_only:
            A("act", lambda e: e.activation(out=enb[:], in_=bneg[:], func=AF.Exp, scale=1.0 / 16), reads=["bneg"], writes=["enb"])
            A("dve", lambda e: e.tensor_tensor(out=qtl[:], in0=qTb[:, :, tok0:tok0 + 128],
                                               in1=eb[:].rearrange("p (h n) -> p h n", h=4), op=ALU.mult),
              reads=["qTb", "eb"], writes=["qtl"])
            A("dve", lambda e: e.tensor_tensor(out=ktl[:], in0=kTb[:, :, tok0:tok0 + 128],
                                               in1=enb[:].rearrange("p (h n) -> p h n", h=4), op=ALU.mult),
              reads=["kTb", "enb"], writes=["ktl"])
            for i in range(nch):
                A("dve", lambda e, i=i: e.tensor_copy(out=qz[i][:, :, i * c:(i + 1) * c], in_=qtl[:, :, i * c:(i + 1) * c]),
                  reads=["qtl"], writes=["qz%d" % i])
            yield
            for h in range(4):
                A("pe", lambda e, h=h: e.matmul(ps[B_at][:, h * 128:(h + 1) * 128], lhsT=ktl[:, h, :], rhs=qtl[:, h, :], start=True, stop=True),
                  reads=["ktl", "qtl"], writes=["ps%d" % B_at])
            A("dve", lambda e: e.tensor_tensor(out=attn_sb[:].rearrange("p h n -> p (h n)"), in0=ps[B_at][:, :], in1=mask_T4(c), op=ALU.mult),
              reads=["ps%d" % B_at, "cm%d" % c], writes=["attn_sb"])
            yield

        skeys = []
        par = [0]

        def state_update(i):
            cur = par[0]
            skeys.append((Sbf[cur], "Sbf%d" % cur))
            for hp in range(2):
                for hh in range(2):
                    h = hp * 2 + hh
                    A("pe", lambda e, i=i, h=h, hh=hh: e.matmul(ps[B_st][:, hh * 256:(hh + 1) * 256], lhsT=khat[i][:, h * 128:(h + 1) * 128],
                                                                rhs=vb[:, t, h * 256:(h + 1) * 256], start=True, stop=True),
                      reads=["khat%d" % i, "vb"], writes=["ps%d" % B_st])
                for hh in range(2):
                    h = hp * 2 + hh
                    col = h * 128 + (i + 1) * c - 1
                    A("dve", lambda e, h=h, hh=hh, col=col: e.scalar_tensor_tensor(out=S[:, h, :], in0=S[:, h, :], scalar=eb[:, col:col + 1],
                                                                                    in1=ps[B_st][:, hh * 256:(hh + 1) * 256], op0=ALU.mult, op1=ALU.add),
                      reads=["S", "eb", "ps%d" % B_st], writes=["S"])
            if not state_only:
                nxt = 1 - cur
                A("act", lambda e, nxt=nxt: e.copy(out=Sbf[nxt][:], in_=S[:]), reads=["S"], writes=["Sbf%d" % nxt])
                par[0] = nxt

        def sample_states():
            for b in range(4):
                sl = fsb[:, b % 2, :].rearrange("p (h v) -> p h v", h=4)
                slk = "fsb%d" % (b % 2)
                A("sp", lambda e, b=b, sl=sl: e.dma_start(out=sl, in_=sg[b].rearrange("h d v -> d h v")), writes=[slk], dma=True)
                for hp in range(2):
                    for hh in range(2):
                        h = hp * 2 + hh
                        A("pe", lambda e, b=b, h=h, hh=hh: e.matmul(ps[B_st][:, hh * 256:(hh + 1) * 256], lhsT=khat[b][:, h * 128:(h + 1) * 128],
                                                                    rhs=vb[:, t, h * 256:(h + 1) * 256], start=True, stop=True),
                          reads=["khat%d" % b, "vb"], writes=["ps%d" % B_st])
                    for hh in range(2):
                        h = hp * 2 + hh
                        col = h * 128 + (b + 1) * c - 1
                        A("dve", lambda e, h=h, hh=hh, col=col, sl=sl: e.scalar_tensor_tensor(out=sl[:, h, :], in0=sl[:, h, :], scalar=eb[:, col:col + 1],
                                                                                               in1=ps[B_st][:, hh * 256:(hh + 1) * 256], op0=ALU.mult, op1=ALU.add),
                          reads=[slk, "eb", "ps%d" % B_st], writes=[slk])
                A("act", lambda e, b=b: e.dma_start(out=sso[b], in_=fsb[:, b % 2, :]), reads=[slk], dma=True,
                  is_out=True, semkey=slk + "o")

        if state_only:
            for i in range(nch):
                state_update(i)
            return
        if sample:
            sample_states()
        else:
            assert nch == 2
            state_update(0)
            skeys.append((Sbf[par[0]], "Sbf%d" % par[0]))
        yield

        obank = lambda h: B_c if h < 2 else B_st
        for h in range(4):
            hh = h % 2
            ob_ = obank(h)
            A("pe", lambda e, h=h, hh=hh, ob_=ob_: e.matmul(ps[ob_][:, hh * 256:(hh + 1) * 256], lhsT=attn_sb[:, h, :], rhs=vb[:, t, h * 256:(h + 1) * 256],
                                                             start=True, stop=False),
              reads=["attn_sb", "vb"], writes=["ps%d" % ob_])
            for i in range(nch):
                if sample:
                    rhs_ap = Sbs[:, i * 4 + h, :]
                    rk = "aT"
                else:
                    rhs_ap = skeys[i][0][:, h, :]
                    rk = skeys[i][1]
                A("pe", lambda e, h=h, hh=hh, i=i, rhs_ap=rhs_ap, ob_=ob_: e.matmul(ps[ob_][:, hh * 256:(hh + 1) * 256], lhsT=qz[i][:, h, :], rhs=rhs_ap,
                                                                                   start=False, stop=(i == nch - 1)),
                  reads=["qz%d" % i, rk], writes=["ps%d" % ob_])
        yield
        for h in range(4):
            hh = h % 2
            ob_ = obank(h)
            A("act", lambda e, h=h, hh=hh, ob_=ob_: e.activation(out=junk[:, 0:256], in_=ps[ob_][:, hh * 256:(hh + 1) * 256], func=AF.Square,
                                                                 accum_out=ss[:, h:h + 1]),
              reads=["ps%d" % ob_], writes=["junk", "ss"])
        A("act", lambda e: e.activation(out=rstd[:, 0:4], in_=ss[:, 0:4], func=AF.Ln, scale=1.0 / 256, bias=EPS), reads=["ss"], writes=["rstd"])
        A("act", lambda e: e.activation(out=rstd[:, 0:4], in_=rstd[:, 0:4], func=AF.Exp, scale=-0.5), reads=["rstd"], writes=["rstd"])
        for h in range(4):
            hh = h % 2
            ob_ = obank(h)
            A("dve", lambda e, h=h, hh=hh, ob_=ob_: e.scalar_tensor_tensor(out=ob_tok[:, h * 256:(h + 1) * 256], in0=ps[ob_][:, hh * 256:(hh + 1) * 256],
                                                                            scalar=rstd[:, h:h + 1], in1=rbs[:, t, h * 256:(h + 1) * 256],
                                                                            op0=ALU.mult, op1=ALU.mult),
              reads=["ps%d" % ob_, "rstd", "rbs"], writes=["ob_tok"])
        yield
        if not sample:
            skeys.pop()
            state_update(1)
            skeys.pop()
        yield
        transpose_to(ob_tok, "ob_tok", 8, obT, "obT", tok0)

    def attn_prompt_tile(t, a):
        tok0 = t * 128
        jA = {0: 0, 1: 1, 4: 2}
        jB = {2: 0, 3: 1}

        def scores(h):
            par = (h // 2) % 2
            blk, pr = h // 2, (h % 2) * 64
            bA, bB = (2, 3) if (h // 2) % 2 == 0 else (0, 1)
            for d in range(5):
                s = (a - d) % 8
                if d in jA:
                    bank, j, bk = ps[bA], jA[d], "ps%d" % bA
                else:
                    bank, j, bk = ps[bB], jB[d], "ps%d" % bB
                A("pe", lambda e, bank=bank, j=j, s=s: e.matmul(bank[:, j * 128:(j + 1) * 128], lhsT=kTa[pr:pr + 64, blk, s * 128:(s + 1) * 128],
                                                               rhs=qTa[pr:pr + 64, blk, tok0:tok0 + 128], start=True, stop=True),
                  reads=["kT%d" % s, "qTa"], writes=[bk])
            A("dve", lambda e: e.scalar_tensor_tensor(out=stmp2[par][:].rearrange("p a b -> p (a b)"), in0=ps[bA][:, 0:384], scalar=0.125,
                                                      in1=biasP_sb[:, h, :], op0=ALU.mult, op1=ALU.add),
              reads=["ps%d" % bA, "biasP"], writes=["stmp%d" % par])
            A("act", lambda e: e.activation(out=expT2[par][:, 0:3, :].rearrange("p a b -> p (a b)"), in_=stmp2[par][:].rearrange("p a b -> p (a b)"),
                                            func=AF.Exp), reads=["stmp%d" % par], writes=["expT%d" % par])
            A("act", lambda e: e.activation(out=expT2[par][:, 3:5, :].rearrange("p a b -> p (a b)"), in_=ps[bB][:, 0:256], func=AF.Exp,
                                            scale=0.125, bias=cbias_sb[:, h:h + 1]), reads=["ps%d" % bB, "cbias"], writes=["expT%d" % par])

        def pv(h, slot, grp):
            par = (h // 2) % 2
            pb = 5
            order = [(0, 0), (1, 1), (4, 2), (2, 3), (3, 4)]
            for n, (d, j) in enumerate(order):
                s = (a - d) % 8
                A("pe", lambda e, j=j, s=s, n=n: e.matmul(ps[pb][:, slot * 65:(slot + 1) * 65], lhsT=expT2[par][:, j, :],
                                                          rhs=vA[:, s * 8 + h, :], start=(n == 0), stop=(n == 4)),
                  reads=["expT%d" % par, "vA%d" % s], writes=["ps%d" % pb])
            if slot == 3:
                gi = 0 if grp[0] == 0 else 1
                pv3 = ps[pb][:, 0:260].rearrange("p (h n) -> p h n", h=4)
                A("dve", lambda e: e.reciprocal(out=rden[:, gi * 4:gi * 4 + 4], in_=pv3[:, :, 64]), reads=["ps%d" % pb], writes=["rden%d" % gi])
                for k, h2 in enumerate(grp):
                    A("dve", lambda e, h2=h2, k=k: e.tensor_scalar(out=oa_tok[:, h2 * 64:(h2 + 1) * 64], in0=ps[pb][:, k * 65:k * 65 + 64],
                                                                   scalar1=rden[:, gi * 4 + k:gi * 4 + k + 1], scalar2=None, op0=ALU.mult),
                      reads=["ps%d" % pb, "rden%d" % gi], writes=["oa_tok"])

        horder = [0, 2, 4, 6, 1, 3, 5, 7]
        scores(horder[0])
        yield
        for i_, h in enumerate(horder):
            if i_ + 1 < 8:
                scores(horder[i_ + 1])
                yield
            pv(h, i_ % 4, horder[(i_ // 4) * 4:(i_ // 4) * 4 + 4])
            yield
        transpose_to(oa_tok, "oa_tok", 4, oaT, "oaT", tok0)

    def attn_sample():
        vc2 = xg[:, 2:4, :].rearrange("p a b -> p (a b)").bitcast(BF16)[:, 0:2080].rearrange("p (n d) -> p n d", d=65)
        vcb = [vc, vc2]
        vck = [["vc"], ["xg2", "xg3"]]
        A("dve", lambda e: e.memset(vc2[:, :, 64:65], 1.0), writes=["xg2", "xg3"])

        def load_k(b):
            A("pool", lambda e: e.dma_start(out=kc_tok, in_=ck[b].rearrange("(t p) f -> p t f", p=128)), writes=["poolq", "tmpf"], dma=True, semkey="poolq")

        def load_v(b):
            for kt in range(4):
                A("pool", lambda e, kt=kt: e.dma_start(out=vcb[b % 2][:, kt * 8:(kt + 1) * 8, 0:64],
                                                       in_=cv[b][kt * 128:(kt + 1) * 128, :].rearrange("p (h d) -> p h d", h=8)),
                  writes=["poolq"] + vck[b % 2], dma=True, semkey="poolq")

        load_k(0)
        load_v(0)
        for b in range(4):
            vcur = vcb[b % 2]
            vkeys = vck[b % 2]
            for blk in range(4):
                for kt in range(4):
                    A("pe", lambda e, blk=blk, kt=kt: e.transpose(out=pst[:, kt, :], in_=kc_tok[:, kt, blk * 128:(blk + 1) * 128], identity=ident[:]),
                      reads=["tmpf", "ident"], writes=["pst"])
                A("act", lambda e, blk=blk: e.copy(out=kTc[:, blk, :], in_=pst[:, 0:4, :].rearrange("p a b -> p (a b)")),
                  reads=["pst"], writes=["obT"])
            if b + 1 < 4:
                load_k(b + 1)
                load_v(b + 1)
            def s_scores(h, b=b):
                blk, pr = h // 2, (h % 2) * 64
                par = (h // 2) % 2
                bB, bA = (3, 2) if par == 0 else (1, 0)
                for kt in range(3):
                    A("pe", lambda e, kt=kt: e.matmul(ps[bB][:, kt * 32:(kt + 1) * 32], lhsT=kTc[pr:pr + 64, blk, kt * 128:(kt + 1) * 128],
                                                      rhs=qTa[pr:pr + 64, blk, b * 32:(b + 1) * 32], start=True, stop=True),
                      reads=["obT", "qTa"], writes=["ps%d" % bB])
                A("pe", lambda e: e.matmul(ps[bA][:, 0:32], lhsT=kTc[pr:pr + 64, blk, 384:512],
                                           rhs=qTa[pr:pr + 64, blk, b * 32:(b + 1) * 32], start=True, stop=True),
                  reads=["obT", "qTa"], writes=["ps%d" % bA])
                A("pe", lambda e: e.matmul(ps[bA][0:32, 32:64], lhsT=kTa[pr:pr + 64, blk, b * 32:(b + 1) * 32],
                                           rhs=qTa[pr:pr + 64, blk, b * 32:(b + 1) * 32], start=True, stop=True),
                  reads=["kT0", "qTa"], writes=["ps%d" % bA])
                ex, st_ = expTs2[par], stmps2[par]
                A("act", lambda e: e.activation(out=ex[:, 0:3, :].rearrange("p a b -> p (a b)"), in_=ps[bB][:, 0:96], func=AF.Exp,
                                                scale=0.125, bias=cbias_sb[:, h:h + 1]),
                  reads=["ps%d" % bB, "cbias"], writes=["expTs%d" % par])
                A("dve", lambda e: e.scalar_tensor_tensor(out=st_[:, 0, :], in0=ps[bA][:, 0:32], scalar=0.125, in1=biasS_sb[:, h, 0:32],
                                                          op0=ALU.mult, op1=ALU.add),
                  reads=["ps%d" % bA, "biasS"], writes=["stmps%d" % par])
                A("dve", lambda e: e.scalar_tensor_tensor(out=st_[0:32, 1, :], in0=ps[bA][0:32, 32:64], scalar=0.125, in1=biasS_sb[0:32, h, 32:64],
                                                          op0=ALU.mult, op1=ALU.add),
                  reads=["ps%d" % bA, "biasS"], writes=["stmps%d" % par])
                A("act", lambda e: e.activation(out=ex[:, 3, :], in_=st_[:, 0, :], func=AF.Exp), reads=["stmps%d" % par], writes=["expTs%d" % par])
                A("act", lambda e: e.activation(out=ex[0:32, 4, :], in_=st_[0:32, 1, :], func=AF.Exp), reads=["stmps%d" % par], writes=["expTs%d" % par])

            def s_pv(h, slot, grp, b=b, vcur=vcur, vkeys=vkeys):
                par = (h // 2) % 2
                ex = expTs2[par]
                for kt in range(4):
                    A("pe", lambda e, kt=kt: e.matmul(ps[5][0:32, slot * 65:slot * 65 + 65], lhsT=ex[:, kt, :], rhs=vcur[:, kt * 8 + h, :],
                                                      start=(kt == 0), stop=False),
                      reads=["expTs%d" % par] + vkeys, writes=["ps5"])
                A("pe", lambda e: e.matmul(ps[5][0:32, slot * 65:slot * 65 + 65], lhsT=ex[0:32, 4, :], rhs=vnew[0:32, b * 8 + h, :],
                                           start=False, stop=True),
                  reads=["expTs%d" % par, "aT"], writes=["ps5"])
                if slot == 3:
                    pv3 = ps[5][0:32, 0:260].rearrange("p (h n) -> p h n", h=4)
                    A("dve", lambda e: e.reciprocal(out=rdens[:, 0:4], in_=pv3[:, :, 64]), reads=["ps5"], writes=["rdens"])
                    for k, h2 in enumerate(grp):
                        A("dve", lambda e, h2=h2, k=k: e.tensor_scalar(out=oas[:, b, h2 * 64:(h2 + 1) * 64], in0=ps[5][0:32, k * 65:k * 65 + 64],
                                                                       scalar1=rdens[:, k:k + 1], scalar2=None, op0=ALU.mult),
                          reads=["ps5", "rdens"], writes=["aT"])

            horder = [0, 2, 4, 6, 1, 3, 5, 7]
            s_scores(horder[0])
            for i_, h in enumerate(horder):
                if i_ + 1 < 8:
                    s_scores(horder[i_ + 1])
                s_pv(h, i_ % 4, horder[(i_ // 4) * 4:(i_ // 4) * 4 + 4])
        for b in range(4):
            for blk in range(4):
                A("pe", lambda e, b=b, blk=blk: e.transpose(out=pst[:, blk, 0:32], in_=oas[:, b, blk * 128:(blk + 1) * 128], identity=ident[0:32, 0:32]),
                  reads=["aT", "ident"], writes=["pst"])
            A("act", lambda e, b=b: e.copy(out=oaT[:, 0:4, b * 32:(b + 1) * 32], in_=pst[:, 0:4, 0:32]), reads=["pst"], writes=["oaT"])

    def prep_group(xsrc, row0, ntiles, xpar):
        for t in range(ntiles):
            xi = xpar * G + t
            A("sp", lambda e, t=t, xi=xi: e.dma_start(out=xg[:, xi, :], in_=xsrc[row0 + t * 128:row0 + (t + 1) * 128, :]), writes=["xg%d" % xi], dma=True)
            norm_to_hT(xg[:, xi, :], "xg%d" % xi, gpre_sb, "gpre", t * 128)

    def run_group(kind, xsrc, row0, ntiles, a0=None, out_ap=None, kv_out=None, xpar=0, prefetched=False, next_prep=None):
        Tg = ntiles * 128
        c = 32 if kind == "sample" else 64
        if not prefetched:
            prep_group(xsrc, row0, ntiles, xpar)

        def ev_copy(dst, dkey, scale=None, func=AF.Copy, rows=128):
            def f(bank, bkey, ob):
                if scale is None:
                    A("act", lambda e: e.activation(out=dst(ob), in_=bank[0:rows, 0:Tg], func=func), reads=[bkey], writes=[dkey(ob)])
                else:
                    A("act", lambda e: e.activation(out=dst(ob), in_=bank[0:rows, 0:Tg], func=func, scale=scale), reads=[bkey], writes=[dkey(ob)])
            return f

        full = kind in ("prompt", "sample")
        if full:
            s = load_w("w_in", 0, 8, QA, 512)
            bstyle(s, 8, 4, hT, "hT", Tg, ev_copy(lambda ob: qTa[:, ob, 0:Tg], lambda ob: "qTa"))
        if kind != "pre":
            s = load_w("w_in", 0, 8, KA, 512)
            if kind == "sample":
                bstyle(s, 8, 4, hT, "hT", Tg, ev_copy(lambda ob: kTa[:, ob, 0:128], lambda ob: "kT0"))
            else:
                s0 = a0 % 8
                def kdst(ob):
                    return kTa[:, ob, s0 * 128:s0 * 128 + Tg]
                def kev(bank, bkey, ob):
                    A("act", lambda e: e.copy(out=kdst(ob), in_=bank[:, 0:Tg]), reads=[bkey], writes=["kT%d" % ((a0 + i) % 8) for i in range(ntiles)])
                bstyle(s, 8, 4, hT, "hT", Tg, kev)
            if kv_out is not None:
                for t in range(ntiles):
                    def kout(bank, bkey, t=t):
                        st = kvst[t % 2]
                        sk = "kvst%d" % (t % 2)
                        A("act", lambda e: e.copy(out=st[:], in_=bank[:, 0:512]), reads=[bkey], writes=[sk])
                        A("act", lambda e: e.dma_start(out=kv_out[0][kv_out[2] + t * 128:kv_out[2] + (t + 1) * 128, :], in_=st[:]), reads=[sk], dma=True,
                          is_out=True, semkey=sk + "o")
                    astyle_tile(s, 8, 512, hT, "hT", t * 128, 128, kout)
            s = load_w("w_in", 0, 8, VA, 512)
            for t in range(ntiles):
                slot_v = 0 if kind == "sample" else (a0 + t) % 8
                def vev(bank, bkey, t=t, slot_v=slot_v):
                    if kind != "sample":
                        A("act", lambda e: e.copy(out=vA[:, slot_v * 8:(slot_v + 1) * 8, 0:64], in_=bank[:, 0:512].rearrange("p (h d) -> p h d", h=8)),
                          reads=[bkey], writes=["vA%d" % slot_v])
                        if kind == "halo":
                            A("act", lambda e: e.activation(out=vA[:, slot_v * 8:(slot_v + 1) * 8, 64], in_=ones8[:], func=AF.Copy, scale=valid_sb[:, 0:1]),
                              reads=["ones8", "valid"], writes=["vA%d" % slot_v])
                        else:
                            A("act", lambda e: e.copy(out=vA[:, slot_v * 8:(slot_v + 1) * 8, 64], in_=ones8[:]), reads=["ones8"], writes=["vA%d" % slot_v])
                    if kv_out is not None:
                        st = kvst[t % 2]
                        sk = "kvst%d" % (t % 2)
                        A("act", lambda e: e.copy(out=st[:], in_=bank[:, 0:512]), reads=[bkey], writes=[sk])
                        A("act", lambda e: e.dma_start(out=kv_out[1][kv_out[2] + t * 128:kv_out[2] + (t + 1) * 128, :], in_=st[:]), reads=[sk], dma=True,
                          is_out=True, semkey=sk + "o")
                astyle_tile(s, 8, 512, hT, "hT", t * 128, 128, vev)
            if kind == "sample":
                A("dve", lambda e: e.memset(vnew[:, :, 64:65], 1.0), writes=["aT"])
                for b in range(4):
                    def vnev(bank, bkey, b=b):
                        A("act", lambda e: e.copy(out=vnew[:, b * 8:(b + 1) * 8, 0:64], in_=bank[0:32, 0:512].rearrange("p (h d) -> p h d", h=8)),
                          reads=[bkey], writes=["aT"])
                    astyle_tile(s, 8, 512, hT, "hT", b * 32, 32, vnev)
        if kind == "halo":
            return
        if full:
            s = load_w("w_in", 0, 8, QB, 512)
            bstyle(s, 8, 4, hT, "hT", Tg, ev_copy(lambda ob: qTb[:, ob, 0:Tg], lambda ob: "qTb", scale=128.0 ** -0.5))
        s = load_w("w_in", 0, 8, KB, 512)
        if full:
            bstyle(s, 8, 4, hT, "hT", Tg, ev_copy(lambda ob: kTb[:, ob, 0:Tg], lambda ob: "kTb"))
        for t in range(ntiles):
            def kbev(bank, bkey, t=t):
                A("act", lambda e: e.copy(out=kbt[:, t, :], in_=bank[:, 0:512]), reads=[bkey], writes=["kbt"])
            astyle_tile(s, 8, 512, hT, "hT", t * 128, 128, kbev)
        for half in range(2):
            s = load_w("w_in", 0, 8, VB + half * 512, 512)
            for t in range(ntiles):
                def vbev(bank, bkey, t=t, half=half):
                    A("act", lambda e: e.copy(out=vb[:, t, half * 512:(half + 1) * 512], in_=bank[:, 0:512]), reads=[bkey], writes=["vb"])
                astyle_tile(s, 8, 512, hT, "hT", t * 128, 128, vbev)
        if full:
            for half in range(2):
                s = load_w("w_in", 0, 8, RB + half * 512, 512)
                for t in range(ntiles):
                    def rbev(bank, bkey, t=t, half=half):
                        A("act", lambda e: e.activation(out=rbs[:, t, half * 512:(half + 1) * 512], in_=bank[:, 0:512], func=AF.Silu),
                          reads=[bkey], writes=["rbs"])
                        for j in range(2):
                            c0 = half * 512 + j * 256
                            A("dve", lambda e, c0=c0: e.tensor_tensor(out=rbs[:, t, c0:c0 + 256], in0=rbs[:, t, c0:c0 + 256], in1=gn_sb[:], op=ALU.mult),
                              reads=["rbs", "gn"], writes=["rbs"])
                    astyle_tile(s, 8, 512, hT, "hT", t * 128, 128, rbev)
        s = load_w("w_in", 0, 8, DLR, 16)
        def dlev(bank, bkey, ob):
            A("act", lambda e: e.copy(out=dlrT[0:16, 0:Tg], in_=bank[0:16, 0:Tg]), reads=[bkey], writes=["dlrT"])
        bstyle(s, 8, 1, hT, "hT", Tg, dlev, mrows=16)
        if full:
            for half in range(2):
                s = load_w("w_in", 0, 8, GA + half * 512, 512)
                bstyle(s, 8, 4, hT, "hT", Tg, ev_copy(lambda ob, half=half: gaT[:, half * 4 + ob, 0:Tg], lambda ob: "gaT", func=AF.Sigmoid))
            for half in range(2):
                s = load_w("w_in", 0, 8, GB + half * 512, 512)
                bstyle(s, 8, 4, hT, "hT", Tg, ev_copy(lambda ob, half=half: gbT[:, half * 4 + ob, 0:Tg], lambda ob: "gbT", func=AF.Sigmoid))

        if kind == "pre":
            for t in range(ntiles):
                for _ in gla_tile(t, 64, True):
                    pass
            return
        if kind == "prompt":
            ibk = {"la": 4, "laT": 6, "c": 4, "at": 6, "st": 6}
            gens = []
            for t in range(ntiles):
                gens += [attn_prompt_tile(t, a0 + t), gla_tile(t, 64, False, bk=ibk)]
            att = [g_ for i_, g_ in enumerate(gens) if i_ % 2 == 0]
            gl = [g_ for i_, g_ in enumerate(gens) if i_ % 2 == 1]
            while att or gl:
                if att:
                    try:
                        next(att[0])
                    except StopIteration:
                        att.pop(0)
                if gl:
                    try:
                        next(gl[0])
                    except StopIteration:
                        gl.pop(0)
        else:
            attn_sample()
            dump("s_oaT", oaT[:, :, 0:128], [128, 4, 128], BF16, "oaT")
            dump("s_qTa", qTa[:, :, 0:128], [128, 4, 128], BF16, "qTa")
            for b in range(4):
                A("pool", lambda e, b=b: e.dma_start(out=Sbs[:, b * 4:(b + 1) * 4, :], in_=sg[b].rearrange("h d v -> d h v")),
                  writes=["poolq", "aT"], dma=True, semkey="poolq")
            for _ in gla_tile(0, 32, False, sample=True):
                pass
            dump("s_obT", obT[:, :, 0:128], [128, 8, 128], BF16, "obT")

        for half in range(2):
            s = load_w("wpa", 0, 4, half * 512, 512)
            def paev(bank, bkey, ob, half=half):
                A("dve", lambda e: e.tensor_tensor(out=gaT[:, half * 4 + ob, 0:Tg], in0=gaT[:, half * 4 + ob, 0:Tg], in1=bank[:, 0:Tg], op=ALU.mult),
                  reads=[bkey, "gaT"], writes=["gaT"])
            bstyle(s, 4, 4, oaT, "oaT", Tg, paev)
        for half in range(2):
            s = load_w("wpb", 0, 8, half * 512, 512)
            def pbev(bank, bkey, ob, half=half):
                A("dve", lambda e: e.tensor_tensor(out=gbT[:, half * 4 + ob, 0:Tg], in0=gbT[:, half * 4 + ob, 0:Tg], in1=bank[:, 0:Tg], op=ALU.mult),
                  reads=[bkey, "gbT"], writes=["gbT"])
                A("dve", lambda e: e.tensor_tensor(out=gaT[:, half * 4 + ob, 0:Tg], in0=gaT[:, half * 4 + ob, 0:Tg], in1=gbT[:, half * 4 + ob, 0:Tg], op=ALU.add),
                  reads=["gbT", "gaT"], writes=["gaT"])
            bstyle(s, 8, 4, obT, "obT", Tg, pbev)

        def a_into_fsb(wname, nk_total, src, srckey):
            kgs = []
            k = 0
            while k < nk_total:
                kgs.append((k, min(8, nk_total - k)))
                k += 8
            for cb in range(2):
                for gi, (k0, nk) in enumerate(kgs):
                    s = load_w(wname, k0, nk, cb * 512, 512)
                    for t in range(ntiles):
                        def fev(bank, bkey, t=t, cb=cb, gi=gi):
                            if gi == 0:
                                A("act", lambda e: e.copy(out=fsb[:, t, cb * 512:(cb + 1) * 512], in_=bank[:, 0:512]), reads=[bkey], writes=["fsb%d" % t])
                            else:
                                A("dve", lambda e: e.tensor_tensor(out=fsb[:, t, cb * 512:(cb + 1) * 512], in0=fsb[:, t, cb * 512:(cb + 1) * 512],
                                                                   in1=bank[:, 0:512], op=ALU.add), reads=[bkey, "fsb%d" % t], writes=["fsb%d" % t])
                        astyle_tile(s, nk, 512, src, srckey, t * 128, 128, fev, kc_off=k0)

        def norm_residual(t, g_sb, gkey, dst, dkey):
            A("act", lambda e: e.activation(out=junk[:], in_=fsb[:, t, :], func=AF.Square, accum_out=ss[:, 0:1]), reads=["fsb%d" % t], writes=["junk", "ss"])
            A("act", lambda e: e.activation(out=rstd[:, 0:1], in_=ss[:, 0:1], func=AF.Ln, scale=1.0 / D, bias=EPS), reads=["ss"], writes=["rstd"])
            A("act", lambda e: e.activation(out=rstd[:, 0:1], in_=rstd[:, 0:1], func=AF.Exp, scale=-0.5), reads=["rstd"], writes=["rstd"])
            A("dve", lambda e: e.scalar_tensor_tensor(out=tmpf[:], in0=fsb[:, t, :], scalar=rstd[:, 0:1], in1=g_sb[:], op0=ALU.mult, op1=ALU.mult),
              reads=["fsb%d" % t, "rstd", gkey], writes=["tmpf"])
            A("dve", lambda e: e.tensor_tensor(out=dst, in0=tmpf[:], in1=xg[:, xpar * G + t, :], op=ALU.add), reads=["tmpf", "xg%d" % (xpar * G + t)],
              writes=[dkey])

        if kind == "sample":
            dump("s_mixT", gaT[:, :, 0:128], [128, 8, 128], BF16, "gaT")
        a_into_fsb("wout", 8, gaT, "gaT")
        TT = list(range(ntiles))
        xi_ = lambda t: xpar * G + t
        for t in TT:
            A("act", lambda e, t=t: e.activation(out=junk[:], in_=fsb[:, t, :], func=AF.Square, accum_out=ssn[:, t:t + 1]),
              reads=["fsb%d" % t], writes=["junk", "ssn%d" % t])
        for t in TT:
            A("act", lambda e, t=t: e.activation(out=rstdn[:, t:t + 1], in_=ssn[:, t:t + 1], func=AF.Ln, scale=1.0 / D, bias=EPS),
              reads=["ssn%d" % t], writes=["rstdn%d" % t])
        for t in TT:
            A("act", lambda e, t=t: e.activation(out=rstdn[:, t:t + 1], in_=rstdn[:, t:t + 1], func=AF.Exp, scale=-0.5),
              reads=["rstdn%d" % t], writes=["rstdn%d" % t])
        for t in TT:
            A("dve", lambda e, t=t: e.scalar_tensor_tensor(out=fsb[:, t, :], in0=fsb[:, t, :], scalar=rstdn[:, t:t + 1], in1=gpost_sb[:], op0=ALU.mult, op1=ALU.mult),
              reads=["fsb%d" % t, "rstdn%d" % t, "gpost"], writes=["fsb%d" % t])
        for t in TT:
            A("dve", lambda e, t=t: e.tensor_tensor(out=xg[:, xi_(t), :], in0=fsb[:, t, :], in1=xg[:, xi_(t), :], op=ALU.add),
              reads=["fsb%d" % t, "xg%d" % xi_(t)], writes=["xg%d" % xi_(t)])
        for t in TT:
            A("act", lambda e, t=t: e.activation(out=junk[:], in_=xg[:, xi_(t), :], func=AF.Square, accum_out=ssn[:, 2 + t:3 + t]),
              reads=["xg%d" % xi_(t)], writes=["junk", "ssn%d" % (2 + t)])
        for t in TT:
            A("act", lambda e, t=t: e.activation(out=rstdn[:, 2 + t:3 + t], in_=ssn[:, 2 + t:3 + t], func=AF.Ln, scale=1.0 / D, bias=EPS),
              reads=["ssn%d" % (2 + t)], writes=["rstdn%d" % (2 + t)])
        for t in TT:
            A("act", lambda e, t=t: e.activation(out=rstdn[:, 2 + t:3 + t], in_=rstdn[:, 2 + t:3 + t], func=AF.Exp, scale=-0.5),
              reads=["rstdn%d" % (2 + t)], writes=["rstdn%d" % (2 + t)])
        for t in TT:
            hbt = hb if t == 0 else hb2
            A("dve", lambda e, t=t, hbt=hbt: e.tensor_scalar(out=hbt[:], in0=xg[:, xi_(t), :], scalar1=rstdn[:, 2 + t:3 + t], scalar2=None, op0=ALU.mult),
              reads=["xg%d" % xi_(t), "rstdn%d" % (2 + t)], writes=["hb" if t == 0 else "hb2"])
        for t in TT:
            hbt = hb if t == 0 else hb2
            hk = "hb" if t == 0 else "hb2"
            for kc in range(8):
                A("pe", lambda e, kc=kc, hbt=hbt: e.transpose(out=pst[:, kc, :], in_=hbt[:, kc * 128:(kc + 1) * 128], identity=ident[:]),
                  reads=[hk, "ident"], writes=["pst"])
            for kc in range(8):
                A("act", lambda e, kc=kc, t=t: e.activation(out=hT[:, kc, t * 128:(t + 1) * 128], in_=pst[:, kc, :], func=AF.Copy,
                                                            scale=gffn_sb[:, kc:kc + 1]), reads=["pst", "gffn"], writes=["hT"])

        for it_, fb4 in enumerate(range(0, 22, 4)):
            nb = min(4, 22 - fb4)
            sg_ = load_w("wg", 0, 8, fb4 * 128, nb * 128, slot=(2 * it_) % 4)
            su_ = load_w("wu", 0, 8, fb4 * 128, nb * 128, slot=(2 * it_ + 1) % 4)
            for ob in range(nb):
                bg = 0 if ob % 2 == 0 else 2
                bu = 1 if ob % 2 == 0 else 3
                for kc in range(8):
                    A("pe", lambda e, kc=kc, ob=ob, sg_=sg_, bg=bg: e.matmul(ps[bg][:, 0:Tg], lhsT=wblk[sg_][:, kc, ob * 128:(ob + 1) * 128], rhs=hT[:, kc, 0:Tg],
                                                            start=(kc == 0), stop=(kc == 7)), reads=slotkeys[sg_] + ["hT"], writes=["ps%d" % bg])
                for kc in range(8):
                    A("pe", lambda e, kc=kc, ob=ob, su_=su_, bu=bu: e.matmul(ps[bu][:, 0:Tg], lhsT=wblk[su_][:, kc, ob * 128:(ob + 1) * 128], rhs=hT[:, kc, 0:Tg],
                                                            start=(kc == 0), stop=(kc == 7)), reads=slotkeys[su_] + ["hT"], writes=["ps%d" % bu])
                A("act", lambda e, bg=bg: e.activation(out=junk[:, 0:Tg], in_=ps[bg][:, 0:Tg], func=AF.Silu), reads=["ps%d" % bg], writes=["junk"])
                A("dve", lambda e, ob=ob, fb4=fb4, bu=bu: e.tensor_tensor(out=aT[:, fb4 + ob, 0:Tg], in0=junk[:, 0:Tg], in1=ps[bu][:, 0:Tg], op=ALU.mult),
                  reads=["junk", "ps%d" % bu], writes=["aT"])
        if next_prep is not None:
            next_prep()
        a_into_fsb("wd", 22, aT, "aT")
        for t in range(ntiles):
            yst = ystage[t % 2]
            yk = "yst%d" % (t % 2)
            norm_residual(t, gfpost_sb, "gfpost", yst[:, 0:D], yk)
            A("act", lambda e, t=t, yst=yst: e.dma_start(out=out_ap[row0 + t * 128:row0 + (t + 1) * 128, :], in_=yst[:, 0:D]), reads=[yk], dma=True,
              is_out=True, semkey=yk + "o")

    A("sp", lambda e: e.dma_start(out=dlrT[16:32, :], in_=ones_row[:, :]), writes=["dlrT"], dma=True, semkey="dlr1")

    A("dve", lambda e: e.tensor_copy(out=wup_bf[:], in_=wup_sb[:]), reads=["wup"], writes=["wup_bf"])
    A("dve", lambda e: e.tensor_copy(out=utf_bf[:], in_=cm[64][:, 640:768]), reads=["cm64"], writes=["utf_bf"])
    A("dve", lambda e: e.memset(onesb[:], 1.0), writes=["onesb"])
    skb, sv0, sv1 = 0, 1, 2
    wctr[0] = 3

    xgf = xg[:].rearrange("p a b -> p (a b)")
    w_in3 = w_in.rearrange("(k p) n -> p k n", p=128)
    for slot_, c0_ in ((skb, KB), (sv0, VB), (sv1, VB + 512)):
        for half_ in range(2):
            A("sp", lambda e, c0_=c0_, half_=half_: e.dma_start(out=xgf[:, 0:2048].rearrange("p (k n) -> p k n", k=4),
                                                                  in_=w_in3[:, half_ * 4:(half_ + 1) * 4, c0_:c0_ + 512]),
              writes=["xg0", "xg1"], dma=True, semkey="xg0")
            for j in range(4):
                kc = half_ * 4 + j
                if j % 2 == 0:
                    A("act", lambda e, slot_=slot_, kc=kc, j=j: e.activation(out=wblk[slot_][:, kc, :], in_=xgf[:, j * 512:(j + 1) * 512], func=AF.Copy,
                                                                            scale=gpre_sb[:, kc:kc + 1]),
                      reads=["xg0", "xg1", "gpre"], writes=["w%d" % slot_])
                else:
                    A("dve", lambda e, slot_=slot_, kc=kc, j=j: e.tensor_scalar(out=wblk[slot_][:, kc, :], in0=xgf[:, j * 512:(j + 1) * 512],
                                                                                scalar1=gpre_sb[:, kc:kc + 1], scalar2=None, op0=ALU.mult),
                      reads=["xg0", "xg1", "gpre"], writes=["w%d" % slot_])
    A("sp", lambda e: e.dma_start(out=tmpf[:, 0:128].rearrange("p (k n) -> p k n", k=8), in_=w_in3[:, :, DLR:DLR + 16]), writes=["tmpf"], dma=True,
      semkey="tmpfw")
    for kc in range(8):
        A("dve", lambda e, kc=kc: e.tensor_scalar(out=wdlr[:, kc, :], in0=tmpf[:, kc * 16:(kc + 1) * 16], scalar1=gpre_sb[:, kc:kc + 1], scalar2=None,
                                                  op0=ALU.mult), reads=["tmpf", "gpre"], writes=["wdlr"])
    A("dve", lambda e: e.memset(ssp[:], 0.0), reads=["tmpf"], writes=["tmpf0", "tmpf1", "ssp0", "ssp1"])

    def pk(name, i, depth=2):
        return "pp_%s%d" % (name, i % depth)

    def pp_s0(i):
        p = i % 2
        xt = xg[:, p, :]
        A("sp", lambda e: e.dma_start(out=xt, in_=xpre[i * 128:(i + 1) * 128, :]), writes=["xg%d" % p], dma=True)
        A("act", lambda e: e.activation(out=junk[:], in_=xt, func=AF.Square, accum_out=ssp[:, p:p + 1]), reads=["xg%d" % p], writes=["junk", "ssp%d" % p])
        A("act", lambda e: e.activation(out=rstdp[:, p:p + 1], in_=ssp[:, p:p + 1], func=AF.Ln, scale=1.0 / D, bias=EPS),
          reads=["ssp%d" % p], writes=["rstdp%d" % p])
        A("act", lambda e: e.activation(out=rstdp[:, p:p + 1], in_=rstdp[:, p:p + 1], func=AF.Exp, scale=-0.5),
          reads=["rstdp%d" % p], writes=["rstdp%d" % p])
        A("dve", lambda e: e.tensor_scalar(out=pp_hb[p], in0=xt, scalar1=rstdp[:, p:p + 1], scalar2=None, op0=ALU.mult),
          reads=["xg%d" % p, "rstdp%d" % p], writes=[pk("hb", i)])

    def pp_s1(i):
        p = i % 2
        for kc in range(8):
            A("pe", lambda e, kc=kc: e.transpose(out=pst[:, kc, :], in_=pp_hb[p][:, kc * 128:(kc + 1) * 128], identity=ident[:]),
              reads=[pk("hb", i), "ident"], writes=["pst"])
        if p == 0:
            A("dve", lambda e: e.tensor_copy(out=pp_hT[p], in_=pst[:, :, :]), reads=["pst"], writes=[pk("hT", i)])
        else:
            A("act", lambda e: e.copy(out=pp_hT[p], in_=pst[:, :, :]), reads=["pst"], writes=[pk("hT", i)])

    def pp_s2(i):
        p = i % 2
        q4 = i % 4
        for (slot, bank) in ((skb, 0), (sv0, 1), (sv1, 2)):
            for kc in range(8):
                A("pe", lambda e, kc=kc, slot=slot, bank=bank: e.matmul(ps[bank][:, :], lhsT=pp_hT[p][:, kc, :], rhs=wblk[slot][:, kc, :],
                                                                        start=(kc == 0), stop=(kc == 7)),
                  reads=[pk("hT", i), "w%d" % slot], writes=["ps%d" % bank])
        A("act", lambda e: e.copy(out=pp_kbt[q4], in_=ps[0][:, :]), reads=["ps0"], writes=[pk("kbt", i, 4)])
        A("act", lambda e: e.copy(out=pp_vb[q4][:, 0:512], in_=ps[1][:, :]), reads=["ps1"], writes=[pk("vba", i, 4)])
        A("dve", lambda e: e.tensor_copy(out=pp_vb[q4][:, 512:1024], in_=ps[2][:, :]), reads=["ps2"], writes=[pk("vbb", i, 4)])

    def pp_s2b(i):
        p = i % 2
        for kc in range(8):
            A("pe", lambda e, kc=kc: e.matmul(ps[3][0:16, 0:128], lhsT=wdlr[:, kc, :], rhs=pp_hT[p][:, kc, :], start=(kc == 0), stop=(kc == 7)),
              reads=[pk("hT", i), "wdlr"], writes=["ps3"])
        A("act", lambda e: e.copy(out=dlrTb[0:16, p * 128:(p + 1) * 128], in_=ps[3][0:16, 0:128]), reads=["ps3"], writes=[pk("dl", i)])

    def pp_s3(i):
        p = i % 2
        A("pe", lambda e: e.matmul(ps[4][:, :], lhsT=dlrTb[0:32, p * 128:(p + 1) * 128], rhs=wup_bf[0:32, :], start=True, stop=True),
          reads=[pk("dl", i), "dlrTb", "wup_bf"], writes=["ps4"])
        A("act", lambda e: e.activation(out=tmpf[:, p * 512:(p + 1) * 512], in_=ps[4][:, :], func=AF.Exp, scale=-1.0), reads=["ps4"], writes=["tmpf%d" % p])
        A("act", lambda e: e.activation(out=pp_la[p], in_=tmpf[:, p * 512:(p + 1) * 512], func=AF.Ln, bias=1.0), reads=["tmpf%d" % p], writes=[pk("la", i)])

    def pp_s4(i):
        p = i % 2
        q4 = i % 4
        A("pe", lambda e: e.matmul(ps[5][:, :], lhsT=utf_bf[:], rhs=pp_la[p], start=True, stop=True), reads=[pk("la", i), "utf_bf"], writes=["ps5"])
        for h in range(4):
            A("pe", lambda e, h=h: e.matmul(ps[3][:, 256 + h * 2:258 + h * 2], lhsT=pp_la[p][:, h * 128:(h + 1) * 128], rhs=onesb[:, 0:2],
                                            start=True, stop=True), reads=[pk("la", i), "onesb"], writes=["ps3"])
        A("act", lambda e: e.activation(out=pp_ec[p], in_=ps[5][:, :], func=AF.Exp, scale=-1.0 / 16), reads=["ps5"], writes=[pk("ec", i)])
        A("act", lambda e: e.activation(out=etile[:, p, :], in_=ps[3][:, 256:264].rearrange("p (h n) -> p h n", n=2)[:, :, 0], func=AF.Exp,
                                        scale=-1.0 / 16), reads=["ps3"], writes=["etile%d" % p])
        A("dve", lambda e: e.tensor_tensor(out=pp_khat[p], in0=pp_kbt[q4], in1=pp_ec[p], op=ALU.mult), reads=[pk("kbt", i, 4), pk("ec", i)],
          writes=[pk("kh", i)])

    def pp_s5(i):
        p = i % 2
        q4 = i % 4
        for hp in range(2):
            bank = 6 if hp == 0 else 0
            for h in (2 * hp, 2 * hp + 1):
                A("pe", lambda e, h=h, bank=bank: e.matmul(ps[bank][:, (h % 2) * 256:(h % 2 + 1) * 256], lhsT=pp_khat[p][:, h * 128:(h + 1) * 128],
                                                           rhs=pp_vb[q4][:, h * 256:(h + 1) * 256], start=True, stop=True),
                  reads=[pk("kh", i), pk("vba", i, 4), pk("vbb", i, 4)], writes=["ps%d" % bank])
        for hp in range(2):
            bank = 6 if hp == 0 else 0
            for h in (2 * hp, 2 * hp + 1):
                A("dve", lambda e, h=h, bank=bank: e.scalar_tensor_tensor(out=S[:, h, :], in0=S[:, h, :], scalar=etile[:, p, h:h + 1],
                                                                          in1=ps[bank][:, (h % 2) * 256:(h % 2 + 1) * 256], op0=ALU.mult, op1=ALU.add),
                  reads=["S%d" % h, "etile%d" % p, "ps%d" % bank], writes=["S%d" % h])

    NPRE = 7 * NT_P
    for n in range(-2, NPRE + 3):
        for st, off in ((pp_s0, 2), (pp_s1, 1), (pp_s2, 0), (pp_s5, -3), (pp_s2b, 0), (pp_s3, -1), (pp_s4, -2)):
            i = n + off
            if 0 <= i < NPRE:
                st(i)
    A("act", lambda e: e.copy(out=Sbf[0][:], in_=S[:]), reads=["S%d" % h for h in range(4)] + ["S"], writes=["Sbf0"])
    ppkeys = ["S%d" % h for h in range(4)] + ["junk", "tmpf0", "tmpf1", "wdlr", "dlrTb", "wup_bf", "utf_bf", "onesb"]
    for p_ in range(4):
        ppkeys += ["pp_kbt%d" % p_, "pp_vba%d" % p_, "pp_vbb%d" % p_]
    for p_ in range(2):
        ppkeys += ["pp_hb%d" % p_, "pp_hT%d" % p_, "pp_la%d" % p_, "pp_ec%d" % p_, "pp_kh%d" % p_, "pp_dl%d" % p_, "etile%d" % p_, "ssp%d" % p_, "rstdp%d" % p_]
    A("dve", lambda e: e.memset(dlrT[0:16, :], 0.0), reads=ppkeys,
      writes=["dlrT", "aT", "obT", "gaT", "gbT", "hT", "tmpf", "fsb0", "fsb1", "S", "ps6", "pst", "ps0", "ps1", "ps2", "ps3", "ps4", "ps5", "xg0", "xg1"])
    for g in range(4 // G):
        run_group("halo", xh, g * T, G, a0=-4 + g * G)
    run_group("sample", xs, 0, 1, out_ap=ys, kv_out=(ksn, vsn, 0))
    for i in range(2):
        A("dve", lambda e, i=i: e.memset(qz[i][:], 0.0), writes=["qz%d" % i])
    ngroups = NT_P // G
    for g in range(ngroups):
        rbase = (g * G - (NT_P - 4)) * 128
        nxt = None
        if g + 1 < ngroups:
            nxt = (lambda g=g: prep_group(xp, (g + 1) * T, G, (g + 1) % 2))
        run_group("prompt", xp, g * T, G, a0=g * G, out_ap=yp, kv_out=(kp, vp, rbase) if rbase >= 0 else None,
                  xpar=g % 2, prefetched=(g > 0), next_prep=nxt)
    A("act", lambda e: e.dma_start(out=spo[:, :], in_=S[:].rearrange("p h v -> p (h v)")), reads=["S"], dma=True, is_out=True, semkey="spo")
    P.finish()
    P.emit()
    return nc, P


def _const_masks(c):
    n = 128
    s = np.arange(n)[:, None]
    t = np.arange(n)[None, :]
    same = (s // c) == (t // c)
    UT = ((s > t) & same).astype(np.float32)
    maskT = ((s <= t) & same).astype(np.float32)
    restart = np.ones((128, 4, 128), np.float32)
    restart[:, :, ::c] = 0.0
    nch = n // c
    rm = np.zeros((128, nch), np.float32)
    for i in range(nch):
        rm[i * c:(i + 1) * c, i] = 1.0
    out = np.zeros((128, 6 * 128 + 512 + nch), np.float32)
    out[:, 0:128] = UT
    out[:, 128:640] = np.tile(maskT, (1, 4))
    out[:, 640:768] = (s > t).astype(np.float32)
    out[:, 768:1280] = restart.reshape(128, 512)
    out[:, 1280:1280 + nch] = rm
    return out


def _bias_tables(rel):
    p = np.arange(128)[:, None]
    col = np.arange(128)[None, :]
    kc, kl = p // 64, p % 64
    qc, ql = col // 64, col % 64
    idxP = np.zeros((3, 128, 128), np.int64)
    mk = np.zeros((128, 3, 128), np.float32)
    for j, d in enumerate((0, 1, 4)):
        o = 2 * d + qc - kc
        dist = 64 * o + ql - kl
        idxP[j] = np.clip(dist, -128, 128) + 128
        mk[:, j, :] = np.where((o >= 0) & (o <= 8), 0.0, NEG)
    biasP = rel[:, idxP]
    biasP = np.ascontiguousarray(biasP.transpose(2, 0, 1, 3)).reshape(128, 8 * 384)
    q32 = np.arange(32)[None, :]
    d0 = q32 + 512 - (384 + p)
    d1 = q32 - np.minimum(p, 31)
    idxS = np.stack([np.clip(d0, -128, 128) + 128, np.clip(d1, -128, 128) + 128], 0)
    biasS = rel[:, idxS]
    biasS = np.ascontiguousarray(biasS.transpose(2, 0, 1, 3)).reshape(128, 8 * 64)
    cb = np.ascontiguousarray(np.broadcast_to(rel[:, 256][None, :], (128, 8)))
    return biasP.astype(np.float32), mk.reshape(128, 384), biasS.astype(np.float32), cb.astype(np.float32)


_CACHE = {}


def kernel(x_prompt, x_sample, cache_attn_k, cache_attn_v, state_gla,
           norm_mix_pre, norm_mix_post, norm_ffn_pre, norm_ffn_post,
           w_in, w_decay_up, b_decay, rel_bias, gla_norm, w_proj_a, w_proj_b, w_out,
           w_ffn_gate, w_ffn_up, w_ffn_down):
    f = lambda a: np.ascontiguousarray(np.asarray(a, dtype=np.float32))
    x_prompt, x_sample = f(x_prompt), f(x_sample)
    if "nc" not in _CACHE:
        _CACHE["nc"] = build_program()
    nc, P = _CACHE["nc"]
    xpf = x_prompt.reshape(SEQ, D)
    wup = np.zeros((32, 512), np.float32)
    wup[0:16] = f(w_decay_up)[0]
    wup[16] = f(b_decay)[0]
    ones_row = np.zeros((16, G * 128), np.float32)
    ones_row[0] = 1.0
    bP, mkP, bS, cb = _bias_tables(f(rel_bias)[0])
    shared = {
        "w_in": f(w_in)[0], "wup": wup,
        "gpre": np.ascontiguousarray(f(norm_mix_pre)[0].reshape(8, 128).T),
        "gffn": np.ascontiguousarray(f(norm_ffn_pre)[0].reshape(8, 128).T),
        "gpost": f(norm_mix_post)[0].reshape(1, D), "gfpost": f(norm_ffn_post)[0].reshape(1, D),
        "gnorm": f(gla_norm)[0].reshape(1, 256),
        "wpa": f(w_proj_a)[0], "wpb": f(w_proj_b)[0], "wout": f(w_out)[0],
        "wg": f(w_ffn_gate)[0], "wu": f(w_ffn_up)[0], "wd": f(w_ffn_down)[0],
        "biasP": bP, "mkP": mkP, "biasS": bS, "cbias": cb,
        "cm64": _const_masks(64), "cm32": _const_masks(32), "ones_row": ones_row,
    }
    ck = f(cache_attn_k)[0].reshape(32, 512, 512)
    cv = f(cache_attn_v)[0].reshape(32, 512, 512)
    sgl = f(state_gla)[0]
    in_maps = []
    for c in range(NCORES):
        m = dict(shared)
        m["xp"] = xpf[c * TOK_CORE:(c + 1) * TOK_CORE]
        m["xh"] = xpf[c * TOK_CORE - 512:c * TOK_CORE] if c > 0 else np.zeros((512, D), np.float32)
        xpre = np.zeros((7 * TOK_CORE, D), np.float32)
        if c > 0:
            xpre[(7 - c) * TOK_CORE:] = xpf[0:c * TOK_CORE]
        m["xpre"] = xpre
        m["xs"] = x_sample[4 * c:4 * c + 4].reshape(128, D)
        m["ck"] = ck[4 * c:4 * c + 4]
        m["cv"] = cv[4 * c:4 * c + 4]
        m["sg"] = sgl[4 * c:4 * c + 4]
        m["valid"] = np.full((128, 1), 1.0 if c > 0 else 0.0, np.float32)
        in_maps.append(m)
    res = run_bass_kernel_spmd(nc, in_maps, core_ids=list(range(NCORES)))
    R = res.results
    yp = np.concatenate([R[c]["yp"] for c in range(NCORES)], 0).reshape(1, SEQ, D)
    ys = np.concatenate([R[c]["ys"] for c in range(NCORES)], 0).reshape(32, 32, D)
    kpo = R[NCORES - 1]["kp"].reshape(1, 1, 512, 8, 64)
    vpo = R[NCORES - 1]["vp"].reshape(1, 1, 512, 8, 64)
    spo = R[NCORES - 1]["spo"].reshape(128, 4, 256).transpose(1, 0, 2).reshape(1, 1, 4, 128, 256)
    kso = np.concatenate([R[c]["ksn"] for c in range(NCORES)], 0).reshape(1, 32, 32, 8, 64)
    vso = np.concatenate([R[c]["vsn"] for c in range(NCORES)], 0).reshape(1, 32, 32, 8, 64)
    sso = np.concatenate([R[c]["sso"] for c in range(NCORES)], 0).reshape(32, 128, 4, 256).transpose(0, 2, 1, 3).reshape(1, 32, 4, 128, 256)
    return (yp, ys, kpo, vpo, np.ascontiguousarray(spo), kso, vso, np.ascontiguousarray(sso))
```

```python
import contextlib
import numpy as np
import concourse.bass as bass
import concourse.mybir as mybir
from concourse.bass_utils import run_bass_kernel_spmd

F32 = mybir.dt.float32
BF16 = mybir.dt.bfloat16
AF = mybir.ActivationFunctionType
ALU = mybir.AluOpType

ENGS = ("pe", "act", "dve", "pool", "sp")


class Op:
    __slots__ = ("eng", "fn", "deps", "idx", "sig", "seq", "is_dma", "semkey", "sem", "semval", "inc")


class KeyState:
    __slots__ = ("writers", "readers")

    def __init__(self):
        self.writers = {}
        self.readers = {}


class Prog:
    def __init__(self, nc):
        self.nc = nc
        self.ops = []
        self.state = {}
        self.stack = contextlib.ExitStack()
        self.out_dmas = []
        self.last_dma = {}
        self.cc_barrier = None

    def sbuf(self, name, shape, dtype):
        return self.stack.enter_context(self.nc.sbuf_tensor(name, list(shape), dtype))

    def psum(self, name, shape, dtype):
        return self.stack.enter_context(self.nc.psum_tensor(name, list(shape), dtype))

    def add(self, eng, fn, reads=(), writes=(), dma=False, semkey=None, is_out=False, inc=16, barrier=False):
        op = Op()
        op.eng = eng
        op.fn = fn
        op.idx = len(self.ops)
        op.sig = False
        op.seq = None
        op.is_dma = dma
        op.semkey = None
        op.sem = None
        op.semval = None
        op.inc = inc
        deps = {}
        ek = ("dma", op.idx) if dma else eng
        psr = [k for k in reads if k.startswith("ps")]
        if psr:
            reads = [k for k in reads if not k.startswith("ps")]
            writes = list(writes) + [k for k in psr if k not in writes]
        for k in reads:
            st = self.state.get(k)
            if st is None:
                st = self.state[k] = KeyState()
            for d in st.writers.values():
                deps[d.idx] = d
        for k in writes:
            st = self.state.get(k)
            if st is None:
                st = self.state[k] = KeyState()
            for d in st.writers.values():
                deps[d.idx] = d
            for d in st.readers.values():
                deps[d.idx] = d
        for k in reads:
            self.state[k].readers[ek] = op
        for k in writes:
            st = self.state[k]
            st.writers = {ek: op}
            st.readers = {}
        if dma:
            if barrier:
                for d in self.last_dma.values():
                    deps[d.idx] = d
                self.cc_barrier = op
            elif self.cc_barrier is not None:
                deps[self.cc_barrier.idx] = self.cc_barrier
        deps.pop(op.idx, None)
        op.deps = list(deps.values())
        if dma:
            if semkey is None:
                semkey = writes[0] if writes else reads[0]
            op.semkey = semkey
            self.last_dma[semkey] = op
            if is_out:
                self.out_dmas.append(op)
        self.ops.append(op)
        return op

    def finish(self):
        op = self.add("sp", None)
        op.deps = list(self.out_dmas)

    def emit(self):
        nc = self.nc

        def need_wait(op, d):
            if d.is_dma:
                return True
            if d.eng == op.eng:
                if op.is_dma:
                    return True
                if op.eng == "pe":
                    return False
                return True
            return True

        for op in self.ops:
            for d in op.deps:
                if not d.is_dma and need_wait(op, d):
                    d.sig = True
        counters = {e: 0 for e in ENGS}
        for op in self.ops:
            if op.sig and not op.is_dma:
                counters[op.eng] += 1
                op.seq = counters[op.eng]
        semkeys = []
        seen = set()
        for op in self.ops:
            if op.is_dma and op.semkey not in seen:
                seen.add(op.semkey)
                semkeys.append(op.semkey)
        self.n_sems = len(semkeys) + len(ENGS)
        engsem = {e: self.stack.enter_context(nc.semaphore("s_" + e)) for e in ENGS}
        dmasem = {k: self.stack.enter_context(nc.semaphore("d_%d" % i)) for i, k in enumerate(semkeys)}
        dmacnt = {k: 0 for k in semkeys}
        for op in self.ops:
            if op.is_dma:
                dmacnt[op.semkey] += op.inc
                op.sem = dmasem[op.semkey]
                op.semval = dmacnt[op.semkey]
        per_eng = {e: [o for o in self.ops if o.eng == e] for e in ENGS}

        def run(e, engobj):
            waited = {}
            for op in per_eng[e]:
                for d in op.deps:
                    if not need_wait(op, d):
                        continue
                    if d.is_dma:
                        key = ("d", d.semkey)
                        val = d.semval
                        sem = d.sem
                    else:
                        key = ("e", d.eng)
                        val = d.seq
                        sem = engsem[d.eng]
                    if waited.get(key, 0) >= val:
                        continue
                    waited[key] = val
                    engobj.wait_ge(sem, val)
                if op.fn is None:
                    continue
                ins = op.fn(engobj)
                if op.is_dma:
                    if op.inc == 16:
                        ins.then_inc(op.sem, 16)
                    else:
                        ins.then_inc(op.sem)
                elif op.sig:
                    ins.then_inc(engsem[e], 1)

        with nc.Block() as block:
            @block.sync
            def _(eng):
                run("sp", eng)

            @block.tensor
            def _(eng):
                run("pe", eng)

            @block.scalar
            def _(eng):
                run("act", eng)

            @block.vector
            def _(eng):
                run("dve", eng)

            @block.gpsimd
            def _(eng):
                run("pool", eng)

    def close(self):
        self.stack.close()


D = 1024
DIN = 6672
DFF = 2816
NCORES = 8
SEQ = 16384
TOK_CORE = SEQ // NCORES
NT_P = TOK_CORE // 128
G = 2
QA, KA, VA, QB, KB, VB, RB, DLR, GA, GB = 0, 512, 1024, 1536, 2048, 2560, 3584, 4608, 4624, 5648
EPS = 1e-6
NEG = -30000.0


def build_program(debug=False):
    nc = bass.Bass("TRN2", target_bir_lowering=False)
    dbg = {}
    P = Prog(nc)
    A = P.add

    def din(name, shape):
        return nc.dram_tensor(name, list(shape), F32, kind="ExternalInput").ap()

    def dout(name, shape):
        return nc.dram_tensor(name, list(shape), F32, kind="ExternalOutput").ap()

    xp = din("xp", [TOK_CORE, D])
    xh = din("xh", [512, D])
    xpre = din("xpre", [7 * TOK_CORE, D])
    ones_row = din("ones_row", [16, G * 128])
    xs = din("xs", [128, D])
    ck = din("ck", [4, 512, 512])
    cv = din("cv", [4, 512, 512])
    sg = din("sg", [4, 4, 128, 256])
    valid = din("valid", [128, 1])
    w_in = din("w_in", [D, DIN])
    wup = din("wup", [32, 512])
    gpre = din("gpre", [128, 8])
    gffn = din("gffn", [128, 8])
    gpost = din("gpost", [1, D])
    gfpost = din("gfpost", [1, D])
    gnorm = din("gnorm", [1, 256])
    wpa = din("wpa", [512, D])
    wpb = din("wpb", [D, D])
    wout = din("wout", [D, D])
    wg = din("wg", [D, DFF])
    wu = din("wu", [D, DFF])
    wd = din("wd", [DFF, D])
    biasP = din("biasP", [128, 8 * 384])
    mkP = din("mkP", [128, 384])
    biasS = din("biasS", [128, 8 * 64])
    cbias = din("cbias", [128, 8])
    cm64 = din("cm64", [128, 6 * 128 + 512 + 2])
    cm32 = din("cm32", [128, 6 * 128 + 512 + 4])

    yp = dout("yp", [TOK_CORE, D])
    ys = dout("ys", [128, D])
    kp = dout("kp", [512, 512])
    vp = dout("vp", [512, 512])
    spo = dout("spo", [128, 1024])
    ksn = dout("ksn", [128, 512])
    vsn = dout("vsn", [128, 512])
    sso = dout("sso", [4, 128, 1024])

    wsrc = {"w_in": (w_in, D, DIN), "wpa": (wpa, 512, D), "wpb": (wpb, D, D), "wout": (wout, D, D),
            "wg": (wg, D, DFF), "wu": (wu, D, DFF), "wd": (wd, DFF, D)}

    T = G * 128
    xg = P.sbuf("xg", [128, 2 * G, D], F32)
    hb = P.sbuf("hb", [128, D], BF16)
    hb2 = P.sbuf("hb2", [128, D], BF16)
    hT = P.sbuf("hT", [128, 8, T], BF16)
    qTa = P.sbuf("qTa", [128, 4, T], BF16)
    kTa = P.sbuf("kTa", [128, 4, 8 * 128], BF16)
    vA = P.sbuf("vA", [128, 64, 65], BF16)
    qTb = P.sbuf("qTb", [128, 4, T], BF16)
    kTb = P.sbuf("kTb", [128, 4, T], BF16)
    kbt = P.sbuf("kbt", [128, G, 512], BF16)
    vb = P.sbuf("vb", [128, G, 1024], BF16)
    rbs = P.sbuf("rbs", [128, G, 1024], BF16)
    dlrT = P.sbuf("dlrT", [32, T], F32)
    gaT = P.sbuf("gaT", [128, 8, T], BF16)
    gbT = P.sbuf("gbT", [128, 8, T], BF16)
    oaT = P.sbuf("oaT", [128, 4, T], BF16)
    obT = P.sbuf("obT", [128, 8, T], BF16)
    aT = P.sbuf("aT", [128, 22, T], BF16)
    fsb = P.sbuf("fsb", [128, G, D], F32)
    wblk = [P.sbuf("wblk%d" % i, [128, 8, 512], BF16) for i in range(3)]
    ystage = [P.sbuf("yst%d" % i, [128, D], F32) for i in range(2)]
    kvst = [P.sbuf("kvst%d" % i, [128, 512], F32) for i in range(2)]
    ident = P.sbuf("ident", [128, 128], BF16)
    gpre_sb = P.sbuf("gpre_sb", [128, 8], F32)
    gffn_sb = P.sbuf("gffn_sb", [128, 8], F32)
    gpost_sb = P.sbuf("gpost_sb", [128, D], F32)
    gfpost_sb = P.sbuf("gfpost_sb", [128, D], F32)
    gn_sb = P.sbuf("gn_sb", [128, 256], F32)
    wup_sb = P.sbuf("wup_sb", [32, 512], F32)
    biasP_sb = P.sbuf("biasP_sb", [128, 8, 384], BF16)
    biasS_sb = P.sbuf("biasS_sb", [128, 8, 64], BF16)
    cbias_sb = P.sbuf("cbias_sb", [128, 8], F32)
    valid_sb = P.sbuf("valid_sb", [128, 1], F32)
    cm = {64: P.sbuf("cm64_sb", [128, 6 * 128 + 512 + 2], F32), 32: P.sbuf("cm32_sb", [128, 6 * 128 + 512 + 4], F32)}
    ones8 = P.sbuf("ones8", [128, 8], F32)
    ss = P.sbuf("ss", [128, 4], F32)
    ssn = P.sbuf("ssn", [128, 4], F32)
    rstdn = P.sbuf("rstdn", [128, 4], F32)
    rstd = P.sbuf("rstd", [128, 4], F32)
    junk = P.sbuf("junk", [128, D], BF16)
    tmpf = P.sbuf("tmpf", [128, D], F32)
    mkP_sb = tmpf[:, 0:384]
    identf = tmpf[:, 512:640]
    stmp2 = [P.sbuf("stmp%d" % i, [128, 3, 128], F32) for i in range(2)]
    expT2 = [P.sbuf("expT%d" % i, [128, 5, 128], BF16) for i in range(2)]
    rden = P.sbuf("rden", [128, 8], F32)
    oa_tok = P.sbuf("oa_tok", [128, 512], BF16)
    ob_tok = P.sbuf("ob_tok", [128, 1024], BF16)
    la_neg = P.sbuf("la_neg", [128, 512], F32)
    laT_neg = P.sbuf("laT_neg", [128, 512], F32)
    bneg = P.sbuf("bneg", [128, 512], F32)
    eb = P.sbuf("eb", [128, 512], F32)
    enb = P.sbuf("enb", [128, 512], F32)
    ec = P.sbuf("ec", [128, 512], F32)
    qtl = P.sbuf("qtl", [128, 4, 128], BF16)
    ktl = P.sbuf("ktl", [128, 4, 128], BF16)
    qz = [P.sbuf("qz%d" % i, [128, 4, 128], BF16) for i in range(4)]
    khat = [P.sbuf("khat%d" % i, [128, 512], BF16) for i in range(4)]
    attn_sb = P.sbuf("attn_sb", [128, 4, 128], BF16)
    S = P.sbuf("S", [128, 4, 256], F32)
    Sbf = [P.sbuf("Sbf%d" % i, [128, 4, 256], BF16) for i in range(2)]
    vc = P.sbuf("vc", [128, 32, 65], BF16)
    rdens = P.sbuf("rdens", [32, 8], F32)
    stmps2 = [P.sbuf("stmps%d" % i, [128, 2, 32], F32) for i in range(2)]
    expTs2 = [P.sbuf("expTs%d" % i, [128, 5, 32], BF16) for i in range(2)]

    vnew = aT[0:32, 0:9, :].rearrange("p a b -> p (a b)")[:, 0:2080].rearrange("p (n d) -> p n d", d=65)
    oas = aT[0:32, 9:18, :].rearrange("p a b -> p (a b)")[:, 0:2048].rearrange("p (n d) -> p n d", d=512)
    kc_tok = tmpf[:].bitcast(BF16).rearrange("p (t f) -> p t f", t=4)
    Sbs = aT[:, 6:22, :]
    kTc = obT[:].rearrange("p a b -> p (a b)").rearrange("p (a b) -> p a b", a=4)
    wdlr = P.sbuf("wdlr", [128, 8, 16], BF16)
    wup_bf = P.sbuf("wup_bf", [32, 512], BF16)
    dlrTb = P.sbuf("dlrTb", [32, 256], BF16)
    onesb = P.sbuf("onesb", [128, 2], BF16)
    utf_bf = P.sbuf("utf_bf", [128, 128], BF16)
    etile = P.sbuf("etile", [128, 2, 4], F32)
    ssp = P.sbuf("ssp", [128, 2], F32)
    rstdp = P.sbuf("rstdp", [128, 2], F32)
    aTf = aT[:].rearrange("p a b -> p (a b)")
    pp_hb = [aTf[:, i * 1024:(i + 1) * 1024] for i in range(2)]
    pp_hT = [aTf[:, 2048 + i * 1024:2048 + (i + 1) * 1024].rearrange("p (k n) -> p k n", k=8) for i in range(2)]
    obTf = obT[:].rearrange("p a b -> p (a b)")
    pp_kbt = [obTf[:, i * 512:(i + 1) * 512] for i in range(4)]
    gaTf = gaT[:].rearrange("p a b -> p (a b)")
    gbTf = gbT[:].rearrange("p a b -> p (a b)")
    pp_vb = [gaTf[:, 0:1024], gaTf[:, 1024:2048], gbTf[:, 0:1024], gbTf[:, 1024:2048]]
    hTf = hT[:].rearrange("p a b -> p (a b)")
    pp_la = [hTf[:, i * 512:(i + 1) * 512] for i in range(2)]
    pp_khat = [hTf[:, 1024 + i * 512:1024 + (i + 1) * 512] for i in range(2)]
    fsbf = fsb[:].rearrange("p a b -> p (a b)")
    pp_ec = [fsbf[:, i * 512:(i + 1) * 512] for i in range(2)]
    ps = [P.psum("ps%d" % i, [128, 512], F32) for i in range(7)]
    pst = P.psum("pst", [128, 8, 128], BF16)

    def dump(name, ap_sb, shape, dtype, key):
        if not debug:
            return
        t = nc.dram_tensor("dbg_" + name, list(shape), F32, kind="ExternalOutput").ap()
        if dtype == F32:
            A("sp", lambda e: e.dma_start(out=t, in_=ap_sb), reads=[key], dma=True, is_out=True, semkey="dbg_" + name)
        else:
            n = shape[1] * shape[2]
            if n <= 1024:
                stg = fsb[:, 1, 0:n].rearrange("p (a b) -> p a b", a=shape[1])
            else:
                stg = fsb[:].rearrange("p a b -> p (a b)")[:, 0:n].rearrange("p (a b) -> p a b", a=shape[1])
            A("dve", lambda e: e.tensor_copy(out=stg, in_=ap_sb), reads=[key], writes=["fsb0", "fsb1"])
            A("sp", lambda e: e.dma_start(out=t, in_=stg), reads=["fsb0", "fsb1"], dma=True, is_out=True, semkey="dbg_" + name)

    def mask_UT(c):
        return cm[c][:, 0:128]

    def mask_T4(c):
        return cm[c][:, 128:640]

    def mask_restart(c):
        return cm[c][:, 768:1280]

    def rowmask(c, i):
        return cm[c][:, 1280 + i:1281 + i]

    def load(dst, src, key, eng="sp"):
        if eng == "pool":
            A(eng, lambda e: e.dma_start(out=dst, in_=src), writes=["poolq", key], dma=True, semkey="poolq")
        else:
            A(eng, lambda e: e.dma_start(out=dst, in_=src), writes=[key], dma=True)

    load(gpre_sb[:], gpre[:, :], "gpre")
    load(gffn_sb[:], gffn[:, :], "gffn")
    load(gpost_sb[:], gpost.broadcast_to([128, D]), "gpost")
    load(gfpost_sb[:], gfpost.broadcast_to([128, D]), "gfpost")
    load(gn_sb[:], gnorm.broadcast_to([128, 256]), "gn")
    load(wup_sb[:], wup[:, :], "wup")
    load(mkP_sb, mkP[:, :], "tmpf")
    load(cbias_sb[:], cbias[:, :], "cbias")
    load(valid_sb[:], valid[:, :], "valid")
    load(cm[64][:], cm64[:, :], "cm64")
    load(cm[32][:], cm32[:, :], "cm32")
    load(biasP_sb[:], biasP.rearrange("p (h n) -> p h n", h=8), "biasP", eng="pool")
    load(biasS_sb[:], biasS.rearrange("p (h n) -> p h n", h=8), "biasS", eng="pool")
    A("dve", lambda e: e.memset(identf, 1.0), writes=["tmpf"])
    A("pool", lambda e: e.affine_select(out=identf, in_=identf, pattern=[[-1, 128]], compare_op=ALU.is_equal,
                                        fill=0.0, base=0, channel_multiplier=1), reads=["tmpf"], writes=["tmpf"])
    A("dve", lambda e: e.tensor_copy(out=ident[:], in_=identf), reads=["tmpf"], writes=["ident"])
    A("dve", lambda e: e.memset(ones8[:], 1.0), writes=["ones8"])
    for h in range(8):
        A("dve", lambda e, h=h: e.tensor_tensor(out=biasP_sb[:, h, :], in0=biasP_sb[:, h, :], in1=mkP_sb, op=ALU.add),
          reads=["biasP", "tmpf"], writes=["biasP"])
    A("dve", lambda e: e.memset(vA[:, :, 64:65], 1.0), writes=["vA%d" % s for s in range(8)])
    A("dve", lambda e: e.memset(vc[:, :, 64:65], 1.0), writes=["vc"])
    A("dve", lambda e: e.memset(dlrT[:], 0.0), writes=["dlrT"])
    for i in range(4):
        A("dve", lambda e, i=i: e.memset(qz[i][:], 0.0), writes=["qz%d" % i])
    A("dve", lambda e: e.memset(S[:], 0.0), writes=["S", "S0", "S1", "S2", "S3"])
    A("dve", lambda e: e.memset(Sbf[0][:], 0.0), writes=["Sbf0"])

    wctr = [0]
    A("dve", lambda e: e.memset(dlrTb[:], 0.0), writes=["dlrTb"])
    A("pool", lambda e: e.dma_start(out=dlrTb[16:32, :], in_=ones_row[:, :]), writes=["poolq", "dlrTb"], dma=True, semkey="poolq")
    cvctr = [0]
    wblocks = {}

    def register_block(wname, k0, nkc, c0, ncols):
        sig = (wname, k0, nkc, c0, ncols)
        if sig in wblocks:
            return wblocks[sig]
        scr = nc.dram_tensor("scr_%s_%d_%d" % (wname, k0, c0), [128, nkc * ncols], BF16).ap()
        src = wsrc[wname][0].rearrange("(k p) n -> p k n", p=128)[:, k0:k0 + nkc, c0:c0 + ncols]
        q = "cvq%d" % (cvctr[0] % 4)
        cvctr[0] += 1
        op = A("pool", lambda e: e.dma_start(out=scr.rearrange("p (k n) -> p k n", k=nkc), in_=src), writes=[q], dma=True, semkey=q)
        wblocks[sig] = (scr, op)
        return wblocks[sig]

    for c0_ in (QA, KA, VA, QB, KB, VB, VB + 512, RB, RB + 512):
        register_block("w_in", 0, 8, c0_, 512)
    register_block("w_in", 0, 8, DLR, 16)
    for c0_ in (GA, GA + 512, GB, GB + 512):
        register_block("w_in", 0, 8, c0_, 512)
    for half_ in range(2):
        register_block("wpa", 0, 4, half_ * 512, 512)
    for half_ in range(2):
        register_block("wpb", 0, 8, half_ * 512, 512)
    for cb_ in range(2):
        register_block("wout", 0, 8, cb_ * 512, 512)
    for fb4_ in range(0, 22, 4):
        nb_ = min(4, 22 - fb4_)
        register_block("wg", 0, 8, fb4_ * 128, nb_ * 128)
        register_block("wu", 0, 8, fb4_ * 128, nb_ * 128)
    for cb_ in range(2):
        for (k0_, nk_) in ((0, 8), (8, 8), (16, 6)):
            register_block("wd", k0_, nk_, cb_ * 512, 512)

    wctr = [0]
    wblk.append(fsb[:].rearrange("p a b -> p (a b)").bitcast(BF16).rearrange("p (k n) -> p k n", k=8))
    slotkeys = {0: ["w0"], 1: ["w1"], 2: ["w2"], 3: ["w3", "fsb0", "fsb1"]}

    def load_w(wname, k0, nkc, c0, ncols, slot=None):
        if slot is None:
            slot = wctr[0] % 3
            wctr[0] += 1
        scr, cop = register_block(wname, k0, nkc, c0, ncols)
        op = A("sp", lambda e: e.dma_start(out=wblk[slot][:, 0:nkc, 0:ncols], in_=scr.rearrange("p (k n) -> p k n", k=nkc)),
               writes=slotkeys[slot], dma=True, semkey="w%d" % slot)
        if cop.idx not in set(d.idx for d in op.deps):
            op.deps.append(cop)
        return slot

    pctr = [0]

    def acc_bank():
        b = pctr[0] % 2
        pctr[0] += 1
        return b

    def bstyle(slot, nkc, nblk, src, srckey, Tg, evac, mrows=128):
        for ob in range(nblk):
            b = acc_bank()
            for kc in range(nkc):
                A("pe", lambda e, kc=kc, ob=ob, b=b: e.matmul(ps[b][0:mrows, 0:Tg], lhsT=wblk[slot][:, kc, ob * 128:ob * 128 + mrows],
                                                               rhs=src[:, kc, 0:Tg], start=(kc == 0), stop=(kc == nkc - 1)),
                  reads=["w%d" % slot, srckey], writes=["ps%d" % b])
            evac(ps[b], "ps%d" % b, ob)

    def astyle_tile(slot, nkc, ncols, src, srckey, tok0, mtok, evac, kc_off=0):
        b = acc_bank()
        for kc in range(nkc):
            A("pe", lambda e, kc=kc, b=b: e.matmul(ps[b][0:mtok, 0:ncols], lhsT=src[:, kc_off + kc, tok0:tok0 + mtok],
                                                   rhs=wblk[slot][:, kc, 0:ncols], start=(kc == 0), stop=(kc == nkc - 1)),
              reads=["w%d" % slot, srckey], writes=["ps%d" % b])
        evac(ps[b], "ps%d" % b)

    def norm_to_hT(src_ap, srckey, gcol_sb, gkey, tok0):
        A("act", lambda e: e.activation(out=junk[:], in_=src_ap, func=AF.Square, accum_out=ss[:, 0:1]),
          reads=[srckey], writes=["junk", "ss"])
        A("act", lambda e: e.activation(out=rstd[:, 0:1], in_=ss[:, 0:1], func=AF.Ln, scale=1.0 / D, bias=EPS),
          reads=["ss"], writes=["rstd"])
        A("act", lambda e: e.activation(out=rstd[:, 0:1], in_=rstd[:, 0:1], func=AF.Exp, scale=-0.5),
          reads=["rstd"], writes=["rstd"])
        A("dve", lambda e: e.tensor_scalar(out=hb[:], in0=src_ap, scalar1=rstd[:, 0:1], scalar2=None, op0=ALU.mult),
          reads=[srckey, "rstd"], writes=["hb"])
        for kc in range(8):
            A("pe", lambda e, kc=kc: e.transpose(out=pst[:, kc, :], in_=hb[:, kc * 128:(kc + 1) * 128], identity=ident[:]),
              reads=["hb", "ident"], writes=["pst"])
        for kc in range(8):
            A("act", lambda e, kc=kc: e.activation(out=hT[:, kc, tok0:tok0 + 128], in_=pst[:, kc, :], func=AF.Copy,
                                                   scale=gcol_sb[:, kc:kc + 1]),
              reads=["pst", gkey], writes=["hT"])

    def transpose_to(src_tok, srckey, nblk, dst, dstkey, tok0, rows=128):
        for blk in range(nblk):
            A("pe", lambda e, blk=blk: e.transpose(out=pst[:, blk, 0:rows], in_=src_tok[0:rows, blk * 128:(blk + 1) * 128],
                                                   identity=ident[0:rows, 0:rows]),
              reads=[srckey, "ident"], writes=["pst"])
        A("act", lambda e: e.copy(out=dst[:, 0:nblk, tok0:tok0 + rows], in_=pst[:, 0:nblk, 0:rows]),
          reads=["pst"], writes=[dstkey])

    def gla_tile(t, c, state_only, sample=False, bk=None):
        bk = bk or {"la": 2, "laT": 3, "c": 4, "at": 5, "st": 6}
        B_la, B_laT, B_c, B_at, B_st = bk["la"], bk["laT"], bk["c"], bk["at"], bk["st"]
        nch = 128 // c
        tok0 = t * 128
        A("pe", lambda e: e.matmul(ps[B_la][:, :], lhsT=dlrT[0:32, tok0:tok0 + 128], rhs=wup_sb[0:32, :], start=True, stop=True),
          reads=["dlrT", "wup"], writes=["ps%d" % B_la])
        for h in range(4):
            A("pe", lambda e, h=h: e.matmul(ps[B_laT][:, h * 128:(h + 1) * 128], lhsT=wup_sb[0:32, h * 128:(h + 1) * 128],
                                            rhs=dlrT[0:32, tok0:tok0 + 128], start=True, stop=True),
              reads=["dlrT", "wup"], writes=["ps%d" % B_laT])
        A("act", lambda e: e.activation(out=la_neg[:], in_=ps[B_la][:, :], func=AF.Exp, scale=-1.0), reads=["ps%d" % B_la], writes=["la_neg"])
        A("act", lambda e: e.activation(out=laT_neg[:], in_=ps[B_laT][:, :], func=AF.Exp, scale=-1.0), reads=["ps%d" % B_laT], writes=["laT_neg"])
        A("act", lambda e: e.activation(out=la_neg[:], in_=la_neg[:], func=AF.Ln, bias=1.0), reads=["la_neg"], writes=["la_neg"])
        A("act", lambda e: e.activation(out=laT_neg[:], in_=laT_neg[:], func=AF.Ln, bias=1.0), reads=["laT_neg"], writes=["laT_neg"])
        yield
        A("pe", lambda e: e.matmul(ps[B_c][:, :], lhsT=mask_UT(c), rhs=la_neg[:], start=True, stop=True),
          reads=["la_neg", "cm%d" % c], writes=["ps%d" % B_c])
        A("dve", lambda e: e.tensor_tensor_scan(out=bneg[:], data0=mask_restart(c), data1=laT_neg[:], initial=0.0,
                                                op0=ALU.mult, op1=ALU.add),
          reads=["laT_neg", "cm%d" % c], writes=["bneg"])
        A("act", lambda e: e.activation(out=ec[:], in_=ps[B_c][:, :], func=AF.Exp, scale=-1.0 / 16), reads=["ps%d" % B_c], writes=["ec"])
        A("act", lambda e: e.activation(out=eb[:], in_=bneg[:], func=AF.Exp, scale=-1.0 / 16), reads=["bneg"], writes=["eb"])
        if not state_only:
            A("act", lambda e: e.activation(out=enb[:], in_=bneg[:], func=AF.Exp, scale=1.0 / 16), reads=["bneg"], writes=["enb"])
        for i in range(nch):
            A("dve", lambda e, i=i: e.scalar_tensor_tensor(out=khat[i][:], in0=kbt[:, t, :], scalar=rowmask(c, i), in1=ec[:],
                                                           op0=ALU.mult, op1=ALU.mult),
              reads=["kbt", "ec", "cm%d" % c], writes=["khat%d" % i])
        yield
        if not state_only:
            A("dve", lambda e: e.tensor_tensor(out=qtl[:], in0=qTb[:, :, tok0:tok0 + 128],
                                               in1=eb[:].rearrange("p (h n) -> p h n", h=4), op=ALU.mult),
              reads=["qTb", "eb"], writes=["qtl"])
            A("dve", lambda e: e.tensor_tensor(out=ktl[:], in0=kTb[:, :, tok0:tok0 + 128],
                                               in1=enb[:].rearrange("p (h n) -> p h n", h=4), op=ALU.mult),
              reads=["kTb", "enb"], writes=["ktl"])
            for i in range(nch):
                A("dve", lambda e, i=i: e.tensor_copy(out=qz[i][:, :, i * c:(i + 1) * c], in_=qtl[:, :, i * c:(i + 1) * c]),
                  reads=["qtl"], writes=["qz%d" % i])
            yield
            for h in range(4):
                A("pe", lambda e, h=h: e.matmul(ps[B_at][:, h * 128:(h + 1) * 128], lhsT=ktl[:, h, :], rhs=qtl[:, h, :], start=True, stop=True),
                  reads=["ktl", "qtl"], writes=["ps%d" % B_at])
            A("dve", lambda e: e.tensor_tensor(out=attn_sb[:].rearrange("p h n -> p (h n)"), in0=ps[B_at][:, :], in1=mask_T4(c), op=ALU.mult),
              reads=["ps%d" % B_at, "cm%d" % c], writes=["attn_sb"])
            yield

        skeys = []
        par = [0]

        def state_update(i):
            cur = par[0]
            skeys.append((Sbf[cur], "Sbf%d" % cur))
            sbank = lambda h: B_st if h < 2 else B_c
            for h in range(4):
                hh = h % 2
                bk_ = sbank(h)
                A("pe", lambda e, i=i, h=h, hh=hh, bk_=bk_: e.matmul(ps[bk_][:, hh * 256:(hh + 1) * 256], lhsT=khat[i][:, h * 128:(h + 1) * 128],
                                                                      rhs=vb[:, t, h * 256:(h + 1) * 256], start=True, stop=True),
                  reads=["khat%d" % i, "vb"], writes=["ps%d" % bk_])
            for h in range(4):
                hh = h % 2
                bk_ = sbank(h)
                col = h * 128 + (i + 1) * c - 1
                A("dve", lambda e, h=h, hh=hh, col=col, bk_=bk_: e.scalar_tensor_tensor(out=S[:, h, :], in0=S[:, h, :], scalar=eb[:, col:col + 1],
                                                                                         in1=ps[bk_][:, hh * 256:(hh + 1) * 256], op0=ALU.mult, op1=ALU.add),
                  reads=["S", "eb", "ps%d" % bk_], writes=["S"])
            if not state_only:
                nxt = 1 - cur
                A("act", lambda e, nxt=nxt: e.copy(out=Sbf[nxt][:], in_=S[:]), reads=["S"], writes=["Sbf%d" % nxt])
                par[0] = nxt

        def sample_states():
            for b in range(4):
                sl = fsb[:, b % 2, :].rearrange("p (h v) -> p h v", h=4)
                slk = "fsb%d" % (b % 2)
                A("sp", lambda e, b=b, sl=sl: e.dma_start(out=sl, in_=sg[b].rearrange("h d v -> d h v")), writes=[slk], dma=True)
                for hp in range(2):
                    for hh in range(2):
                        h = hp * 2 + hh
                        A("pe", lambda e, b=b, h=h, hh=hh: e.matmul(ps[B_st][:, hh * 256:(hh + 1) * 256], lhsT=khat[b][:, h * 128:(h + 1) * 128],
                                                                    rhs=vb[:, t, h * 256:(h + 1) * 256], start=True, stop=True),
                          reads=["khat%d" % b, "vb"], writes=["ps%d" % B_st])
                    for hh in range(2):
                        h = hp * 2 + hh
                        col = h * 128 + (b + 1) * c - 1
                        A("dve", lambda e, h=h, hh=hh, col=col, sl=sl: e.scalar_tensor_tensor(out=sl[:, h, :], in0=sl[:, h, :], scalar=eb[:, col:col + 1],
                                                                                               in1=ps[B_st][:, hh * 256:(hh + 1) * 256], op0=ALU.mult, op1=ALU.add),
                          reads=[slk, "eb", "ps%d" % B_st], writes=[slk])
                A("act", lambda e, b=b: e.dma_start(out=sso[b], in_=fsb[:, b % 2, :]), reads=[slk], dma=True,
                  is_out=True, semkey=slk + "o")

        if state_only:
            for i in range(nch):
                state_update(i)
            return
        if sample:
            sample_states()
        else:
            assert nch == 2
            state_update(0)
            skeys.append((Sbf[par[0]], "Sbf%d" % par[0]))
        yield

        obank = lambda h: B_c if h < 2 else B_st
        for h in range(4):
            hh = h % 2
            ob_ = obank(h)
            A("pe", lambda e, h=h, hh=hh, ob_=ob_: e.matmul(ps[ob_][:, hh * 256:(hh + 1) * 256], lhsT=attn_sb[:, h, :], rhs=vb[:, t, h * 256:(h + 1) * 256],
                                                             start=True, stop=False),
              reads=["attn_sb", "vb"], writes=["ps%d" % ob_])
            for i in range(nch):
                if sample:
                    rhs_ap = Sbs[:, i * 4 + h, :]
                    rk = "aT"
                else:
                    rhs_ap = skeys[i][0][:, h, :]
                    rk = skeys[i][1]
                A("pe", lambda e, h=h, hh=hh, i=i, rhs_ap=rhs_ap, ob_=ob_: e.matmul(ps[ob_][:, hh * 256:(hh + 1) * 256], lhsT=qz[i][:, h, :], rhs=rhs_ap,
                                                                                   start=False, stop=(i == nch - 1)),
                  reads=["qz%d" % i, rk], writes=["ps%d" % ob_])
        yield
        for h in range(4):
            hh = h % 2
            ob_ = obank(h)
            A("act", lambda e, h=h, hh=hh, ob_=ob_: e.activation(out=junk[:, 0:256], in_=ps[ob_][:, hh * 256:(hh + 1) * 256], func=AF.Square,
                                                                 accum_out=ss[:, h:h + 1]),
              reads=["ps%d" % ob_], writes=["junk", "ss"])
        A("act", lambda e: e.activation(out=rstd[:, 0:4], in_=ss[:, 0:4], func=AF.Ln, scale=1.0 / 256, bias=EPS), reads=["ss"], writes=["rstd"])
        A("act", lambda e: e.activation(out=rstd[:, 0:4], in_=rstd[:, 0:4], func=AF.Exp, scale=-0.5), reads=["rstd"], writes=["rstd"])
        for h in range(4):
            hh = h % 2
            ob_ = obank(h)
            A("dve", lambda e, h=h, hh=hh, ob_=ob_: e.scalar_tensor_tensor(out=ob_tok[:, h * 256:(h + 1) * 256], in0=ps[ob_][:, hh * 256:(hh + 1) * 256],
                                                                            scalar=rstd[:, h:h + 1], in1=rbs[:, t, h * 256:(h + 1) * 256],
                                                                            op0=ALU.mult, op1=ALU.mult),
              reads=["ps%d" % ob_, "rstd", "rbs"], writes=["ob_tok"])
        yield
        if not sample:
            skeys.pop()
            state_update(1)
            skeys.pop()
        yield
        transpose_to(ob_tok, "ob_tok", 8, obT, "obT", tok0)

    def attn_prompt_tile(t, a):
        tok0 = t * 128
        jA = {0: 0, 1: 1, 4: 2}
        jB = {2: 0, 3: 1}

        def scores(h):
            par = (h // 2) % 2
            blk, pr = h // 2, (h % 2) * 64
            bA, bB = (2, 3) if (h // 2) % 2 == 0 else (0, 1)
            for d in range(5):
                s = (a - d) % 8
                if d in jA:
                    bank, j, bk = ps[bA], jA[d], "ps%d" % bA
                else:
                    bank, j, bk = ps[bB], jB[d], "ps%d" % bB
                A("pe", lambda e, bank=bank, j=j, s=s: e.matmul(bank[:, j * 128:(j + 1) * 128], lhsT=kTa[pr:pr + 64, blk, s * 128:(s + 1) * 128],
                                                               rhs=qTa[pr:pr + 64, blk, tok0:tok0 + 128], start=True, stop=True),
                  reads=["kT%d" % s, "qTa"], writes=[bk])
            A("dve", lambda e: e.scalar_tensor_tensor(out=stmp2[par][:].rearrange("p a b -> p (a b)"), in0=ps[bA][:, 0:384], scalar=0.125,
                                                      in1=biasP_sb[:, h, :], op0=ALU.mult, op1=ALU.add),
              reads=["ps%d" % bA, "biasP"], writes=["stmp%d" % par])
            A("act", lambda e: e.activation(out=expT2[par][:, 0:3, :].rearrange("p a b -> p (a b)"), in_=stmp2[par][:].rearrange("p a b -> p (a b)"),
                                            func=AF.Exp), reads=["stmp%d" % par], writes=["expT%d" % par])
            A("act", lambda e: e.activation(out=expT2[par][:, 3:5, :].rearrange("p a b -> p (a b)"), in_=ps[bB][:, 0:256], func=AF.Exp,
                                            scale=0.125, bias=cbias_sb[:, h:h + 1]), reads=["ps%d" % bB, "cbias"], writes=["expT%d" % par])

        def pv(h, slot, grp):
            par = (h // 2) % 2
            pb = 5
            order = [(0, 0), (1, 1), (4, 2), (2, 3), (3, 4)]
            for n, (d, j) in enumerate(order):
                s = (a - d) % 8
                A("pe", lambda e, j=j, s=s, n=n: e.matmul(ps[pb][:, slot * 65:(slot + 1) * 65], lhsT=expT2[par][:, j, :],
                                                          rhs=vA[:, s * 8 + h, :], start=(n == 0), stop=(n == 4)),
                  reads=["expT%d" % par, "vA%d" % s], writes=["ps%d" % pb])
            if slot == 3:
                gi = 0 if grp[0] == 0 else 1
                pv3 = ps[pb][:, 0:260].rearrange("p (h n) -> p h n", h=4)
                A("dve", lambda e: e.reciprocal(out=rden[:, gi * 4:gi * 4 + 4], in_=pv3[:, :, 64]), reads=["ps%d" % pb], writes=["rden%d" % gi])
                for k, h2 in enumerate(grp):
                    A("dve", lambda e, h2=h2, k=k: e.tensor_scalar(out=oa_tok[:, h2 * 64:(h2 + 1) * 64], in0=ps[pb][:, k * 65:k * 65 + 64],
                                                                   scalar1=rden[:, gi * 4 + k:gi * 4 + k + 1], scalar2=None, op0=ALU.mult),
                      reads=["ps%d" % pb, "rden%d" % gi], writes=["oa_tok"])

        horder = [0, 2, 4, 6, 1, 3, 5, 7]
        scores(horder[0])
        yield
        for i_, h in enumerate(horder):
            if i_ + 1 < 8:
                scores(horder[i_ + 1])
                yield
            pv(h, i_ % 4, horder[(i_ // 4) * 4:(i_ // 4) * 4 + 4])
            yield
        transpose_to(oa_tok, "oa_tok", 4, oaT, "oaT", tok0)

    def attn_sample():
        vc2 = xg[:, 2:4, :].rearrange("p a b -> p (a b)").bitcast(BF16)[:, 0:2080].rearrange("p (n d) -> p n d", d=65)
        vcb = [vc, vc2]
        vck = [["vc"], ["xg2", "xg3"]]
        A("dve", lambda e: e.memset(vc2[:, :, 64:65], 1.0), writes=["xg2", "xg3"])

        def load_k(b):
            A("pool", lambda e: e.dma_start(out=kc_tok, in_=ck[b].rearrange("(t p) f -> p t f", p=128)), writes=["poolq", "tmpf"], dma=True, semkey="poolq")

        def load_v(b):
            for kt in range(4):
                A("pool", lambda e, kt=kt: e.dma_start(out=vcb[b % 2][:, kt * 8:(kt + 1) * 8, 0:64],
                                                       in_=cv[b][kt * 128:(kt + 1) * 128, :].rearrange("p (h d) -> p h d", h=8)),
                  writes=["poolq"] + vck[b % 2], dma=True, semkey="poolq")

        load_k(0)
        load_v(0)
        for b in range(4):
            vcur = vcb[b % 2]
            vkeys = vck[b % 2]
            for blk in range(4):
                for kt in range(4):
                    A("pe", lambda e, blk=blk, kt=kt: e.transpose(out=pst[:, kt, :], in_=kc_tok[:, kt, blk * 128:(blk + 1) * 128], identity=ident[:]),
                      reads=["tmpf", "ident"], writes=["pst"])
                A("act", lambda e, blk=blk: e.copy(out=kTc[:, blk, :], in_=pst[:, 0:4, :].rearrange("p a b -> p (a b)")),
                  reads=["pst"], writes=["obT"])
            if b + 1 < 4:
                load_k(b + 1)
                load_v(b + 1)
            def s_scores(h, b=b):
                blk, pr = h // 2, (h % 2) * 64
                par = (h // 2) % 2
                bB, bA = (3, 2) if par == 0 else (1, 0)
                for kt in range(3):
                    A("pe", lambda e, kt=kt: e.matmul(ps[bB][:, kt * 32:(kt + 1) * 32], lhsT=kTc[pr:pr + 64, blk, kt * 128:(kt + 1) * 128],
                                                      rhs=qTa[pr:pr + 64, blk, b * 32:(b + 1) * 32], start=True, stop=True),
                      reads=["obT", "qTa"], writes=["ps%d" % bB])
                A("pe", lambda e: e.matmul(ps[bA][:, 0:32], lhsT=kTc[pr:pr + 64, blk, 384:512],
                                           rhs=qTa[pr:pr + 64, blk, b * 32:(b + 1) * 32], start=True, stop=True),
                  reads=["obT", "qTa"], writes=["ps%d" % bA])
                A("pe", lambda e: e.matmul(ps[bA][0:32, 32:64], lhsT=kTa[pr:pr + 64, blk, b * 32:(b + 1) * 32],
                                           rhs=qTa[pr:pr + 64, blk, b * 32:(b + 1) * 32], start=True, stop=True),
                  reads=["kT0", "qTa"], writes=["ps%d" % bA])
                ex, st_ = expTs2[par], stmps2[par]
                A("act", lambda e: e.activation(out=ex[:, 0:3, :].rearrange("p a b -> p (a b)"), in_=ps[bB][:, 0:96], func=AF.Exp,
                                                scale=0.125, bias=cbias_sb[:, h:h + 1]),
                  reads=["ps%d" % bB, "cbias"], writes=["expTs%d" % par])
                A("dve", lambda e: e.scalar_tensor_tensor(out=st_[:, 0, :], in0=ps[bA][:, 0:32], scalar=0.125, in1=biasS_sb[:, h, 0:32],
                                                          op0=ALU.mult, op1=ALU.add),
                  reads=["ps%d" % bA, "biasS"], writes=["stmps%d" % par])
                A("dve", lambda e: e.scalar_tensor_tensor(out=st_[0:32, 1, :], in0=ps[bA][0:32, 32:64], scalar=0.125, in1=biasS_sb[0:32, h, 32:64],
                                                          op0=ALU.mult, op1=ALU.add),
                  reads=["ps%d" % bA, "biasS"], writes=["stmps%d" % par])
                A("act", lambda e: e.activation(out=ex[:, 3, :], in_=st_[:, 0, :], func=AF.Exp), reads=["stmps%d" % par], writes=["expTs%d" % par])
                A("act", lambda e: e.activation(out=ex[0:32, 4, :], in_=st_[0:32, 1, :], func=AF.Exp), reads=["stmps%d" % par], writes=["expTs%d" % par])

            def s_pv(h, slot, grp, b=b, vcur=vcur, vkeys=vkeys):
                par = (h // 2) % 2
                ex = expTs2[par]
                for kt in range(4):
                    A("pe", lambda e, kt=kt: e.matmul(ps[5][0:32, slot * 65:slot * 65 + 65], lhsT=ex[:, kt, :], rhs=vcur[:, kt * 8 + h, :],
                                                      start=(kt == 0), stop=False),
                      reads=["expTs%d" % par] + vkeys, writes=["ps5"])
                A("pe", lambda e: e.matmul(ps[5][0:32, slot * 65:slot * 65 + 65], lhsT=ex[0:32, 4, :], rhs=vnew[0:32, b * 8 + h, :],
                                           start=False, stop=True),
                  reads=["expTs%d" % par, "aT"], writes=["ps5"])
                if slot == 3:
                    pv3 = ps[5][0:32, 0:260].rearrange("p (h n) -> p h n", h=4)
                    A("dve", lambda e: e.reciprocal(out=rdens[:, 0:4], in_=pv3[:, :, 64]), reads=["ps5"], writes=["rdens"])
                    for k, h2 in enumerate(grp):
                        A("dve", lambda e, h2=h2, k=k: e.tensor_scalar(out=oas[:, b, h2 * 64:(h2 + 1) * 64], in0=ps[5][0:32, k * 65:k * 65 + 64],
                                                                       scalar1=rdens[:, k:k + 1], scalar2=None, op0=ALU.mult),
                          reads=["ps5", "rdens"], writes=["aT"])

            horder = [0, 2, 4, 6, 1, 3, 5, 7]
            s_scores(horder[0])
            for i_, h in enumerate(horder):
                if i_ + 1 < 8:
                    s_scores(horder[i_ + 1])
                s_pv(h, i_ % 4, horder[(i_ // 4) * 4:(i_ // 4) * 4 + 4])
        for b in range(4):
            for blk in range(4):
                A("pe", lambda e, b=b, blk=blk: e.transpose(out=pst[:, blk, 0:32], in_=oas[:, b, blk * 128:(blk + 1) * 128], identity=ident[0:32, 0:32]),
                  reads=["aT", "ident"], writes=["pst"])
            A("act", lambda e, b=b: e.copy(out=oaT[:, 0:4, b * 32:(b + 1) * 32], in_=pst[:, 0:4, 0:32]), reads=["pst"], writes=["oaT"])

    def prep_group(xsrc, row0, ntiles, xpar):
        for t in range(ntiles):
            xi = xpar * G + t
            A("sp", lambda e, t=t, xi=xi: e.dma_start(out=xg[:, xi, :], in_=xsrc[row0 + t * 128:row0 + (t + 1) * 128, :]), writes=["xg%d" % xi], dma=True)
            norm_to_hT(xg[:, xi, :], "xg%d" % xi, gpre_sb, "gpre", t * 128)

    def run_group(kind, xsrc, row0, ntiles, a0=None, out_ap=None, kv_out=None, xpar=0, prefetched=False, next_prep=None):
        Tg = ntiles * 128
        c = 32 if kind == "sample" else 64
        if not prefetched:
            prep_group(xsrc, row0, ntiles, xpar)

        def ev_copy(dst, dkey, scale=None, func=AF.Copy, rows=128):
            def f(bank, bkey, ob):
                if scale is None:
                    A("act", lambda e: e.activation(out=dst(ob), in_=bank[0:rows, 0:Tg], func=func), reads=[bkey], writes=[dkey(ob)])
                else:
                    A("act", lambda e: e.activation(out=dst(ob), in_=bank[0:rows, 0:Tg], func=func, scale=scale), reads=[bkey], writes=[dkey(ob)])
            return f

        full = kind in ("prompt", "sample")
        if full:
            s = load_w("w_in", 0, 8, QA, 512)
            bstyle(s, 8, 4, hT, "hT", Tg, ev_copy(lambda ob: qTa[:, ob, 0:Tg], lambda ob: "qTa"))
        if kind != "pre":
            s = load_w("w_in", 0, 8, KA, 512)
            if kind == "sample":
                bstyle(s, 8, 4, hT, "hT", Tg, ev_copy(lambda ob: kTa[:, ob, 0:128], lambda ob: "kT0"))
            else:
                s0 = a0 % 8
                def kdst(ob):
                    return kTa[:, ob, s0 * 128:s0 * 128 + Tg]
                def kev(bank, bkey, ob):
                    A("act", lambda e: e.copy(out=kdst(ob), in_=bank[:, 0:Tg]), reads=[bkey], writes=["kT%d" % ((a0 + i) % 8) for i in range(ntiles)])
                bstyle(s, 8, 4, hT, "hT", Tg, kev)
            if kv_out is not None:
                for t in range(ntiles):
                    def kout(bank, bkey, t=t):
                        st = kvst[t % 2]
                        sk = "kvst%d" % (t % 2)
                        A("act", lambda e: e.copy(out=st[:], in_=bank[:, 0:512]), reads=[bkey], writes=[sk])
                        A("act", lambda e: e.dma_start(out=kv_out[0][kv_out[2] + t * 128:kv_out[2] + (t + 1) * 128, :], in_=st[:]), reads=[sk], dma=True,
                          is_out=True, semkey=sk + "o")
                    astyle_tile(s, 8, 512, hT, "hT", t * 128, 128, kout)
            s = load_w("w_in", 0, 8, VA, 512)
            for t in range(ntiles):
                slot_v = 0 if kind == "sample" else (a0 + t) % 8
                def vev(bank, bkey, t=t, slot_v=slot_v):
                    if kind != "sample":
                        A("act", lambda e: e.copy(out=vA[:, slot_v * 8:(slot_v + 1) * 8, 0:64], in_=bank[:, 0:512].rearrange("p (h d) -> p h d", h=8)),
                          reads=[bkey], writes=["vA%d" % slot_v])
                        if kind == "halo":
                            A("act", lambda e: e.activation(out=vA[:, slot_v * 8:(slot_v + 1) * 8, 64], in_=ones8[:], func=AF.Copy, scale=valid_sb[:, 0:1]),
                              reads=["ones8", "valid"], writes=["vA%d" % slot_v])
                        else:
                            A("act", lambda e: e.copy(out=vA[:, slot_v * 8:(slot_v + 1) * 8, 64], in_=ones8[:]), reads=["ones8"], writes=["vA%d" % slot_v])
                    if kv_out is not None:
                        st = kvst[t % 2]
                        sk = "kvst%d" % (t % 2)
                        A("act", lambda e: e.copy(out=st[:], in_=bank[:, 0:512]), reads=[bkey], writes=[sk])
                        A("act", lambda e: e.dma_start(out=kv_out[1][kv_out[2] + t * 128:kv_out[2] + (t + 1) * 128, :], in_=st[:]), reads=[sk], dma=True,
                          is_out=True, semkey=sk + "o")
                astyle_tile(s, 8, 512, hT, "hT", t * 128, 128, vev)
            if kind == "sample":
                A("dve", lambda e: e.memset(vnew[:, :, 64:65], 1.0), writes=["aT"])
                for b in range(4):
                    def vnev(bank, bkey, b=b):
                        A("act", lambda e: e.copy(out=vnew[:, b * 8:(b + 1) * 8, 0:64], in_=bank[0:32, 0:512].rearrange("p (h d) -> p h d", h=8)),
                          reads=[bkey], writes=["aT"])
                    astyle_tile(s, 8, 512, hT, "hT", b * 32, 32, vnev)
        if kind == "halo":
            return
        if full:
            s = load_w("w_in", 0, 8, QB, 512)
            bstyle(s, 8, 4, hT, "hT", Tg, ev_copy(lambda ob: qTb[:, ob, 0:Tg], lambda ob: "qTb", scale=128.0 ** -0.5))
        s = load_w("w_in", 0, 8, KB, 512)
        if full:
            bstyle(s, 8, 4, hT, "hT", Tg, ev_copy(lambda ob: kTb[:, ob, 0:Tg], lambda ob: "kTb"))
        for t in range(ntiles):
            def kbev(bank, bkey, t=t):
                A("act", lambda e: e.copy(out=kbt[:, t, :], in_=bank[:, 0:512]), reads=[bkey], writes=["kbt"])
            astyle_tile(s, 8, 512, hT, "hT", t * 128, 128, kbev)
        for half in range(2):
            s = load_w("w_in", 0, 8, VB + half * 512, 512)
            for t in range(ntiles):
                def vbev(bank, bkey, t=t, half=half):
                    A("act", lambda e: e.copy(out=vb[:, t, half * 512:(half + 1) * 512], in_=bank[:, 0:512]), reads=[bkey], writes=["vb"])
                astyle_tile(s, 8, 512, hT, "hT", t * 128, 128, vbev)
        if full:
            for half in range(2):
                s = load_w("w_in", 0, 8, RB + half * 512, 512)
                for t in range(ntiles):
                    def rbev(bank, bkey, t=t, half=half):
                        A("act", lambda e: e.activation(out=rbs[:, t, half * 512:(half + 1) * 512], in_=bank[:, 0:512], func=AF.Silu),
                          reads=[bkey], writes=["rbs"])
                        for j in range(2):
                            c0 = half * 512 + j * 256
                            A("dve", lambda e, c0=c0: e.tensor_tensor(out=rbs[:, t, c0:c0 + 256], in0=rbs[:, t, c0:c0 + 256], in1=gn_sb[:], op=ALU.mult),
                              reads=["rbs", "gn"], writes=["rbs"])
                    astyle_tile(s, 8, 512, hT, "hT", t * 128, 128, rbev)
        s = load_w("w_in", 0, 8, DLR, 16)
        def dlev(bank, bkey, ob):
            A("act", lambda e: e.copy(out=dlrT[0:16, 0:Tg], in_=bank[0:16, 0:Tg]), reads=[bkey], writes=["dlrT"])
        bstyle(s, 8, 1, hT, "hT", Tg, dlev, mrows=16)
        if full:
            for half in range(2):
                s = load_w("w_in", 0, 8, GA + half * 512, 512)
                bstyle(s, 8, 4, hT, "hT", Tg, ev_copy(lambda ob, half=half: gaT[:, half * 4 + ob, 0:Tg], lambda ob: "gaT", func=AF.Sigmoid))
            for half in range(2):
                s = load_w("w_in", 0, 8, GB + half * 512, 512)
                bstyle(s, 8, 4, hT, "hT", Tg, ev_copy(lambda ob, half=half: gbT[:, half * 4 + ob, 0:Tg], lambda ob: "gbT", func=AF.Sigmoid))

        if kind == "pre":
            for t in range(ntiles):
                for _ in gla_tile(t, 64, True):
                    pass
            return
        if kind == "prompt":
            ibk = {"la": 4, "laT": 6, "c": 4, "at": 6, "st": 6}
            gens = []
            for t in range(ntiles):
                gens += [attn_prompt_tile(t, a0 + t), gla_tile(t, 64, False, bk=ibk)]
            att = [g_ for i_, g_ in enumerate(gens) if i_ % 2 == 0]
            gl = [g_ for i_, g_ in enumerate(gens) if i_ % 2 == 1]
            while att or gl:
                if att:
                    try:
                        next(att[0])
                    except StopIteration:
                        att.pop(0)
                if gl:
                    try:
                        next(gl[0])
                    except StopIteration:
                        gl.pop(0)
        else:
            attn_sample()
            dump("s_oaT", oaT[:, :, 0:128], [128, 4, 128], BF16, "oaT")
            dump("s_qTa", qTa[:, :, 0:128], [128, 4, 128], BF16, "qTa")
            for b in range(4):
                A("pool", lambda e, b=b: e.dma_start(out=Sbs[:, b * 4:(b + 1) * 4, :], in_=sg[b].rearrange("h d v -> d h v")),
                  writes=["poolq", "aT"], dma=True, semkey="poolq")
            for _ in gla_tile(0, 32, False, sample=True):
                pass
            dump("s_obT", obT[:, :, 0:128], [128, 8, 128], BF16, "obT")

        for half in range(2):
            s = load_w("wpa", 0, 4, half * 512, 512)
            def paev(bank, bkey, ob, half=half):
                A("dve", lambda e: e.tensor_tensor(out=gaT[:, half * 4 + ob, 0:Tg], in0=gaT[:, half * 4 + ob, 0:Tg], in1=bank[:, 0:Tg], op=ALU.mult),
                  reads=[bkey, "gaT"], writes=["gaT"])
            bstyle(s, 4, 4, oaT, "oaT", Tg, paev)
        for half in range(2):
            s = load_w("wpb", 0, 8, half * 512, 512)
            def pbev(bank, bkey, ob, half=half):
                A("dve", lambda e: e.tensor_tensor(out=gbT[:, half * 4 + ob, 0:Tg], in0=gbT[:, half * 4 + ob, 0:Tg], in1=bank[:, 0:Tg], op=ALU.mult),
                  reads=[bkey, "gbT"], writes=["gbT"])
                A("dve", lambda e: e.tensor_tensor(out=gaT[:, half * 4 + ob, 0:Tg], in0=gaT[:, half * 4 + ob, 0:Tg], in1=gbT[:, half * 4 + ob, 0:Tg], op=ALU.add),
                  reads=["gbT", "gaT"], writes=["gaT"])
            bstyle(s, 8, 4, obT, "obT", Tg, pbev)

        def a_into_fsb(wname, nk_total, src, srckey):
            kgs = []
            k = 0
            while k < nk_total:
                kgs.append((k, min(8, nk_total - k)))
                k += 8
            for cb in range(2):
                for gi, (k0, nk) in enumerate(kgs):
                    s = load_w(wname, k0, nk, cb * 512, 512)
                    for t in range(ntiles):
                        def fev(bank, bkey, t=t, cb=cb, gi=gi):
                            if gi == 0:
                                A("act", lambda e: e.copy(out=fsb[:, t, cb * 512:(cb + 1) * 512], in_=bank[:, 0:512]), reads=[bkey], writes=["fsb%d" % t])
                            else:
                                A("dve", lambda e: e.tensor_tensor(out=fsb[:, t, cb * 512:(cb + 1) * 512], in0=fsb[:, t, cb * 512:(cb + 1) * 512],
                                                                   in1=bank[:, 0:512], op=ALU.add), reads=[bkey, "fsb%d" % t], writes=["fsb%d" % t])
                        astyle_tile(s, nk, 512, src, srckey, t * 128, 128, fev, kc_off=k0)

        def norm_residual(t, g_sb, gkey, dst, dkey):
            A("act", lambda e: e.activation(out=junk[:], in_=fsb[:, t, :], func=AF.Square, accum_out=ss[:, 0:1]), reads=["fsb%d" % t], writes=["junk", "ss"])
            A("act", lambda e: e.activation(out=rstd[:, 0:1], in_=ss[:, 0:1], func=AF.Ln, scale=1.0 / D, bias=EPS), reads=["ss"], writes=["rstd"])
            A("act", lambda e: e.activation(out=rstd[:, 0:1], in_=rstd[:, 0:1], func=AF.Exp, scale=-0.5), reads=["rstd"], writes=["rstd"])
            A("dve", lambda e: e.scalar_tensor_tensor(out=tmpf[:], in0=fsb[:, t, :], scalar=rstd[:, 0:1], in1=g_sb[:], op0=ALU.mult, op1=ALU.mult),
              reads=["fsb%d" % t, "rstd", gkey], writes=["tmpf"])
            A("dve", lambda e: e.tensor_tensor(out=dst, in0=tmpf[:], in1=xg[:, xpar * G + t, :], op=ALU.add), reads=["tmpf", "xg%d" % (xpar * G + t)],
              writes=[dkey])

        if kind == "sample":
            dump("s_mixT", gaT[:, :, 0:128], [128, 8, 128], BF16, "gaT")
        a_into_fsb("wout", 8, gaT, "gaT")
        TT = list(range(ntiles))
        xi_ = lambda t: xpar * G + t
        for t in TT:
            A("act", lambda e, t=t: e.activation(out=junk[:], in_=fsb[:, t, :], func=AF.Square, accum_out=ssn[:, t:t + 1]),
              reads=["fsb%d" % t], writes=["junk", "ssn%d" % t])
        for t in TT:
            A("act", lambda e, t=t: e.activation(out=rstdn[:, t:t + 1], in_=ssn[:, t:t + 1], func=AF.Ln, scale=1.0 / D, bias=EPS),
              reads=["ssn%d" % t], writes=["rstdn%d" % t])
        for t in TT:
            A("act", lambda e, t=t: e.activation(out=rstdn[:, t:t + 1], in_=rstdn[:, t:t + 1], func=AF.Exp, scale=-0.5),
              reads=["rstdn%d" % t], writes=["rstdn%d" % t])
        for t in TT:
            A("dve", lambda e, t=t: e.scalar_tensor_tensor(out=fsb[:, t, :], in0=fsb[:, t, :], scalar=rstdn[:, t:t + 1], in1=gpost_sb[:], op0=ALU.mult, op1=ALU.mult),
              reads=["fsb%d" % t, "rstdn%d" % t, "gpost"], writes=["fsb%d" % t])
        for t in TT:
            A("dve", lambda e, t=t: e.tensor_tensor(out=xg[:, xi_(t), :], in0=fsb[:, t, :], in1=xg[:, xi_(t), :], op=ALU.add),
              reads=["fsb%d" % t, "xg%d" % xi_(t)], writes=["xg%d" % xi_(t)])
        for t in TT:
            A("act", lambda e, t=t: e.activation(out=junk[:], in_=xg[:, xi_(t), :], func=AF.Square, accum_out=ssn[:, 2 + t:3 + t]),
              reads=["xg%d" % xi_(t)], writes=["junk", "ssn%d" % (2 + t)])
        for t in TT:
            A("act", lambda e, t=t: e.activation(out=rstdn[:, 2 + t:3 + t], in_=ssn[:, 2 + t:3 + t], func=AF.Ln, scale=1.0 / D, bias=EPS),
              reads=["ssn%d" % (2 + t)], writes=["rstdn%d" % (2 + t)])
        for t in TT:
            A("act", lambda e, t=t: e.activation(out=rstdn[:, 2 + t:3 + t], in_=rstdn[:, 2 + t:3 + t], func=AF.Exp, scale=-0.5),
              reads=["rstdn%d" % (2 + t)], writes=["rstdn%d" % (2 + t)])
        for t in TT:
            hbt = hb if t == 0 else hb2
            A("dve", lambda e, t=t, hbt=hbt: e.tensor_scalar(out=hbt[:], in0=xg[:, xi_(t), :], scalar1=rstdn[:, 2 + t:3 + t], scalar2=None, op0=ALU.mult),
              reads=["xg%d" % xi_(t), "rstdn%d" % (2 + t)], writes=["hb" if t == 0 else "hb2"])
        for t in TT:
            hbt = hb if t == 0 else hb2
            hk = "hb" if t == 0 else "hb2"
            for kc in range(8):
                A("pe", lambda e, kc=kc, hbt=hbt: e.transpose(out=pst[:, kc, :], in_=hbt[:, kc * 128:(kc + 1) * 128], identity=ident[:]),
                  reads=[hk, "ident"], writes=["pst"])
            for kc in range(8):
                A("act", lambda e, kc=kc, t=t: e.activation(out=hT[:, kc, t * 128:(t + 1) * 128], in_=pst[:, kc, :], func=AF.Copy,
                                                            scale=gffn_sb[:, kc:kc + 1]), reads=["pst", "gffn"], writes=["hT"])

        for it_, fb4 in enumerate(range(0, 22, 4)):
            nb = min(4, 22 - fb4)
            sg_ = load_w("wg", 0, 8, fb4 * 128, nb * 128, slot=(2 * it_) % 4)
            su_ = load_w("wu", 0, 8, fb4 * 128, nb * 128, slot=(2 * it_ + 1) % 4)
            for ob in range(nb):
                bg = 0 if ob % 2 == 0 else 2
                bu = 1 if ob % 2 == 0 else 3
                for kc in range(8):
                    A("pe", lambda e, kc=kc, ob=ob, sg_=sg_, bg=bg: e.matmul(ps[bg][:, 0:Tg], lhsT=wblk[sg_][:, kc, ob * 128:(ob + 1) * 128], rhs=hT[:, kc, 0:Tg],
                                                            start=(kc == 0), stop=(kc == 7)), reads=slotkeys[sg_] + ["hT"], writes=["ps%d" % bg])
                for kc in range(8):
                    A("pe", lambda e, kc=kc, ob=ob, su_=su_, bu=bu: e.matmul(ps[bu][:, 0:Tg], lhsT=wblk[su_][:, kc, ob * 128:(ob + 1) * 128], rhs=hT[:, kc, 0:Tg],
                                                            start=(kc == 0), stop=(kc == 7)), reads=slotkeys[su_] + ["hT"], writes=["ps%d" % bu])
                A("act", lambda e, bg=bg: e.activation(out=junk[:, 0:Tg], in_=ps[bg][:, 0:Tg], func=AF.Silu), reads=["ps%d" % bg], writes=["junk"])
                A("dve", lambda e, ob=ob, fb4=fb4, bu=bu: e.tensor_tensor(out=aT[:, fb4 + ob, 0:Tg], in0=junk[:, 0:Tg], in1=ps[bu][:, 0:Tg], op=ALU.mult),
                  reads=["junk", "ps%d" % bu], writes=["aT"])
        if next_prep is not None:
            next_prep()
        a_into_fsb("wd", 22, aT, "aT")
        for t in range(ntiles):
            yst = ystage[t % 2]
            yk = "yst%d" % (t % 2)
            norm_residual(t, gfpost_sb, "gfpost", yst[:, 0:D], yk)
            A("act", lambda e, t=t, yst=yst: e.dma_start(out=out_ap[row0 + t * 128:row0 + (t + 1) * 128, :], in_=yst[:, 0:D]), reads=[yk], dma=True,
              is_out=True, semkey=yk + "o")

    A("sp", lambda e: e.dma_start(out=dlrT[16:32, :], in_=ones_row[:, :]), writes=["dlrT"], dma=True, semkey="dlr1")

    A("dve", lambda e: e.tensor_copy(out=wup_bf[:], in_=wup_sb[:]), reads=["wup"], writes=["wup_bf"])
    A("dve", lambda e: e.tensor_copy(out=utf_bf[:], in_=cm[64][:, 640:768]), reads=["cm64"], writes=["utf_bf"])
    A("dve", lambda e: e.memset(onesb[:], 1.0), writes=["onesb"])
    skb, sv0, sv1 = 0, 1, 2
    wctr[0] = 3

    xgf = xg[:].rearrange("p a b -> p (a b)")
    w_in3 = w_in.rearrange("(k p) n -> p k n", p=128)
    for slot_, c0_ in ((skb, KB), (sv0, VB), (sv1, VB + 512)):
        for half_ in range(2):
            A("sp", lambda e, c0_=c0_, half_=half_: e.dma_start(out=xgf[:, 0:2048].rearrange("p (k n) -> p k n", k=4),
                                                                  in_=w_in3[:, half_ * 4:(half_ + 1) * 4, c0_:c0_ + 512]),
              writes=["xg0", "xg1"], dma=True, semkey="xg0")
            for j in range(4):
                kc = half_ * 4 + j
                if j % 2 == 0:
                    A("act", lambda e, slot_=slot_, kc=kc, j=j: e.activation(out=wblk[slot_][:, kc, :], in_=xgf[:, j * 512:(j + 1) * 512], func=AF.Copy,
                                                                            scale=gpre_sb[:, kc:kc + 1]),
                      reads=["xg0", "xg1", "gpre"], writes=["w%d" % slot_])
                else:
                    A("dve", lambda e, slot_=slot_, kc=kc, j=j: e.tensor_scalar(out=wblk[slot_][:, kc, :], in0=xgf[:, j * 512:(j + 1) * 512],
                                                                                scalar1=gpre_sb[:, kc:kc + 1], scalar2=None, op0=ALU.mult),
                      reads=["xg0", "xg1", "gpre"], writes=["w%d" % slot_])
    A("sp", lambda e: e.dma_start(out=tmpf[:, 0:128].rearrange("p (k n) -> p k n", k=8), in_=w_in3[:, :, DLR:DLR + 16]), writes=["tmpf"], dma=True,
      semkey="tmpfw")
    for kc in range(8):
        A("dve", lambda e, kc=kc: e.tensor_scalar(out=wdlr[:, kc, :], in0=tmpf[:, kc * 16:(kc + 1) * 16], scalar1=gpre_sb[:, kc:kc + 1], scalar2=None,
                                                  op0=ALU.mult), reads=["tmpf", "gpre"], writes=["wdlr"])
    A("dve", lambda e: e.memset(ssp[:], 0.0), reads=["tmpf"], writes=["tmpf0", "tmpf1", "ssp0", "ssp1"])

    def pk(name, i, depth=2):
        return "pp_%s%d" % (name, i % depth)

    def pp_s0(i):
        p = i % 2
        xt = xg[:, p, :]
        A("sp", lambda e: e.dma_start(out=xt, in_=xpre[i * 128:(i + 1) * 128, :]), writes=["xg%d" % p], dma=True)
        A("act", lambda e: e.activation(out=junk[:], in_=xt, func=AF.Square, accum_out=ssp[:, p:p + 1]), reads=["xg%d" % p], writes=["junk", "ssp%d" % p])
        A("act", lambda e: e.activation(out=rstdp[:, p:p + 1], in_=ssp[:, p:p + 1], func=AF.Ln, scale=1.0 / D, bias=EPS),
          reads=["ssp%d" % p], writes=["rstdp%d" % p])
        A("act", lambda e: e.activation(out=rstdp[:, p:p + 1], in_=rstdp[:, p:p + 1], func=AF.Exp, scale=-0.5),
          reads=["rstdp%d" % p], writes=["rstdp%d" % p])
        A("dve", lambda e: e.tensor_scalar(out=pp_hb[p], in0=xt, scalar1=rstdp[:, p:p + 1], scalar2=None, op0=ALU.mult),
          reads=["xg%d" % p, "rstdp%d" % p], writes=[pk("hb", i)])

    def pp_s1(i):
        p = i % 2
        for kc in range(8):
            A("pe", lambda e, kc=kc: e.transpose(out=pst[:, kc, :], in_=pp_hb[p][:, kc * 128:(kc + 1) * 128], identity=ident[:]),
              reads=[pk("hb", i), "ident"], writes=["pst"])
        if p == 0:
            A("dve", lambda e: e.tensor_copy(out=pp_hT[p], in_=pst[:, :, :]), reads=["pst"], writes=[pk("hT", i)])
        else:
            A("act", lambda e: e.copy(out=pp_hT[p], in_=pst[:, :, :]), reads=["pst"], writes=[pk("hT", i)])

    def pp_s2(i):
        p = i % 2
        q4 = i % 4
        for (slot, bank) in ((skb, 0), (sv0, 1), (sv1, 2)):
            for kc in range(8):
                A("pe", lambda e, kc=kc, slot=slot, bank=bank: e.matmul(ps[bank][:, :], lhsT=pp_hT[p][:, kc, :], rhs=wblk[slot][:, kc, :],
                                                                        start=(kc == 0), stop=(kc == 7)),
                  reads=[pk("hT", i), "w%d" % slot], writes=["ps%d" % bank])
        A("act", lambda e: e.copy(out=pp_kbt[q4], in_=ps[0][:, :]), reads=["ps0"], writes=[pk("kbt", i, 4)])
        A("act", lambda e: e.copy(out=pp_vb[q4][:, 0:512], in_=ps[1][:, :]), reads=["ps1"], writes=[pk("vba", i, 4)])
        A("dve", lambda e: e.tensor_copy(out=pp_vb[q4][:, 512:1024], in_=ps[2][:, :]), reads=["ps2"], writes=[pk("vbb", i, 4)])

    def pp_s2b(i):
        p = i % 2
        for kc in range(8):
            A("pe", lambda e, kc=kc: e.matmul(ps[3][0:16, 0:128], lhsT=wdlr[:, kc, :], rhs=pp_hT[p][:, kc, :], start=(kc == 0), stop=(kc == 7)),
              reads=[pk("hT", i), "wdlr"], writes=["ps3"])
        A("act", lambda e: e.copy(out=dlrTb[0:16, p * 128:(p + 1) * 128], in_=ps[3][0:16, 0:128]), reads=["ps3"], writes=[pk("dl", i)])

    def pp_s3(i):
        p = i % 2
        A("pe", lambda e: e.matmul(ps[4][:, :], lhsT=dlrTb[0:32, p * 128:(p + 1) * 128], rhs=wup_bf[0:32, :], start=True, stop=True),
          reads=[pk("dl", i), "dlrTb", "wup_bf"], writes=["ps4"])
        A("act", lambda e: e.activation(out=tmpf[:, p * 512:(p + 1) * 512], in_=ps[4][:, :], func=AF.Exp, scale=-1.0), reads=["ps4"], writes=["tmpf%d" % p])
        A("act", lambda e: e.activation(out=pp_la[p], in_=tmpf[:, p * 512:(p + 1) * 512], func=AF.Ln, bias=1.0), reads=["tmpf%d" % p], writes=[pk("la", i)])

    def pp_s4(i):
        p = i % 2
        q4 = i % 4
        A("pe", lambda e: e.matmul(ps[5][:, :], lhsT=utf_bf[:], rhs=pp_la[p], start=True, stop=True), reads=[pk("la", i), "utf_bf"], writes=["ps5"])
        for h in range(4):
            A("pe", lambda e, h=h: e.matmul(ps[3][:, 256 + h * 2:258 + h * 2], lhsT=pp_la[p][:, h * 128:(h + 1) * 128], rhs=onesb[:, 0:2],
                                            start=True, stop=True), reads=[pk("la", i), "onesb"], writes=["ps3"])
        A("act", lambda e: e.activation(out=pp_ec[p], in_=ps[5][:, :], func=AF.Exp, scale=-1.0 / 16), reads=["ps5"], writes=[pk("ec", i)])
        A("act", lambda e: e.activation(out=etile[:, p, :], in_=ps[3][:, 256:264].rearrange("p (h n) -> p h n", n=2)[:, :, 0], func=AF.Exp,
                                        scale=-1.0 / 16), reads=["ps3"], writes=["etile%d" % p])
        A("dve", lambda e: e.tensor_tensor(out=pp_khat[p], in0=pp_kbt[q4], in1=pp_ec[p], op=ALU.mult), reads=[pk("kbt", i, 4), pk("ec", i)],
          writes=[pk("kh", i)])

    def pp_s5(i):
        p = i % 2
        q4 = i % 4
        for hp in range(2):
            bank = 6 if hp == 0 else 0
            for h in (2 * hp, 2 * hp + 1):
                A("pe", lambda e, h=h, bank=bank: e.matmul(ps[bank][:, (h % 2) * 256:(h % 2 + 1) * 256], lhsT=pp_khat[p][:, h * 128:(h + 1) * 128],
                                                           rhs=pp_vb[q4][:, h * 256:(h + 1) * 256], start=True, stop=True),
                  reads=[pk("kh", i), pk("vba", i, 4), pk("vbb", i, 4)], writes=["ps%d" % bank])
        for hp in range(2):
            bank = 6 if hp == 0 else 0
            for h in (2 * hp, 2 * hp + 1):
                A("dve", lambda e, h=h, bank=bank: e.scalar_tensor_tensor(out=S[:, h, :], in0=S[:, h, :], scalar=etile[:, p, h:h + 1],
                                                                          in1=ps[bank][:, (h % 2) * 256:(h % 2 + 1) * 256], op0=ALU.mult, op1=ALU.add),
                  reads=["S%d" % h, "etile%d" % p, "ps%d" % bank], writes=["S%d" % h])

    NPRE = 7 * NT_P
    for n in range(-2, NPRE + 3):
        for st, off in ((pp_s0, 2), (pp_s1, 1), (pp_s2, 0), (pp_s5, -3), (pp_s2b, 0), (pp_s3, -1), (pp_s4, -2)):
            i = n + off
            if 0 <= i < NPRE:
                st(i)
    A("act", lambda e: e.copy(out=Sbf[0][:], in_=S[:]), reads=["S%d" % h for h in range(4)] + ["S"], writes=["Sbf0"])
    ppkeys = ["S%d" % h for h in range(4)] + ["junk", "tmpf0", "tmpf1", "wdlr", "dlrTb", "wup_bf", "utf_bf", "onesb"]
    for p_ in range(4):
        ppkeys += ["pp_kbt%d" % p_, "pp_vba%d" % p_, "pp_vbb%d" % p_]
    for p_ in range(2):
        ppkeys += ["pp_hb%d" % p_, "pp_hT%d" % p_, "pp_la%d" % p_, "pp_ec%d" % p_, "pp_kh%d" % p_, "pp_dl%d" % p_, "etile%d" % p_, "ssp%d" % p_, "rstdp%d" % p_]
    A("dve", lambda e: e.memset(dlrT[0:16, :], 0.0), reads=ppkeys,
      writes=["dlrT", "aT", "obT", "gaT", "gbT", "hT", "tmpf", "fsb0", "fsb1", "S", "ps6", "pst", "ps0", "ps1", "ps2", "ps3", "ps4", "ps5", "xg0", "xg1"])
    for g in range(4 // G):
        run_group("halo", xh, g * T, G, a0=-4 + g * G)
    run_group("sample", xs, 0, 1, out_ap=ys, kv_out=(ksn, vsn, 0))
    for i in range(2):
        A("dve", lambda e, i=i: e.memset(qz[i][:], 0.0), writes=["qz%d" % i])
    ngroups = NT_P // G
    for g in range(ngroups):
        rbase = (g * G - (NT_P - 4)) * 128
        nxt = None
        if g + 1 < ngroups:
            nxt = (lambda g=g: prep_group(xp, (g + 1) * T, G, (g + 1) % 2))
        run_group("prompt", xp, g * T, G, a0=g * G, out_ap=yp, kv_out=(kp, vp, rbase) if rbase >= 0 else None,
                  xpar=g % 2, prefetched=(g > 0), next_prep=nxt)
    A("act", lambda e: e.dma_start(out=spo[:, :], in_=S[:].rearrange("p h v -> p (h v)")), reads=["S"], dma=True, is_out=True, semkey="spo")
    P.finish()
    P.emit()
    return nc, P


def _const_masks(c):
    n = 128
    s = np.arange(n)[:, None]
    t = np.arange(n)[None, :]
    same = (s // c) == (t // c)
    UT = ((s > t) & same).astype(np.float32)
    maskT = ((s <= t) & same).astype(np.float32)
    restart = np.ones((128, 4, 128), np.float32)
    restart[:, :, ::c] = 0.0
    nch = n // c
    rm = np.zeros((128, nch), np.float32)
    for i in range(nch):
        rm[i * c:(i + 1) * c, i] = 1.0
    out = np.zeros((128, 6 * 128 + 512 + nch), np.float32)
    out[:, 0:128] = UT
    out[:, 128:640] = np.tile(maskT, (1, 4))
    out[:, 640:768] = (s > t).astype(np.float32)
    out[:, 768:1280] = restart.reshape(128, 512)
    out[:, 1280:1280 + nch] = rm
    return out


def _bias_tables(rel):
    p = np.arange(128)[:, None]
    col = np.arange(128)[None, :]
    kc, kl = p // 64, p % 64
    qc, ql = col // 64, col % 64
    idxP = np.zeros((3, 128, 128), np.int64)
    mk = np.zeros((128, 3, 128), np.float32)
    for j, d in enumerate((0, 1, 4)):
        o = 2 * d + qc - kc
        dist = 64 * o + ql - kl
        idxP[j] = np.clip(dist, -128, 128) + 128
        mk[:, j, :] = np.where((o >= 0) & (o <= 8), 0.0, NEG)
    biasP = rel[:, idxP]
    biasP = np.ascontiguousarray(biasP.transpose(2, 0, 1, 3)).reshape(128, 8 * 384)
    q32 = np.arange(32)[None, :]
    d0 = q32 + 512 - (384 + p)
    d1 = q32 - np.minimum(p, 31)
    idxS = np.stack([np.clip(d0, -128, 128) + 128, np.clip(d1, -128, 128) + 128], 0)
    biasS = rel[:, idxS]
    biasS = np.ascontiguousarray(biasS.transpose(2, 0, 1, 3)).reshape(128, 8 * 64)
    cb = np.ascontiguousarray(np.broadcast_to(rel[:, 256][None, :], (128, 8)))
    return biasP.astype(np.float32), mk.reshape(128, 384), biasS.astype(np.float32), cb.astype(np.float32)


_CACHE = {}


def kernel(x_prompt, x_sample, cache_attn_k, cache_attn_v, state_gla,
           norm_mix_pre, norm_mix_post, norm_ffn_pre, norm_ffn_post,
           w_in, w_decay_up, b_decay, rel_bias, gla_norm, w_proj_a, w_proj_b, w_out,
           w_ffn_gate, w_ffn_up, w_ffn_down):
    f = lambda a: np.ascontiguousarray(np.asarray(a, dtype=np.float32))
    x_prompt, x_sample = f(x_prompt), f(x_sample)
    if "nc" not in _CACHE:
        _CACHE["nc"] = build_program()
    nc, P = _CACHE["nc"]
    xpf = x_prompt.reshape(SEQ, D)
    wup = np.zeros((32, 512), np.float32)
    wup[0:16] = f(w_decay_up)[0]
    wup[16] = f(b_decay)[0]
    ones_row = np.zeros((16, G * 128), np.float32)
    ones_row[0] = 1.0
    bP, mkP, bS, cb = _bias_tables(f(rel_bias)[0])
    shared = {
        "w_in": f(w_in)[0], "wup": wup,
        "gpre": np.ascontiguousarray(f(norm_mix_pre)[0].reshape(8, 128).T),
        "gffn": np.ascontiguousarray(f(norm_ffn_pre)[0].reshape(8, 128).T),
        "gpost": f(norm_mix_post)[0].reshape(1, D), "gfpost": f(norm_ffn_post)[0].reshape(1, D),
        "gnorm": f(gla_norm)[0].reshape(1, 256),
        "wpa": f(w_proj_a)[0], "wpb": f(w_proj_b)[0], "wout": f(w_out)[0],
        "wg": f(w_ffn_gate)[0], "wu": f(w_ffn_up)[0], "wd": f(w_ffn_down)[0],
        "biasP": bP, "mkP": mkP, "biasS": bS, "cbias": cb,
        "cm64": _const_masks(64), "cm32": _const_masks(32), "ones_row": ones_row,
    }
    ck = f(cache_attn_k)[0].reshape(32, 512, 512)
    cv = f(cache_attn_v)[0].reshape(32, 512, 512)
    sgl = f(state_gla)[0]
    in_maps = []
    for c in range(NCORES):
        m = dict(shared)
        m["xp"] = xpf[c * TOK_CORE:(c + 1) * TOK_CORE]
        m["xh"] = xpf[c * TOK_CORE - 512:c * TOK_CORE] if c > 0 else np.zeros((512, D), np.float32)
        xpre = np.zeros((7 * TOK_CORE, D), np.float32)
        if c > 0:
            xpre[(7 - c) * TOK_CORE:] = xpf[0:c * TOK_CORE]
        m["xpre"] = xpre
        m["xs"] = x_sample[4 * c:4 * c + 4].reshape(128, D)
        m["ck"] = ck[4 * c:4 * c + 4]
        m["cv"] = cv[4 * c:4 * c + 4]
        m["sg"] = sgl[4 * c:4 * c + 4]
        m["valid"] = np.full((128, 1), 1.0 if c > 0 else 0.0, np.float32)
        in_maps.append(m)
    res = run_bass_kernel_spmd(nc, in_maps, core_ids=list(range(NCORES)))
    R = res.results
    yp = np.concatenate([R[c]["yp"] for c in range(NCORES)], 0).reshape(1, SEQ, D)
    ys = np.concatenate([R[c]["ys"] for c in range(NCORES)], 0).reshape(32, 32, D)
    kpo = R[NCORES - 1]["kp"].reshape(1, 1, 512, 8, 64)
    vpo = R[NCORES - 1]["vp"].reshape(1, 1, 512, 8, 64)
    spo = R[NCORES - 1]["spo"].reshape(128, 4, 256).transpose(1, 0, 2).reshape(1, 1, 4, 128, 256)
    kso = np.concatenate([R[c]["ksn"] for c in range(NCORES)], 0).reshape(1, 32, 32, 8, 64)
    vso = np.concatenate([R[c]["vsn"] for c in range(NCORES)], 0).reshape(1, 32, 32, 8, 64)
    sso = np.concatenate([R[c]["sso"] for c in range(NCORES)], 0).reshape(32, 128, 4, 256).transpose(0, 2, 1, 3).reshape(1, 32, 4, 128, 256)
    return (yp, ys, kpo, vpo, np.ascontiguousarray(spo), kso, vso, np.ascontiguousarray(sso))
```

```python
import contextlib
import numpy as np
import concourse.bass as bass
import concourse.mybir as mybir
from concourse.bass_utils import run_bass_kernel_spmd

F32 = mybir.dt.float32
BF16 = mybir.dt.bfloat16
AF = mybir.ActivationFunctionType
ALU = mybir.AluOpType

ENGS = ("pe", "act", "dve", "pool", "sp")


class Op:
    __slots__ = ("eng", "fn", "deps", "idx", "sig", "seq", "is_dma", "semkey", "sem", "semval", "inc")


class KeyState:
    __slots__ = ("writers", "readers")

    def __init__(self):
        self.writers = {}
        self.readers = {}


class Prog:
    def __init__(self, nc):
        self.nc = nc
        self.ops = []
        self.state = {}
        self.stack = contextlib.ExitStack()
        self.out_dmas = []
        self.last_dma = {}
        self.cc_barrier = None

    def sbuf(self, name, shape, dtype):
        return self.stack.enter_context(self.nc.sbuf_tensor(name, list(shape), dtype))

    def psum(self, name, shape, dtype):
        return self.stack.enter_context(self.nc.psum_tensor(name, list(shape), dtype))

    def add(self, eng, fn, reads=(), writes=(), dma=False, semkey=None, is_out=False, inc=16, barrier=False):
        op = Op()
        op.eng = eng
        op.fn = fn
        op.idx = len(self.ops)
        op.sig = False
        op.seq = None
        op.is_dma = dma
        op.semkey = None
        op.sem = None
        op.semval = None
        op.inc = inc
        deps = {}
        ek = ("dma", op.idx) if dma else eng
        psr = [k for k in reads if k.startswith("ps")]
        if psr:
            reads = [k for k in reads if not k.startswith("ps")]
            writes = list(writes) + [k for k in psr if k not in writes]
        for k in reads:
            st = self.state.get(k)
            if st is None:
                st = self.state[k] = KeyState()
            for d in st.writers.values():
                deps[d.idx] = d
        for k in writes:
            st = self.state.get(k)
            if st is None:
                st = self.state[k] = KeyState()
            for d in st.writers.values():
                deps[d.idx] = d
            for d in st.readers.values():
                deps[d.idx] = d
        for k in reads:
            self.state[k].readers[ek] = op
        for k in writes:
            st = self.state[k]
            st.writers = {ek: op}
            st.readers = {}
        if dma:
            if barrier:
                for d in self.last_dma.values():
                    deps[d.idx] = d
                self.cc_barrier = op
            elif self.cc_barrier is not None:
                deps[self.cc_barrier.idx] = self.cc_barrier
        deps.pop(op.idx, None)
        op.deps = list(deps.values())
        if dma:
            if semkey is None:
                semkey = writes[0] if writes else reads[0]
            op.semkey = semkey
            self.last_dma[semkey] = op
            if is_out:
                self.out_dmas.append(op)
        self.ops.append(op)
        return op

    def finish(self):
        op = self.add("sp", None)
        op.deps = list(self.out_dmas)

    def emit(self):
        nc = self.nc

        def need_wait(op, d):
            if d.is_dma:
                return True
            if d.eng == op.eng:
                if op.is_dma:
                    return True
                if op.eng == "pe":
                    return False
                return True
            return True

        for op in self.ops:
            for d in op.deps:
                if not d.is_dma and need_wait(op, d):
                    d.sig = True
        counters = {e: 0 for e in ENGS}
        for op in self.ops:
            if op.sig and not op.is_dma:
                counters[op.eng] += 1
                op.seq = counters[op.eng]
        semkeys = []
        seen = set()
        for op in self.ops:
            if op.is_dma and op.semkey not in seen:
                seen.add(op.semkey)
                semkeys.append(op.semkey)
        self.n_sems = len(semkeys) + len(ENGS)
        engsem = {e: self.stack.enter_context(nc.semaphore("s_" + e)) for e in ENGS}
        dmasem = {k: self.stack.enter_context(nc.semaphore("d_%d" % i)) for i, k in enumerate(semkeys)}
        dmacnt = {k: 0 for k in semkeys}
        for op in self.ops:
            if op.is_dma:
                dmacnt[op.semkey] += op.inc
                op.sem = dmasem[op.semkey]
                op.semval = dmacnt[op.semkey]
        per_eng = {e: [o for o in self.ops if o.eng == e] for e in ENGS}

        def run(e, engobj):
            waited = {}
            for op in per_eng[e]:
                for d in op.deps:
                    if not need_wait(op, d):
                        continue
                    if d.is_dma:
                        key = ("d", d.semkey)
                        val = d.semval
                        sem = d.sem
                    else:
                        key = ("e", d.eng)
                        val = d.seq
                        sem = engsem[d.eng]
                    if waited.get(key, 0) >= val:
                        continue
                    waited[key] = val
                    engobj.wait_ge(sem, val)
                if op.fn is None:
                    continue
                ins = op.fn(engobj)
                if op.is_dma:
                    if op.inc == 16:
                        ins.then_inc(op.sem, 16)
                    else:
                        ins.then_inc(op.sem)
                elif op.sig:
                    ins.then_inc(engsem[e], 1)

        with nc.Block() as block:
            @block.sync
            def _(eng):
                run("sp", eng)

            @block.tensor
            def _(eng):
                run("pe", eng)

            @block.scalar
            def _(eng):
                run("act", eng)

            @block.vector
            def _(eng):
                run("dve", eng)

            @block.gpsimd
            def _(eng):
                run("pool", eng)

    def close(self):
        self.stack.close()


D = 1024
DIN = 6672
DFF = 2816
NCORES = 8
SEQ = 16384
TOK_CORE = SEQ // NCORES
NT_P = TOK_CORE // 128
G = 2
QA, KA, VA, QB, KB, VB, RB, DLR, GA, GB = 0, 512, 1024, 1536, 2048, 2560, 3584, 4608, 4624, 5648
EPS = 1e-6
NEG = -30000.0


def build_program(debug=False):
    nc = bass.Bass("TRN2", target_bir_lowering=False)
    dbg = {}
    P = Prog(nc)
    A = P.add

    def din(name, shape):
        return nc.dram_tensor(name, list(shape), F32, kind="ExternalInput").ap()

    def dout(name, shape):
        return nc.dram_tensor(name, list(shape), F32, kind="ExternalOutput").ap()

    xp = din("xp", [TOK_CORE, D])
    xh = din("xh", [512, D])
    xpre = din("xpre", [7 * TOK_CORE, D])
    ones_row = din("ones_row", [16, G * 128])
    xs = din("xs", [128, D])
    ck = din("ck", [4, 512, 512])
    cv = din("cv", [4, 512, 512])
    sg = din("sg", [4, 4, 128, 256])
    valid = din("valid", [128, 1])
    w_in = din("w_in", [D, DIN])
    wup = din("wup", [32, 512])
    gpre = din("gpre", [128, 8])
    gffn = din("gffn", [128, 8])
    gpost = din("gpost", [1, D])
    gfpost = din("gfpost", [1, D])
    gnorm = din("gnorm", [1, 256])
    wpa = din("wpa", [512, D])
    wpb = din("wpb", [D, D])
    wout = din("wout", [D, D])
    wg = din("wg", [D, DFF])
    wu = din("wu", [D, DFF])
    wd = din("wd", [DFF, D])
    biasP = din("biasP", [128, 8 * 384])
    mkP = din("mkP", [128, 384])
    biasS = din("biasS", [128, 8 * 64])
    cbias = din("cbias", [128, 8])
    cm64 = din("cm64", [128, 6 * 128 + 512 + 2])
    cm32 = din("cm32", [128, 6 * 128 + 512 + 4])

    yp = dout("yp", [TOK_CORE, D])
    ys = dout("ys", [128, D])
    kp = dout("kp", [512, 512])
    vp = dout("vp", [512, 512])
    spo = dout("spo", [128, 1024])
    ksn = dout("ksn", [128, 512])
    vsn = dout("vsn", [128, 512])
    sso = dout("sso", [4, 128, 1024])

    wsrc = {"w_in": (w_in, D, DIN), "wpa": (wpa, 512, D), "wpb": (wpb, D, D), "wout": (wout, D, D),
            "wg": (wg, D, DFF), "wu": (wu, D, DFF), "wd": (wd, DFF, D)}

    T = G * 128
    xg = P.sbuf("xg", [128, 2 * G, D], F32)
    hb = P.sbuf("hb", [128, D], BF16)
    hb2 = P.sbuf("hb2", [128, D], BF16)
    hT = P.sbuf("hT", [128, 8, T], BF16)
    qTa = P.sbuf("qTa", [128, 4, T], BF16)
    kTa = P.sbuf("kTa", [128, 4, 8 * 128], BF16)
    vA = P.sbuf("vA", [128, 64, 65], BF16)
    qTb = P.sbuf("qTb", [128, 4, T], BF16)
    kTb = P.sbuf("kTb", [128, 4, T], BF16)
    kbt = P.sbuf("kbt", [128, G, 512], BF16)
    vb = P.sbuf("vb", [128, G, 1024], BF16)
    rbs = P.sbuf("rbs", [128, G, 1024], BF16)
    dlrT = P.sbuf("dlrT", [32, T], F32)
    gaT = P.sbuf("gaT", [128, 8, T], BF16)
    gbT = P.sbuf("gbT", [128, 8, T], BF16)
    oaT = P.sbuf("oaT", [128, 4, T], BF16)
    obT = P.sbuf("obT", [128, 8, T], BF16)
    aT = P.sbuf("aT", [128, 22, T], BF16)
    fsb = P.sbuf("fsb", [128, G, D], F32)
    wblk = [P.sbuf("wblk%d" % i, [128, 8, 512], BF16) for i in range(3)]
    ystage = [P.sbuf("yst%d" % i, [128, D], F32) for i in range(2)]
    kvst = [P.sbuf("kvst%d" % i, [128, 512], F32) for i in range(2)]
    ident = P.sbuf("ident", [128, 128], BF16)
    gpre_sb = P.sbuf("gpre_sb", [128, 8], F32)
    gffn_sb = P.sbuf("gffn_sb", [128, 8], F32)
    gpost_sb = P.sbuf("gpost_sb", [128, D], F32)
    gfpost_sb = P.sbuf("gfpost_sb", [128, D], F32)
    gn_sb = P.sbuf("gn_sb", [128, 256], F32)
    wup_sb = P.sbuf("wup_sb", [32, 512], F32)
    biasP_sb = P.sbuf("biasP_sb", [128, 8, 384], BF16)
    biasS_sb = P.sbuf("biasS_sb", [128, 8, 64], BF16)
    cbias_sb = P.sbuf("cbias_sb", [128, 8], F32)
    valid_sb = P.sbuf("valid_sb", [128, 1], F32)
    cm = {64: P.sbuf("cm64_sb", [128, 6 * 128 + 512 + 2], F32), 32: P.sbuf("cm32_sb", [128, 6 * 128 + 512 + 4], F32)}
    ones8 = P.sbuf("ones8", [128, 8], F32)
    ss = P.sbuf("ss", [128, 4], F32)
    ssn = P.sbuf("ssn", [128, 4], F32)
    rstdn = P.sbuf("rstdn", [128, 4], F32)
    rstd = P.sbuf("rstd", [128, 4], F32)
    junk = P.sbuf("junk", [128, D], BF16)
    tmpf = P.sbuf("tmpf", [128, D], F32)
    mkP_sb = tmpf[:, 0:384]
    identf = tmpf[:, 512:640]
    stmp2 = [P.sbuf("stmp%d" % i, [128, 3, 128], F32) for i in range(2)]
    expT2 = [P.sbuf("expT%d" % i, [128, 5, 128], BF16) for i in range(2)]
    rden = P.sbuf("rden", [128, 8], F32)
    oa_tok = P.sbuf("oa_tok", [128, 512], BF16)
    ob_tok = P.sbuf("ob_tok", [128, 1024], BF16)
    la_neg = P.sbuf("la_neg", [128, 512], F32)
    laT_neg = P.sbuf("laT_neg", [128, 512], F32)
    bneg = P.sbuf("bneg", [128, 512], F32)
    eb = P.sbuf("eb", [128, 512], F32)
    enb = P.sbuf("enb", [128, 512], F32)
    ec = P.sbuf("ec", [128, 512], F32)
    qtl = P.sbuf("qtl", [128, 4, 128], BF16)
    ktl = P.sbuf("ktl", [128, 4, 128], BF16)
    qz = [P.sbuf("qz%d" % i, [128, 4, 128], BF16) for i in range(4)]
    khat = [P.sbuf("khat%d" % i, [128, 512], BF16) for i in range(4)]
    attn_sb = P.sbuf("attn_sb", [128, 4, 128], BF16)
    S = P.sbuf("S", [128, 4, 256], F32)
    Sbf = [P.sbuf("Sbf%d" % i, [128, 4, 256], BF16) for i in range(2)]
    vc = P.sbuf("vc", [128, 32, 65], BF16)
    rdens = P.sbuf("rdens", [32, 8], F32)
    stmps2 = [P.sbuf("stmps%d" % i, [128, 2, 32], F32) for i in range(2)]
    expTs2 = [P.sbuf("expTs%d" % i, [128, 5, 32], BF16) for i in range(2)]

    vnew = aT[0:32, 0:9, :].rearrange("p a b -> p (a b)")[:, 0:2080].rearrange("p (n d) -> p n d", d=65)
    oas = aT[0:32, 9:18, :].rearrange("p a b -> p (a b)")[:, 0:2048].rearrange("p (n d) -> p n d", d=512)
    kc_tok = tmpf[:].bitcast(BF16).rearrange("p (t f) -> p t f", t=4)
    Sbs = aT[:, 6:22, :]
    kTc = obT[:].rearrange("p a b -> p (a b)").rearrange("p (a b) -> p a b", a=4)
    wdlr = P.sbuf("wdlr", [128, 8, 16], BF16)
    wup_bf = P.sbuf("wup_bf", [32, 512], BF16)
    dlrTb = P.sbuf("dlrTb", [32, 256], BF16)
    onesb = P.sbuf("onesb", [128, 2], BF16)
    utf_bf = P.sbuf("utf_bf", [128, 128], BF16)
    etile = P.sbuf("etile", [128, 2, 4], F32)
    ssp = P.sbuf("ssp", [128, 2], F32)
    rstdp = P.sbuf("rstdp", [128, 2], F32)
    aTf = aT[:].rearrange("p a b -> p (a b)")
    pp_hb = [aTf[:, i * 1024:(i + 1) * 1024] for i in range(2)]
    pp_hT = [aTf[:, 2048 + i * 1024:2048 + (i + 1) * 1024].rearrange("p (k n) -> p k n", k=8) for i in range(2)]
    obTf = obT[:].rearrange("p a b -> p (a b)")
    pp_kbt = [obTf[:, i * 512:(i + 1) * 512] for i in range(4)]
    gaTf = gaT[:].rearrange("p a b -> p (a b)")
    gbTf = gbT[:].rearrange("p a b -> p (a b)")
    pp_vb = [gaTf[:, 0:1024], gaTf[:, 1024:2048], gbTf[:, 0:1024], gbTf[:, 1024:2048]]
    hTf = hT[:].rearrange("p a b -> p (a b)")
    pp_la = [hTf[:, i * 512:(i + 1) * 512] for i in range(2)]
    pp_khat = [hTf[:, 1024 + i * 512:1024 + (i + 1) * 512] for i in range(2)]
    fsbf = fsb[:].rearrange("p a b -> p (a b)")
    pp_ec = [fsbf[:, i * 512:(i + 1) * 512] for i in range(2)]
    ps = [P.psum("ps%d" % i, [128, 512], F32) for i in range(7)]
    pst = P.psum("pst", [128, 8, 128], BF16)

    def dump(name, ap_sb, shape, dtype, key):
        if not debug:
            return
        t = nc.dram_tensor("dbg_" + name, list(shape), F32, kind="ExternalOutput").ap()
        if dtype == F32:
            A("sp", lambda e: e.dma_start(out=t, in_=ap_sb), reads=[key], dma=True, is_out=True, semkey="dbg_" + name)
        else:
            n = shape[1] * shape[2]
            if n <= 1024:
                stg = fsb[:, 1, 0:n].rearrange("p (a b) -> p a b", a=shape[1])
            else:
                stg = fsb[:].rearrange("p a b -> p (a b)")[:, 0:n].rearrange("p (a b) -> p a b", a=shape[1])
            A("dve", lambda e: e.tensor_copy(out=stg, in_=ap_sb), reads=[key], writes=["fsb0", "fsb1"])
            A("sp", lambda e: e.dma_start(out=t, in_=stg), reads=["fsb0", "fsb1"], dma=True, is_out=True, semkey="dbg_" + name)

    def mask_UT(c):
        return cm[c][:, 0:128]

    def mask_T4(c):
        return cm[c][:, 128:640]

    def mask_restart(c):
        return cm[c][:, 768:1280]

    def rowmask(c, i):
        return cm[c][:, 1280 + i:1281 + i]

    def load(dst, src, key, eng="sp"):
        if eng == "pool":
            A(eng, lambda e: e.dma_start(out=dst, in_=src), writes=["poolq", key], dma=True, semkey="poolq")
        else:
            A(eng, lambda e: e.dma_start(out=dst, in_=src), writes=[key], dma=True)

    load(gpre_sb[:], gpre[:, :], "gpre")
    load(gffn_sb[:], gffn[:, :], "gffn")
    load(gpost_sb[:], gpost.broadcast_to([128, D]), "gpost")
    load(gfpost_sb[:], gfpost.broadcast_to([128, D]), "gfpost")
    load(gn_sb[:], gnorm.broadcast_to([128, 256]), "gn")
    load(wup_sb[:], wup[:, :], "wup")
    load(mkP_sb, mkP[:, :], "tmpf")
    load(cbias_sb[:], cbias[:, :], "cbias")
    load(valid_sb[:], valid[:, :], "valid")
    load(cm[64][:], cm64[:, :], "cm64")
    load(cm[32][:], cm32[:, :], "cm32")
    load(biasP_sb[:], biasP.rearrange("p (h n) -> p h n", h=8), "biasP", eng="pool")
    load(biasS_sb[:], biasS.rearrange("p (h n) -> p h n", h=8), "biasS", eng="pool")
    A("dve", lambda e: e.memset(identf, 1.0), writes=["tmpf"])
    A("pool", lambda e: e.affine_select(out=identf, in_=identf, pattern=[[-1, 128]], compare_op=ALU.is_equal,
                                        fill=0.0, base=0, channel_multiplier=1), reads=["tmpf"], writes=["tmpf"])
    A("dve", lambda e: e.tensor_copy(out=ident[:], in_=identf), reads=["tmpf"], writes=["ident"])
    A("dve", lambda e: e.memset(ones8[:], 1.0), writes=["ones8"])
    for h in range(8):
        A("dve", lambda e, h=h: e.tensor_tensor(out=biasP_sb[:, h, :], in0=biasP_sb[:, h, :], in1=mkP_sb, op=ALU.add),
          reads=["biasP", "tmpf"], writes=["biasP"])
    A("dve", lambda e: e.memset(vA[:, :, 64:65], 1.0), writes=["vA%d" % s for s in range(8)])
    A("dve", lambda e: e.memset(vc[:, :, 64:65], 1.0), writes=["vc"])
    A("dve", lambda e: e.memset(dlrT[:], 0.0), writes=["dlrT"])
    for i in range(4):
        A("dve", lambda e, i=i: e.memset(qz[i][:], 0.0), writes=["qz%d" % i])
    A("dve", lambda e: e.memset(S[:], 0.0), writes=["S", "S0", "S1", "S2", "S3"])
    A("dve", lambda e: e.memset(Sbf[0][:], 0.0), writes=["Sbf0"])

    wctr = [0]
    A("dve", lambda e: e.memset(dlrTb[:], 0.0), writes=["dlrTb"])
    A("pool", lambda e: e.dma_start(out=dlrTb[16:32, :], in_=ones_row[:, :]), writes=["poolq", "dlrTb"], dma=True, semkey="poolq")
    cvctr = [0]
    wblocks = {}

    def register_block(wname, k0, nkc, c0, ncols):
        sig = (wname, k0, nkc, c0, ncols)
        if sig in wblocks:
            return wblocks[sig]
        scr = nc.dram_tensor("scr_%s_%d_%d" % (wname, k0, c0), [128, nkc * ncols], BF16).ap()
        src = wsrc[wname][0].rearrange("(k p) n -> p k n", p=128)[:, k0:k0 + nkc, c0:c0 + ncols]
        q = "cvq%d" % (cvctr[0] % 4)
        cvctr[0] += 1
        op = A("pool", lambda e: e.dma_start(out=scr.rearrange("p (k n) -> p k n", k=nkc), in_=src), writes=[q], dma=True, semkey=q)
        wblocks[sig] = (scr, op)
        return wblocks[sig]

    for c0_ in (QA, KA, VA, QB, KB, VB, VB + 512, RB, RB + 512):
        register_block("w_in", 0, 8, c0_, 512)
    register_block("w_in", 0, 8, DLR, 16)
    for c0_ in (GA, GA + 512, GB, GB + 512):
        register_block("w_in", 0, 8, c0_, 512)
    for half_ in range(2):
        register_block("wpa", 0, 4, half_ * 512, 512)
    for half_ in range(2):
        register_block("wpb", 0, 8, half_ * 512, 512)
    for cb_ in range(2):
        register_block("wout", 0, 8, cb_ * 512, 512)
    for fb4_ in range(0, 22, 4):
        nb_ = min(4, 22 - fb4_)
        register_block("wg", 0, 8, fb4_ * 128, nb_ * 128)
        register_block("wu", 0, 8, fb4_ * 128, nb_ * 128)
    for cb_ in range(2):
        for (k0_, nk_) in ((0, 8), (8, 8), (16, 6)):
            register_block("wd", k0_, nk_, cb_ * 512, 512)

    wctr = [0]
    wblk.append(fsb[:].rearrange("p a b -> p (a b)").bitcast(BF16).rearrange("p (k n) -> p k n", k=8))
    slotkeys = {0: ["w0"], 1: ["w1"], 2: ["w2"], 3: ["w3", "fsb0", "fsb1"]}

    def load_w(wname, k0, nkc, c0, ncols, slot=None):
        if slot is None:
            slot = wctr[0] % 3
            wctr[0] += 1
        scr, cop = register_block(wname, k0, nkc, c0, ncols)
        op = A("sp", lambda e: e.dma_start(out=wblk[slot][:, 0:nkc, 0:ncols], in_=scr.rearrange("p (k n) -> p k n", k=nkc)),
               writes=slotkeys[slot], dma=True, semkey="w%d" % slot)
        if cop.idx not in set(d.idx for d in op.deps):
            op.deps.append(cop)
        return slot

    pctr = [0]

    def acc_bank():
        b = pctr[0] % 2
        pctr[0] += 1
        return b

    def bstyle(slot, nkc, nblk, src, srckey, Tg, evac, mrows=128):
        for ob in range(nblk):
            b = acc_bank()
            for kc in range(nkc):
                A("pe", lambda e, kc=kc, ob=ob, b=b: e.matmul(ps[b][0:mrows, 0:Tg], lhsT=wblk[slot][:, kc, ob * 128:ob * 128 + mrows],
                                                               rhs=src[:, kc, 0:Tg], start=(kc == 0), stop=(kc == nkc - 1)),
                  reads=["w%d" % slot, srckey], writes=["ps%d" % b])
            evac(ps[b], "ps%d" % b, ob)

    def astyle_tile(slot, nkc, ncols, src, srckey, tok0, mtok, evac, kc_off=0):
        b = acc_bank()
        for kc in range(nkc):
            A("pe", lambda e, kc=kc, b=b: e.matmul(ps[b][0:mtok, 0:ncols], lhsT=src[:, kc_off + kc, tok0:tok0 + mtok],
                                                   rhs=wblk[slot][:, kc, 0:ncols], start=(kc == 0), stop=(kc == nkc - 1)),
              reads=["w%d" % slot, srckey], writes=["ps%d" % b])
        evac(ps[b], "ps%d" % b)

    def norm_to_hT(src_ap, srckey, gcol_sb, gkey, tok0):
        A("act", lambda e: e.activation(out=junk[:], in_=src_ap, func=AF.Square, accum_out=ss[:, 0:1]),
          reads=[srckey], writes=["junk", "ss"])
        A("act", lambda e: e.activation(out=rstd[:, 0:1], in_=ss[:, 0:1], func=AF.Ln, scale=1.0 / D, bias=EPS),
          reads=["ss"], writes=["rstd"])
        A("act", lambda e: e.activation(out=rstd[:, 0:1], in_=rstd[:, 0:1], func=AF.Exp, scale=-0.5),
          reads=["rstd"], writes=["rstd"])
        A("dve", lambda e: e.tensor_scalar(out=hb[:], in0=src_ap, scalar1=rstd[:, 0:1], scalar2=None, op0=ALU.mult),
          reads=[srckey, "rstd"], writes=["hb"])
        for kc in range(8):
            A("pe", lambda e, kc=kc: e.transpose(out=pst[:, kc, :], in_=hb[:, kc * 128:(kc + 1) * 128], identity=ident[:]),
              reads=["hb", "ident"], writes=["pst"])
        for kc in range(8):
            A("act", lambda e, kc=kc: e.activation(out=hT[:, kc, tok0:tok0 + 128], in_=pst[:, kc, :], func=AF.Copy,
                                                   scale=gcol_sb[:, kc:kc + 1]),
              reads=["pst", gkey], writes=["hT"])

    def transpose_to(src_tok, srckey, nblk, dst, dstkey, tok0, rows=128):
        for blk in range(nblk):
            A("pe", lambda e, blk=blk: e.transpose(out=pst[:, blk, 0:rows], in_=src_tok[0:rows, blk * 128:(blk + 1) * 128],
                                                   identity=ident[0:rows, 0:rows]),
              reads=[srckey, "ident"], writes=["pst"])
        A("act", lambda e: e.copy(out=dst[:, 0:nblk, tok0:tok0 + rows], in_=pst[:, 0:nblk, 0:rows]),
          reads=["pst"], writes=[dstkey])

    def gla_tile(t, c, state_only, sample=False, bk=None):
        bk = bk or {"la": 2, "laT": 3, "c": 4, "at": 5, "st": 6}
        B_la, B_laT, B_c, B_at, B_st = bk["la"], bk["laT"], bk["c"], bk["at"], bk["st"]
        nch = 128 // c
        tok0 = t * 128
        A("pe", lambda e: e.matmul(ps[B_la][:, :], lhsT=dlrT[0:32, tok0:tok0 + 128], rhs=wup_sb[0:32, :], start=True, stop=True),
          reads=["dlrT", "wup"], writes=["ps%d" % B_la])
        for h in range(4):
            A("pe", lambda e, h=h: e.matmul(ps[B_laT][:, h * 128:(h + 1) * 128], lhsT=wup_sb[0:32, h * 128:(h + 1) * 128],
                                            rhs=dlrT[0:32, tok0:tok0 + 128], start=True, stop=True),
              reads=["dlrT", "wup"], writes=["ps%d" % B_laT])
        A("act", lambda e: e.activation(out=la_neg[:], in_=ps[B_la][:, :], func=AF.Exp, scale=-1.0), reads=["ps%d" % B_la], writes=["la_neg"])
        A("act", lambda e: e.activation(out=laT_neg[:], in_=ps[B_laT][:, :], func=AF.Exp, scale=-1.0), reads=["ps%d" % B_laT], writes=["laT_neg"])
        A("act", lambda e: e.activation(out=la_neg[:], in_=la_neg[:], func=AF.Ln, bias=1.0), reads=["la_neg"], writes=["la_neg"])
        A("act", lambda e: e.activation(out=laT_neg[:], in_=laT_neg[:], func=AF.Ln, bias=1.0), reads=["laT_neg"], writes=["laT_neg"])
        yield
        A("pe", lambda e: e.matmul(ps[B_c][:, :], lhsT=mask_UT(c), rhs=la_neg[:], start=True, stop=True),
          reads=["la_neg", "cm%d" % c], writes=["ps%d" % B_c])
        A("dve", lambda e: e.tensor_tensor_scan(out=bneg[:], data0=mask_restart(c), data1=laT_neg[:], initial=0.0,
                                                op0=ALU.mult, op1=ALU.add),
          reads=["laT_neg", "cm%d" % c], writes=["bneg"])
        A("act", lambda e: e.activation(out=ec[:], in_=ps[B_c][:, :], func=AF.Exp, scale=-1.0 / 16), reads=["ps%d" % B_c], writes=["ec"])
        A("act", lambda e: e.activation(out=eb[:], in_=bneg[:], func=AF.Exp, scale=-1.0 / 16), reads=["bneg"], writes=["eb"])
        if not state_only:
            A("act", lambda e: e.activation(out=enb[:], in_=bneg[:], func=AF.Exp, scale=1.0 / 16), reads=["bneg"], writes=["enb"])
        for i in range(nch):
            A("dve", lambda e, i=i: e.scalar_tensor_tensor(out=khat[i][:], in0=kbt[:, t, :], scalar=rowmask(c, i), in1=ec[:],
                                                           op0=ALU.mult, op1=ALU.mult),
              reads=["kbt", "ec", "cm%d" % c], writes=["khat%d" % i])
        yield
        if not state_only:
            A("dve", lambda e: e.tensor_tensor(out=qtl[:], in0=qTb[:, :, tok0:tok0 + 128],
                                               in1=eb[:].rearrange("p (h n) -> p h n", h=4), op=ALU.mult),
              reads=["qTb", "eb"], writes=["qtl"])
            A("dve", lambda e: e.tensor_tensor(out=ktl[:], in0=kTb[:, :, tok0:tok0 + 128],
                                               in1=enb[:].rearrange("p (h n) -> p h n", h=4), op=ALU.mult),
              reads=["kTb", "enb"], writes=["ktl"])
            for i in range(nch):
                A("dve", lambda e, i=i: e.tensor_copy(out=qz[i][:, :, i * c:(i + 1) * c], in_=qtl[:, :, i * c:(i + 1) * c]),
                  reads=["qtl"], writes=["qz%d" % i])
            yield
            for h in range(4):
                A("pe", lambda e, h=h: e.matmul(ps[B_at][:, h * 128:(h + 1) * 128], lhsT=ktl[:, h, :], rhs=qtl[:, h, :], start=True, stop=True),
                  reads=["ktl", "qtl"], writes=["ps%d" % B_at])
            A("dve", lambda e: e.tensor_tensor(out=attn_sb[:].rearrange("p h n -> p (h n)"), in0=ps[B_at][:, :], in1=mask_T4(c), op=ALU.mult),
              reads=["ps%d" % B_at, "cm%d" % c], writes=["attn_sb"])
            yield

        skeys = []
        par = [0]

        def state_update(i):
            cur = par[0]
            skeys.append((Sbf[cur], "Sbf%d" % cur))
            sbank = lambda h: B_st if h < 2 else B_c
            for h in range(4):
                hh = h % 2
                bk_ = sbank(h)
                A("pe", lambda e, i=i, h=h, hh=hh, bk_=bk_: e.matmul(ps[bk_][:, hh * 256:(hh + 1) * 256], lhsT=khat[i][:, h * 128:(h + 1) * 128],
                                                                      rhs=vb[:, t, h * 256:(h + 1) * 256], start=True, stop=True),
                  reads=["khat%d" % i, "vb"], writes=["ps%d" % bk_])
            for h in range(4):
                hh = h % 2
                bk_ = sbank(h)
                col = h * 128 + (i + 1) * c - 1
                A("dve", lambda e, h=h, hh=hh, col=col, bk_=bk_: e.scalar_tensor_tensor(out=S[:, h, :], in0=S[:, h, :], scalar=eb[:, col:col + 1],
                                                                                         in1=ps[bk_][:, hh * 256:(hh + 1) * 256], op0=ALU.mult, op1=ALU.add),
                  reads=["S", "eb", "ps%d" % bk_], writes=["S"])
            if not state_only:
                nxt = 1 - cur
                A("act", lambda e, nxt=nxt: e.copy(out=Sbf[nxt][:], in_=S[:]), reads=["S"], writes=["Sbf%d" % nxt])
                par[0] = nxt

        def sample_states():
            for b in range(4):
                sl = fsb[:, b % 2, :].rearrange("p (h v) -> p h v", h=4)
                slk = "fsb%d" % (b % 2)
                A("sp", lambda e, b=b, sl=sl: e.dma_start(out=sl, in_=sg[b].rearrange("h d v -> d h v")), writes=[slk], dma=True)
                for hp in range(2):
                    for hh in range(2):
                        h = hp * 2 + hh
                        A("pe", lambda e, b=b, h=h, hh=hh: e.matmul(ps[B_st][:, hh * 256:(hh + 1) * 256], lhsT=khat[b][:, h * 128:(h + 1) * 128],
                                                                    rhs=vb[:, t, h * 256:(h + 1) * 256], start=True, stop=True),
                          reads=["khat%d" % b, "vb"], writes=["ps%d" % B_st])
                    for hh in range(2):
                        h = hp * 2 + hh
                        col = h * 128 + (b + 1) * c - 1
                        A("dve", lambda e, h=h, hh=hh, col=col, sl=sl: e.scalar_tensor_tensor(out=sl[:, h, :], in0=sl[:, h, :], scalar=eb[:, col:col + 1],
                                                                                               in1=ps[B_st][:, hh * 256:(hh + 1) * 256], op0=ALU.mult, op1=ALU.add),
                          reads=[slk, "eb", "ps%d" % B_st], writes=[slk])
                A("act", lambda e, b=b: e.dma_start(out=sso[b], in_=fsb[:, b % 2, :]), reads=[slk], dma=True,
                  is_out=True, semkey=slk + "o")

        if state_only:
            for i in range(nch):
                state_update(i)
            return
        if sample:
            sample_states()
        else:
            assert nch == 2
            state_update(0)
            skeys.append((Sbf[par[0]], "Sbf%d" % par[0]))
        yield

        obank = lambda h: B_c if h < 2 else B_st
        for h in range(4):
            hh = h % 2
            ob_ = obank(h)
            A("pe", lambda e, h=h, hh=hh, ob_=ob_: e.matmul(ps[ob_][:, hh * 256:(hh + 1) * 256], lhsT=attn_sb[:, h, :], rhs=vb[:, t, h * 256:(h + 1) * 256],
                                                             start=True, stop=False),
              reads=["attn_sb", "vb"], writes=["ps%d" % ob_])
            for i in range(nch):
                if sample:
                    rhs_ap = Sbs[:, i * 4 + h, :]
                    rk = "aT"
                else:
                    rhs_ap = skeys[i][0][:, h, :]
                    rk = skeys[i][1]
                A("pe", lambda e, h=h, hh=hh, i=i, rhs_ap=rhs_ap, ob_=ob_: e.matmul(ps[ob_][:, hh * 256:(hh + 1) * 256], lhsT=qz[i][:, h, :], rhs=rhs_ap,
                                                                                   start=False, stop=(i == nch - 1)),
                  reads=["qz%d" % i, rk], writes=["ps%d" % ob_])
        yield
        for h in range(4):
            hh = h % 2
            ob_ = obank(h)
            A("act", lambda e, h=h, hh=hh, ob_=ob_: e.activation(out=junk[:, 0:256], in_=ps[ob_][:, hh * 256:(hh + 1) * 256], func=AF.Square,
                                                                 accum_out=ss[:, h:h + 1]),
              reads=["ps%d" % ob_], writes=["junk", "ss"])
        A("act", lambda e: e.activation(out=rstd[:, 0:4], in_=ss[:, 0:4], func=AF.Ln, scale=1.0 / 256, bias=EPS), reads=["ss"], writes=["rstd"])
        A("act", lambda e: e.activation(out=rstd[:, 0:4], in_=rstd[:, 0:4], func=AF.Exp, scale=-0.5), reads=["rstd"], writes=["rstd"])
        for h in range(4):
            hh = h % 2
            ob_ = obank(h)
            A("dve", lambda e, h=h, hh=hh, ob_=ob_: e.scalar_tensor_tensor(out=ob_tok[:, h * 256:(h + 1) * 256], in0=ps[ob_][:, hh * 256:(hh + 1) * 256],
                                                                            scalar=rstd[:, h:h + 1], in1=rbs[:, t, h * 256:(h + 1) * 256],
                                                                            op0=ALU.mult, op1=ALU.mult),
              reads=["ps%d" % ob_, "rstd", "rbs"], writes=["ob_tok"])
        yield
        if not sample:
            skeys.pop()
            state_update(1)
            skeys.pop()
        yield
        transpose_to(ob_tok, "ob_tok", 8, obT, "obT", tok0)

    def attn_prompt_tile(t, a):
        tok0 = t * 128
        jA = {0: 0, 1: 1, 4: 2}
        jB = {2: 0, 3: 1}

        def scores(h):
            par = (h // 2) % 2
            blk, pr = h // 2, (h % 2) * 64
            bA, bB = (2, 3) if (h // 2) % 2 == 0 else (0, 1)
            for d in range(5):
                s = (a - d) % 8
                if d in jA:
                    bank, j, bk = ps[bA], jA[d], "ps%d" % bA
                else:
                    bank, j, bk = ps[bB], jB[d], "ps%d" % bB
                A("pe", lambda e, bank=bank, j=j, s=s: e.matmul(bank[:, j * 128:(j + 1) * 128], lhsT=kTa[pr:pr + 64, blk, s * 128:(s + 1) * 128],
                                                               rhs=qTa[pr:pr + 64, blk, tok0:tok0 + 128], start=True, stop=True),
                  reads=["kT%d" % s, "qTa"], writes=[bk])
            A("dve", lambda e: e.scalar_tensor_tensor(out=stmp2[par][:].rearrange("p a b -> p (a b)"), in0=ps[bA][:, 0:384], scalar=0.125,
                                                      in1=biasP_sb[:, h, :], op0=ALU.mult, op1=ALU.add),
              reads=["ps%d" % bA, "biasP"], writes=["stmp%d" % par])
            A("act", lambda e: e.activation(out=expT2[par][:, 0:3, :].rearrange("p a b -> p (a b)"), in_=stmp2[par][:].rearrange("p a b -> p (a b)"),
                                            func=AF.Exp), reads=["stmp%d" % par], writes=["expT%d" % par])
            A("act", lambda e: e.activation(out=expT2[par][:, 3:5, :].rearrange("p a b -> p (a b)"), in_=ps[bB][:, 0:256], func=AF.Exp,
                                            scale=0.125, bias=cbias_sb[:, h:h + 1]), reads=["ps%d" % bB, "cbias"], writes=["expT%d" % par])

        def pv(h, slot, grp):
            par = (h // 2) % 2
            pb = 5
            order = [(0, 0), (1, 1), (4, 2), (2, 3), (3, 4)]
            for n, (d, j) in enumerate(order):
                s = (a - d) % 8
                A("pe", lambda e, j=j, s=s, n=n: e.matmul(ps[pb][:, slot * 65:(slot + 1) * 65], lhsT=expT2[par][:, j, :],
                                                          rhs=vA[:, s * 8 + h, :], start=(n == 0), stop=(n == 4)),
                  reads=["expT%d" % par, "vA%d" % s], writes=["ps%d" % pb])
            if slot == 3:
                gi = 0 if grp[0] == 0 else 1
                pv3 = ps[pb][:, 0:260].rearrange("p (h n) -> p h n", h=4)
                A("dve", lambda e: e.reciprocal(out=rden[:, gi * 4:gi * 4 + 4], in_=pv3[:, :, 64]), reads=["ps%d" % pb], writes=["rden%d" % gi])
                for k, h2 in enumerate(grp):
                    A("dve", lambda e, h2=h2, k=k: e.tensor_scalar(out=oa_tok[:, h2 * 64:(h2 + 1) * 64], in0=ps[pb][:, k * 65:k * 65 + 64],
                                                                   scalar1=rden[:, gi * 4 + k:gi * 4 + k + 1], scalar2=None, op0=ALU.mult),
                      reads=["ps%d" % pb, "rden%d" % gi], writes=["oa_tok"])

        horder = [0, 2, 4, 6, 1, 3, 5, 7]
        scores(horder[0])
        yield
        for i_, h in enumerate(horder):
            if i_ + 1 < 8:
                scores(horder[i_ + 1])
                yield
            pv(h, i_ % 4, horder[(i_ // 4) * 4:(i_ // 4) * 4 + 4])
            yield
        transpose_to(oa_tok, "oa_tok", 4, oaT, "oaT", tok0)

    def attn_sample():
        vc2 = xg[:, 2:4, :].rearrange("p a b -> p (a b)").bitcast(BF16)[:, 0:2080].rearrange("p (n d) -> p n d", d=65)
        vcb = [vc, vc2]
        vck = [["vc"], ["xg2", "xg3"]]
        A("dve", lambda e: e.memset(vc2[:, :, 64:65], 1.0), writes=["xg2", "xg3"])

        def load_k(b):
            A("pool", lambda e: e.dma_start(out=kc_tok, in_=ck[b].rearrange("(t p) f -> p t f", p=128)), writes=["poolq", "tmpf"], dma=True, semkey="poolq")

        def load_v(b):
            for kt in range(4):
                A("pool", lambda e, kt=kt: e.dma_start(out=vcb[b % 2][:, kt * 8:(kt + 1) * 8, 0:64],
                                                       in_=cv[b][kt * 128:(kt + 1) * 128, :].rearrange("p (h d) -> p h d", h=8)),
                  writes=["poolq"] + vck[b % 2], dma=True, semkey="poolq")

        load_k(0)
        load_v(0)
        for b in range(4):
            vcur = vcb[b % 2]
            vkeys = vck[b % 2]
            for blk in range(4):
                for kt in range(4):
                    A("pe", lambda e, blk=blk, kt=kt: e.transpose(out=pst[:, kt, :], in_=kc_tok[:, kt, blk * 128:(blk + 1) * 128], identity=ident[:]),
                      reads=["tmpf", "ident"], writes=["pst"])
                A("act", lambda e, blk=blk: e.copy(out=kTc[:, blk, :], in_=pst[:, 0:4, :].rearrange("p a b -> p (a b)")),
                  reads=["pst"], writes=["obT"])
            if b + 1 < 4:
                load_k(b + 1)
                load_v(b + 1)
            def s_scores(h, b=b):
                blk, pr = h // 2, (h % 2) * 64
                par = (h // 2) % 2
                bB, bA = (3, 2) if par == 0 else (1, 0)
                for kt in range(3):
                    A("pe", lambda e, kt=kt: e.matmul(ps[bB][:, kt * 32:(kt + 1) * 32], lhsT=kTc[pr:pr + 64, blk, kt * 128:(kt + 1) * 128],
                                                      rhs=qTa[pr:pr + 64, blk, b * 32:(b + 1) * 32], start=True, stop=True),
                      reads=["obT", "qTa"], writes=["ps%d" % bB])
                A("pe", lambda e: e.matmul(ps[bA][:, 0:32], lhsT=kTc[pr:pr + 64, blk, 384:512],
                                           rhs=qTa[pr:pr + 64, blk, b * 32:(b + 1) * 32], start=True, stop=True),
                  reads=["obT", "qTa"], writes=["ps%d" % bA])
                A("pe", lambda e: e.matmul(ps[bA][0:32, 32:64], lhsT=kTa[pr:pr + 64, blk, b * 32:(b + 1) * 32],
                                           rhs=qTa[pr:pr + 64, blk, b * 32:(b + 1) * 32], start=True, stop=True),
                  reads=["kT0", "qTa"], writes=["ps%d" % bA])
                ex, st_ = expTs2[par], stmps2[par]
                A("act", lambda e: e.activation(out=ex[:, 0:3, :].rearrange("p a b -> p (a b)"), in_=ps[bB][:, 0:96], func=AF.Exp,
                                                scale=0.125, bias=cbias_sb[:, h:h + 1]),
                  reads=["ps%d" % bB, "cbias"], writes=["expTs%d" % par])
                A("dve", lambda e: e.scalar_tensor_tensor(out=st_[:, 0, :], in0=ps[bA][:, 0:32], scalar=0.125, in1=biasS_sb[:, h, 0:32],
                                                          op0=ALU.mult, op1=ALU.add),
                  reads=["ps%d" % bA, "biasS"], writes=["stmps%d" % par])
                A("dve", lambda e: e.scalar_tensor_tensor(out=st_[0:32, 1, :], in0=ps[bA][0:32, 32:64], scalar=0.125, in1=biasS_sb[0:32, h, 32:64],
                                                          op0=ALU.mult, op1=ALU.add),
                  reads=["ps%d" % bA, "biasS"], writes=["stmps%d" % par])
                A("act", lambda e: e.activation(out=ex[:, 3, :], in_=st_[:, 0, :], func=AF.Exp), reads=["stmps%d" % par], writes=["expTs%d" % par])
                A("act", lambda e: e.activation(out=ex[0:32, 4, :], in_=st_[0:32, 1, :], func=AF.Exp), reads=["stmps%d" % par], writes=["expTs%d" % par])

            def s_pv(h, slot, grp, b=b, vcur=vcur, vkeys=vkeys):
                par = (h // 2) % 2
                ex = expTs2[par]
                for kt in range(4):
                    A("pe", lambda e, kt=kt: e.matmul(ps[5][0:32, slot * 65:slot * 65 + 65], lhsT=ex[:, kt, :], rhs=vcur[:, kt * 8 + h, :],
                                                      start=(kt == 0), stop=False),
                      reads=["expTs%d" % par] + vkeys, writes=["ps5"])
                A("pe", lambda e: e.matmul(ps[5][0:32, slot * 65:slot * 65 + 65], lhsT=ex[0:32, 4, :], rhs=vnew[0:32, b * 8 + h, :],
                                           start=False, stop=True),
                  reads=["expTs%d" % par, "aT"], writes=["ps5"])
                if slot == 3:
                    pv3 = ps[5][0:32, 0:260].rearrange("p (h n) -> p h n", h=4)
                    A("dve", lambda e: e.reciprocal(out=rdens[:, 0:4], in_=pv3[:, :, 64]), reads=["ps5"], writes=["rdens"])
                    for k, h2 in enumerate(grp):
                        A("dve", lambda e, h2=h2, k=k: e.tensor_scalar(out=oas[:, b, h2 * 64:(h2 + 1) * 64], in0=ps[5][0:32, k * 65:k * 65 + 64],
                                                                       scalar1=rdens[:, k:k + 1], scalar2=None, op0=ALU.mult),
                          reads=["ps5", "rdens"], writes=["aT"])

            horder = [0, 2, 4, 6, 1, 3, 5, 7]
            s_scores(horder[0])
            for i_, h in enumerate(horder):
                if i_ + 1 < 8:
                    s_scores(horder[i_ + 1])
                s_pv(h, i_ % 4, horder[(i_ // 4) * 4:(i_ // 4) * 4 + 4])
        for b in range(4):
            for blk in range(4):
                A("pe", lambda e, b=b, blk=blk: e.transpose(out=pst[:, blk, 0:32], in_=oas[:, b, blk * 128:(blk + 1) * 128], identity=ident[0:32, 0:32]),
                  reads=["aT", "ident"], writes=["pst"])
            A("act", lambda e, b=b: e.copy(out=oaT[:, 0:4, b * 32:(b + 1) * 32], in_=pst[:, 0:4, 0:32]), reads=["pst"], writes=["oaT"])

    def prep_group(xsrc, row0, ntiles, xpar):
        for t in range(ntiles):
            xi = xpar * G + t
            A("sp", lambda e, t=t, xi=xi: e.dma_start(out=xg[:, xi, :], in_=xsrc[row0 + t * 128:row0 + (t + 1) * 128, :]), writes=["xg%d" % xi], dma=True)
            norm_to_hT(xg[:, xi, :], "xg%d" % xi, gpre_sb, "gpre", t * 128)

    def run_group(kind, xsrc, row0, ntiles, a0=None, out_ap=None, kv_out=None, xpar=0, prefetched=False, next_prep=None):
        Tg = ntiles * 128
        c = 32 if kind == "sample" else 64
        if not prefetched:
            prep_group(xsrc, row0, ntiles, xpar)

        def ev_copy(dst, dkey, scale=None, func=AF.Copy, rows=128):
            def f(bank, bkey, ob):
                if scale is None:
                    A("act", lambda e: e.activation(out=dst(ob), in_=bank[0:rows, 0:Tg], func=func), reads=[bkey], writes=[dkey(ob)])
                else:
                    A("act", lambda e: e.activation(out=dst(ob), in_=bank[0:rows, 0:Tg], func=func, scale=scale), reads=[bkey], writes=[dkey(ob)])
            return f

        full = kind in ("prompt", "sample")
        if full:
            s = load_w("w_in", 0, 8, QA, 512)
            bstyle(s, 8, 4, hT, "hT", Tg, ev_copy(lambda ob: qTa[:, ob, 0:Tg], lambda ob: "qTa"))
        if kind != "pre":
            s = load_w("w_in", 0, 8, KA, 512)
            if kind == "sample":
                bstyle(s, 8, 4, hT, "hT", Tg, ev_copy(lambda ob: kTa[:, ob, 0:128], lambda ob: "kT0"))
            else:
                s0 = a0 % 8
                def kdst(ob):
                    return kTa[:, ob, s0 * 128:s0 * 128 + Tg]
                def kev(bank, bkey, ob):
                    A("act", lambda e: e.copy(out=kdst(ob), in_=bank[:, 0:Tg]), reads=[bkey], writes=["kT%d" % ((a0 + i) % 8) for i in range(ntiles)])
                bstyle(s, 8, 4, hT, "hT", Tg, kev)
            if kv_out is not None:
                for t in range(ntiles):
                    def kout(bank, bkey, t=t):
                        st = kvst[t % 2]
                        sk = "kvst%d" % (t % 2)
                        A("act", lambda e: e.copy(out=st[:], in_=bank[:, 0:512]), reads=[bkey], writes=[sk])
                        A("act", lambda e: e.dma_start(out=kv_out[0][kv_out[2] + t * 128:kv_out[2] + (t + 1) * 128, :], in_=st[:]), reads=[sk], dma=True,
                          is_out=True, semkey=sk + "o")
                    astyle_tile(s, 8, 512, hT, "hT", t * 128, 128, kout)
            s = load_w("w_in", 0, 8, VA, 512)
            for t in range(ntiles):
                slot_v = 0 if kind == "sample" else (a0 + t) % 8
                def vev(bank, bkey, t=t, slot_v=slot_v):
                    if kind != "sample":
                        A("act", lambda e: e.copy(out=vA[:, slot_v * 8:(slot_v + 1) * 8, 0:64], in_=bank[:, 0:512].rearrange("p (h d) -> p h d", h=8)),
                          reads=[bkey], writes=["vA%d" % slot_v])
                        if kind == "halo":
                            A("act", lambda e: e.activation(out=vA[:, slot_v * 8:(slot_v + 1) * 8, 64], in_=ones8[:], func=AF.Copy, scale=valid_sb[:, 0:1]),
                              reads=["ones8", "valid"], writes=["vA%d" % slot_v])
                        else:
                            A("act", lambda e: e.copy(out=vA[:, slot_v * 8:(slot_v + 1) * 8, 64], in_=ones8[:]), reads=["ones8"], writes=["vA%d" % slot_v])
                    if kv_out is not None:
                        st = kvst[t % 2]
                        sk = "kvst%d" % (t % 2)
                        A("act", lambda e: e.copy(out=st[:], in_=bank[:, 0:512]), reads=[bkey], writes=[sk])
                        A("act", lambda e: e.dma_start(out=kv_out[1][kv_out[2] + t * 128:kv_out[2] + (t + 1) * 128, :], in_=st[:]), reads=[sk], dma=True,
                          is_out=True, semkey=sk + "o")
                astyle_tile(s, 8, 512, hT, "hT", t * 128, 128, vev)
            if kind == "sample":
                A("dve", lambda e: e.memset(vnew[:, :, 64:65], 1.0), writes=["aT"])
                for b in range(4):
                    def vnev(bank, bkey, b=b):
                        A("act", lambda e: e.copy(out=vnew[:, b * 8:(b + 1) * 8, 0:64], in_=bank[0:32, 0:512].rearrange("p (h d) -> p h d", h=8)),
                          reads=[bkey], writes=["aT"])
                    astyle_tile(s, 8, 512, hT, "hT", b * 32, 32, vnev)
        if kind == "halo":
            return
        if full:
            s = load_w("w_in", 0, 8, QB, 512)
            bstyle(s, 8, 4, hT, "hT", Tg, ev_copy(lambda ob: qTb[:, ob, 0:Tg], lambda ob: "qTb", scale=128.0 ** -0.5))
        s = load_w("w_in", 0, 8, KB, 512)
        if full:
            bstyle(s, 8, 4, hT, "hT", Tg, ev_copy(lambda ob: kTb[:, ob, 0:Tg], lambda ob: "kTb"))
        for t in range(ntiles):
            def kbev(bank, bkey, t=t):
                A("act", lambda e: e.copy(out=kbt[:, t, :], in_=bank[:, 0:512]), reads=[bkey], writes=["kbt"])
            astyle_tile(s, 8, 512, hT, "hT", t * 128, 128, kbev)
        for half in range(2):
            s = load_w("w_in", 0, 8, VB + half * 512, 512)
            for t in range(ntiles):
                def vbev(bank, bkey, t=t, half=half):
                    A("act", lambda e: e.copy(out=vb[:, t, half * 512:(half + 1) * 512], in_=bank[:, 0:512]), reads=[bkey], writes=["vb"])
                astyle_tile(s, 8, 512, hT, "hT", t * 128, 128, vbev)
        if full:
            for half in range(2):
                s = load_w("w_in", 0, 8, RB + half * 512, 512)
                for t in range(ntiles):
                    def rbev(bank, bkey, t=t, half=half):
                        A("act", lambda e: e.activation(out=rbs[:, t, half * 512:(half + 1) * 512], in_=bank[:, 0:512], func=AF.Silu),
                          reads=[bkey], writes=["rbs"])
                        for j in range(2):
                            c0 = half * 512 + j * 256
                            A("dve", lambda e, c0=c0: e.tensor_tensor(out=rbs[:, t, c0:c0 + 256], in0=rbs[:, t, c0:c0 + 256], in1=gn_sb[:], op=ALU.mult),
                              reads=["rbs", "gn"], writes=["rbs"])
                    astyle_tile(s, 8, 512, hT, "hT", t * 128, 128, rbev)
        s = load_w("w_in", 0, 8, DLR, 16)
        def dlev(bank, bkey, ob):
            A("act", lambda e: e.copy(out=dlrT[0:16, 0:Tg], in_=bank[0:16, 0:Tg]), reads=[bkey], writes=["dlrT"])
        bstyle(s, 8, 1, hT, "hT", Tg, dlev, mrows=16)
        if full:
            for half in range(2):
                s = load_w("w_in", 0, 8, GA + half * 512, 512)
                bstyle(s, 8, 4, hT, "hT", Tg, ev_copy(lambda ob, half=half: gaT[:, half * 4 + ob, 0:Tg], lambda ob: "gaT", func=AF.Sigmoid))
            for half in range(2):
                s = load_w("w_in", 0, 8, GB + half * 512, 512)
                bstyle(s, 8, 4, hT, "hT", Tg, ev_copy(lambda ob, half=half: gbT[:, half * 4 + ob, 0:Tg], lambda ob: "gbT", func=AF.Sigmoid))

        if kind == "pre":
            for t in range(ntiles):
                for _ in gla_tile(t, 64, True):
                    pass
            return
        if kind == "prompt":
            ibk = {"la": 4, "laT": 6, "c": 4, "at": 6, "st": 6}
            gens = []
            for t in range(ntiles):
                gens += [attn_prompt_tile(t, a0 + t), gla_tile(t, 64, False, bk=ibk)]
            att = [g_ for i_, g_ in enumerate(gens) if i_ % 2 == 0]
            gl = [g_ for i_, g_ in enumerate(gens) if i_ % 2 == 1]
            while att or gl:
                for _rep in range(2):
                    if att:
                        try:
                            next(att[0])
                        except StopIteration:
                            att.pop(0)
                if gl:
                    try:
                        next(gl[0])
                    except StopIteration:
                        gl.pop(0)
        else:
            attn_sample()
            dump("s_oaT", oaT[:, :, 0:128], [128, 4, 128], BF16, "oaT")
            dump("s_qTa", qTa[:, :, 0:128], [128, 4, 128], BF16, "qTa")
            for b in range(4):
                A("pool", lambda e, b=b: e.dma_start(out=Sbs[:, b * 4:(b + 1) * 4, :], in_=sg[b].rearrange("h d v -> d h v")),
                  writes=["poolq", "aT"], dma=True, semkey="poolq")
            for _ in gla_tile(0, 32, False, sample=True):
                pass
            dump("s_obT", obT[:, :, 0:128], [128, 8, 128], BF16, "obT")

        for half in range(2):
            s = load_w("wpa", 0, 4, half * 512, 512)
            def paev(bank, bkey, ob, half=half):
                A("dve", lambda e: e.tensor_tensor(out=gaT[:, half * 4 + ob, 0:Tg], in0=gaT[:, half * 4 + ob, 0:Tg], in1=bank[:, 0:Tg], op=ALU.mult),
                  reads=[bkey, "gaT"], writes=["gaT"])
            bstyle(s, 4, 4, oaT, "oaT", Tg, paev)
        for half in range(2):
            s = load_w("wpb", 0, 8, half * 512, 512)
            def pbev(bank, bkey, ob, half=half):
                A("dve", lambda e: e.tensor_tensor(out=gbT[:, half * 4 + ob, 0:Tg], in0=gbT[:, half * 4 + ob, 0:Tg], in1=bank[:, 0:Tg], op=ALU.mult),
                  reads=[bkey, "gbT"], writes=["gbT"])
                A("dve", lambda e: e.tensor_tensor(out=gaT[:, half * 4 + ob, 0:Tg], in0=gaT[:, half * 4 + ob, 0:Tg], in1=gbT[:, half * 4 + ob, 0:Tg], op=ALU.add),
                  reads=["gbT", "gaT"], writes=["gaT"])
            bstyle(s, 8, 4, obT, "obT", Tg, pbev)

        def a_into_fsb(wname, nk_total, src, srckey):
            kgs = []
            k = 0
            while k < nk_total:
                kgs.append((k, min(8, nk_total - k)))
                k += 8
            for cb in range(2):
                for gi, (k0, nk) in enumerate(kgs):
                    s = load_w(wname, k0, nk, cb * 512, 512)
                    for t in range(ntiles):
                        def fev(bank, bkey, t=t, cb=cb, gi=gi):
                            if gi == 0:
                                A("act", lambda e: e.copy(out=fsb[:, t, cb * 512:(cb + 1) * 512], in_=bank[:, 0:512]), reads=[bkey], writes=["fsb%d" % t])
                            else:
                                A("dve", lambda e: e.tensor_tensor(out=fsb[:, t, cb * 512:(cb + 1) * 512], in0=fsb[:, t, cb * 512:(cb + 1) * 512],
                                                                   in1=bank[:, 0:512], op=ALU.add), reads=[bkey, "fsb%d" % t], writes=["fsb%d" % t])
                        astyle_tile(s, nk, 512, src, srckey, t * 128, 128, fev, kc_off=k0)

        def norm_residual(t, g_sb, gkey, dst, dkey):
            A("act", lambda e: e.activation(out=junk[:], in_=fsb[:, t, :], func=AF.Square, accum_out=ss[:, 0:1]), reads=["fsb%d" % t], writes=["junk", "ss"])
            A("act", lambda e: e.activation(out=rstd[:, 0:1], in_=ss[:, 0:1], func=AF.Ln, scale=1.0 / D, bias=EPS), reads=["ss"], writes=["rstd"])
            A("act", lambda e: e.activation(out=rstd[:, 0:1], in_=rstd[:, 0:1], func=AF.Exp, scale=-0.5), reads=["rstd"], writes=["rstd"])
            A("dve", lambda e: e.scalar_tensor_tensor(out=tmpf[:], in0=fsb[:, t, :], scalar=rstd[:, 0:1], in1=g_sb[:], op0=ALU.mult, op1=ALU.mult),
              reads=["fsb%d" % t, "rstd", gkey], writes=["tmpf"])
            A("dve", lambda e: e.tensor_tensor(out=dst, in0=tmpf[:], in1=xg[:, xpar * G + t, :], op=ALU.add), reads=["tmpf", "xg%d" % (xpar * G + t)],
              writes=[dkey])

        if kind == "sample":
            dump("s_mixT", gaT[:, :, 0:128], [128, 8, 128], BF16, "gaT")
        a_into_fsb("wout", 8, gaT, "gaT")
        TT = list(range(ntiles))
        xi_ = lambda t: xpar * G + t
        for t in TT:
            A("act", lambda e, t=t: e.activation(out=junk[:], in_=fsb[:, t, :], func=AF.Square, accum_out=ssn[:, t:t + 1]),
              reads=["fsb%d" % t], writes=["junk", "ssn%d" % t])
        for t in TT:
            A("act", lambda e, t=t: e.activation(out=rstdn[:, t:t + 1], in_=ssn[:, t:t + 1], func=AF.Ln, scale=1.0 / D, bias=EPS),
              reads=["ssn%d" % t], writes=["rstdn%d" % t])
        for t in TT:
            A("act", lambda e, t=t: e.activation(out=rstdn[:, t:t + 1], in_=rstdn[:, t:t + 1], func=AF.Exp, scale=-0.5),
              reads=["rstdn%d" % t], writes=["rstdn%d" % t])
        for t in TT:
            A("dve", lambda e, t=t: e.scalar_tensor_tensor(out=fsb[:, t, :], in0=fsb[:, t, :], scalar=rstdn[:, t:t + 1], in1=gpost_sb[:], op0=ALU.mult, op1=ALU.mult),
              reads=["fsb%d" % t, "rstdn%d" % t, "gpost"], writes=["fsb%d" % t])
        for t in TT:
            A("dve", lambda e, t=t: e.tensor_tensor(out=xg[:, xi_(t), :], in0=fsb[:, t, :], in1=xg[:, xi_(t), :], op=ALU.add),
              reads=["fsb%d" % t, "xg%d" % xi_(t)], writes=["xg%d" % xi_(t)])
        for t in TT:
            A("act", lambda e, t=t: e.activation(out=junk[:], in_=xg[:, xi_(t), :], func=AF.Square, accum_out=ssn[:, 2 + t:3 + t]),
              reads=["xg%d" % xi_(t)], writes=["junk", "ssn%d" % (2 + t)])
        for t in TT:
            A("act", lambda e, t=t: e.activation(out=rstdn[:, 2 + t:3 + t], in_=ssn[:, 2 + t:3 + t], func=AF.Ln, scale=1.0 / D, bias=EPS),
              reads=["ssn%d" % (2 + t)], writes=["rstdn%d" % (2 + t)])
        for t in TT:
            A("act", lambda e, t=t: e.activation(out=rstdn[:, 2 + t:3 + t], in_=rstdn[:, 2 + t:3 + t], func=AF.Exp, scale=-0.5),
              reads=["rstdn%d" % (2 + t)], writes=["rstdn%d" % (2 + t)])
        for t in TT:
            hbt = hb if t == 0 else hb2
            A("dve", lambda e, t=t, hbt=hbt: e.tensor_scalar(out=hbt[:], in0=xg[:, xi_(t), :], scalar1=rstdn[:, 2 + t:3 + t], scalar2=None, op0=ALU.mult),
              reads=["xg%d" % xi_(t), "rstdn%d" % (2 + t)], writes=["hb" if t == 0 else "hb2"])
        for t in TT:
            hbt = hb if t == 0 else hb2
            hk = "hb" if t == 0 else "hb2"
            for kc in range(8):
                A("pe", lambda e, kc=kc, hbt=hbt: e.transpose(out=pst[:, kc, :], in_=hbt[:, kc * 128:(kc + 1) * 128], identity=ident[:]),
                  reads=[hk, "ident"], writes=["pst"])
            for kc in range(8):
                A("act", lambda e, kc=kc, t=t: e.activation(out=hT[:, kc, t * 128:(t + 1) * 128], in_=pst[:, kc, :], func=AF.Copy,
                                                            scale=gffn_sb[:, kc:kc + 1]), reads=["pst", "gffn"], writes=["hT"])

        for it_, fb4 in enumerate(range(0, 22, 4)):
            nb = min(4, 22 - fb4)
            sg_ = load_w("wg", 0, 8, fb4 * 128, nb * 128, slot=(2 * it_) % 4)
            su_ = load_w("wu", 0, 8, fb4 * 128, nb * 128, slot=(2 * it_ + 1) % 4)
            for ob in range(nb):
                bg = 0 if ob % 2 == 0 else 2
                bu = 1 if ob % 2 == 0 else 3
                for kc in range(8):
                    A("pe", lambda e, kc=kc, ob=ob, sg_=sg_, bg=bg: e.matmul(ps[bg][:, 0:Tg], lhsT=wblk[sg_][:, kc, ob * 128:(ob + 1) * 128], rhs=hT[:, kc, 0:Tg],
                                                            start=(kc == 0), stop=(kc == 7)), reads=slotkeys[sg_] + ["hT"], writes=["ps%d" % bg])
                for kc in range(8):
                    A("pe", lambda e, kc=kc, ob=ob, su_=su_, bu=bu: e.matmul(ps[bu][:, 0:Tg], lhsT=wblk[su_][:, kc, ob * 128:(ob + 1) * 128], rhs=hT[:, kc, 0:Tg],
                                                            start=(kc == 0), stop=(kc == 7)), reads=slotkeys[su_] + ["hT"], writes=["ps%d" % bu])
                A("act", lambda e, bg=bg: e.activation(out=junk[:, 0:Tg], in_=ps[bg][:, 0:Tg], func=AF.Silu), reads=["ps%d" % bg], writes=["junk"])
                A("dve", lambda e, ob=ob, fb4=fb4, bu=bu: e.tensor_tensor(out=aT[:, fb4 + ob, 0:Tg], in0=junk[:, 0:Tg], in1=ps[bu][:, 0:Tg], op=ALU.mult),
                  reads=["junk", "ps%d" % bu], writes=["aT"])
        if next_prep is not None:
            next_prep()
        a_into_fsb("wd", 22, aT, "aT")
        for t in range(ntiles):
            yst = ystage[t % 2]
            yk = "yst%d" % (t % 2)
            norm_residual(t, gfpost_sb, "gfpost", yst[:, 0:D], yk)
            A("act", lambda e, t=t, yst=yst: e.dma_start(out=out_ap[row0 + t * 128:row0 + (t + 1) * 128, :], in_=yst[:, 0:D]), reads=[yk], dma=True,
              is_out=True, semkey=yk + "o")

    A("sp", lambda e: e.dma_start(out=dlrT[16:32, :], in_=ones_row[:, :]), writes=["dlrT"], dma=True, semkey="dlr1")

    A("dve", lambda e: e.tensor_copy(out=wup_bf[:], in_=wup_sb[:]), reads=["wup"], writes=["wup_bf"])
    A("dve", lambda e: e.tensor_copy(out=utf_bf[:], in_=cm[64][:, 640:768]), reads=["cm64"], writes=["utf_bf"])
    A("dve", lambda e: e.memset(onesb[:], 1.0), writes=["onesb"])
    skb, sv0, sv1 = 0, 1, 2
    wctr[0] = 3

    xgf = xg[:].rearrange("p a b -> p (a b)")
    w_in3 = w_in.rearrange("(k p) n -> p k n", p=128)
    for slot_, c0_ in ((skb, KB), (sv0, VB), (sv1, VB + 512)):
        for half_ in range(2):
            A("sp", lambda e, c0_=c0_, half_=half_: e.dma_start(out=xgf[:, 0:2048].rearrange("p (k n) -> p k n", k=4),
                                                                  in_=w_in3[:, half_ * 4:(half_ + 1) * 4, c0_:c0_ + 512]),
              writes=["xg0", "xg1"], dma=True, semkey="xg0")
            for j in range(4):
                kc = half_ * 4 + j
                if j % 2 == 0:
                    A("act", lambda e, slot_=slot_, kc=kc, j=j: e.activation(out=wblk[slot_][:, kc, :], in_=xgf[:, j * 512:(j + 1) * 512], func=AF.Copy,
                                                                            scale=gpre_sb[:, kc:kc + 1]),
                      reads=["xg0", "xg1", "gpre"], writes=["w%d" % slot_])
                else:
                    A("dve", lambda e, slot_=slot_, kc=kc, j=j: e.tensor_scalar(out=wblk[slot_][:, kc, :], in0=xgf[:, j * 512:(j + 1) * 512],
                                                                                scalar1=gpre_sb[:, kc:kc + 1], scalar2=None, op0=ALU.mult),
                      reads=["xg0", "xg1", "gpre"], writes=["w%d" % slot_])
    A("sp", lambda e: e.dma_start(out=tmpf[:, 0:128].rearrange("p (k n) -> p k n", k=8), in_=w_in3[:, :, DLR:DLR + 16]), writes=["tmpf"], dma=True,
      semkey="tmpfw")
    for kc in range(8):
        A("dve", lambda e, kc=kc: e.tensor_scalar(out=wdlr[:, kc, :], in0=tmpf[:, kc * 16:(kc + 1) * 16], scalar1=gpre_sb[:, kc:kc + 1], scalar2=None,
                                                  op0=ALU.mult), reads=["tmpf", "gpre"], writes=["wdlr"])
    A("dve", lambda e: e.memset(ssp[:], 0.0), reads=["tmpf"], writes=["tmpf0", "tmpf1", "ssp0", "ssp1"])

    def pk(name, i, depth=2):
        return "pp_%s%d" % (name, i % depth)

    def pp_s0(i):
        p = i % 2
        xt = xg[:, p, :]
        A("sp", lambda e: e.dma_start(out=xt, in_=xpre[i * 128:(i + 1) * 128, :]), writes=["xg%d" % p], dma=True)
        A("act", lambda e: e.activation(out=junk[:], in_=xt, func=AF.Square, accum_out=ssp[:, p:p + 1]), reads=["xg%d" % p], writes=["junk", "ssp%d" % p])
        A("act", lambda e: e.activation(out=rstdp[:, p:p + 1], in_=ssp[:, p:p + 1], func=AF.Ln, scale=1.0 / D, bias=EPS),
          reads=["ssp%d" % p], writes=["rstdp%d" % p])
        A("act", lambda e: e.activation(out=rstdp[:, p:p + 1], in_=rstdp[:, p:p + 1], func=AF.Exp, scale=-0.5),
          reads=["rstdp%d" % p], writes=["rstdp%d" % p])
        A("dve", lambda e: e.tensor_scalar(out=pp_hb[p], in0=xt, scalar1=rstdp[:, p:p + 1], scalar2=None, op0=ALU.mult),
          reads=["xg%d" % p, "rstdp%d" % p], writes=[pk("hb", i)])

    def pp_s1(i):
        p = i % 2
        for kc in range(8):
            A("pe", lambda e, kc=kc: e.transpose(out=pst[:, kc, :], in_=pp_hb[p][:, kc * 128:(kc + 1) * 128], identity=ident[:]),
              reads=[pk("hb", i), "ident"], writes=["pst"])
        if p == 0:
            A("dve", lambda e: e.tensor_copy(out=pp_hT[p], in_=pst[:, :, :]), reads=["pst"], writes=[pk("hT", i)])
        else:
            A("act", lambda e: e.copy(out=pp_hT[p], in_=pst[:, :, :]), reads=["pst"], writes=[pk("hT", i)])

    def pp_s2(i):
        p = i % 2
        q4 = i % 4
        for (slot, bank) in ((skb, 0), (sv0, 1), (sv1, 2)):
            for kc in range(8):
                A("pe", lambda e, kc=kc, slot=slot, bank=bank: e.matmul(ps[bank][:, :], lhsT=pp_hT[p][:, kc, :], rhs=wblk[slot][:, kc, :],
                                                                        start=(kc == 0), stop=(kc == 7)),
                  reads=[pk("hT", i), "w%d" % slot], writes=["ps%d" % bank])
        A("act", lambda e: e.copy(out=pp_kbt[q4], in_=ps[0][:, :]), reads=["ps0"], writes=[pk("kbt", i, 4)])
        A("act", lambda e: e.copy(out=pp_vb[q4][:, 0:512], in_=ps[1][:, :]), reads=["ps1"], writes=[pk("vba", i, 4)])
        A("dve", lambda e: e.tensor_copy(out=pp_vb[q4][:, 512:1024], in_=ps[2][:, :]), reads=["ps2"], writes=[pk("vbb", i, 4)])

    def pp_s2b(i):
        p = i % 2
        for kc in range(8):
            A("pe", lambda e, kc=kc: e.matmul(ps[3][0:16, 0:128], lhsT=wdlr[:, kc, :], rhs=pp_hT[p][:, kc, :], start=(kc == 0), stop=(kc == 7)),
              reads=[pk("hT", i), "wdlr"], writes=["ps3"])
        A("act", lambda e: e.copy(out=dlrTb[0:16, p * 128:(p + 1) * 128], in_=ps[3][0:16, 0:128]), reads=["ps3"], writes=[pk("dl", i)])

    def pp_s3(i):
        p = i % 2
        A("pe", lambda e: e.matmul(ps[4][:, :], lhsT=dlrTb[0:32, p * 128:(p + 1) * 128], rhs=wup_bf[0:32, :], start=True, stop=True),
          reads=[pk("dl", i), "dlrTb", "wup_bf"], writes=["ps4"])
        A("act", lambda e: e.activation(out=tmpf[:, p * 512:(p + 1) * 512], in_=ps[4][:, :], func=AF.Exp, scale=-1.0), reads=["ps4"], writes=["tmpf%d" % p])
        A("act", lambda e: e.activation(out=pp_la[p], in_=tmpf[:, p * 512:(p + 1) * 512], func=AF.Ln, bias=1.0), reads=["tmpf%d" % p], writes=[pk("la", i)])

    def pp_s4(i):
        p = i % 2
        q4 = i % 4
        A("pe", lambda e: e.matmul(ps[5][:, :], lhsT=utf_bf[:], rhs=pp_la[p], start=True, stop=True), reads=[pk("la", i), "utf_bf"], writes=["ps5"])
        for h in range(4):
            A("pe", lambda e, h=h: e.matmul(ps[3][:, 256 + h * 2:258 + h * 2], lhsT=pp_la[p][:, h * 128:(h + 1) * 128], rhs=onesb[:, 0:2],
                                            start=True, stop=True), reads=[pk("la", i), "onesb"], writes=["ps3"])
        A("act", lambda e: e.activation(out=pp_ec[p], in_=ps[5][:, :], func=AF.Exp, scale=-1.0 / 16), reads=["ps5"], writes=[pk("ec", i)])
        A("act", lambda e: e.activation(out=etile[:, p, :], in_=ps[3][:, 256:264].rearrange("p (h n) -> p h n", n=2)[:, :, 0], func=AF.Exp,
                                        scale=-1.0 / 16), reads=["ps3"], writes=["etile%d" % p])
        A("dve", lambda e: e.tensor_tensor(out=pp_khat[p], in0=pp_kbt[q4], in1=pp_ec[p], op=ALU.mult), reads=[pk("kbt", i, 4), pk("ec", i)],
          writes=[pk("kh", i)])

    def pp_s5(i):
        p = i % 2
        q4 = i % 4
        for hp in range(2):
            bank = 6 if hp == 0 else 0
            for h in (2 * hp, 2 * hp + 1):
                A("pe", lambda e, h=h, bank=bank: e.matmul(ps[bank][:, (h % 2) * 256:(h % 2 + 1) * 256], lhsT=pp_khat[p][:, h * 128:(h + 1) * 128],
                                                           rhs=pp_vb[q4][:, h * 256:(h + 1) * 256], start=True, stop=True),
                  reads=[pk("kh", i), pk("vba", i, 4), pk("vbb", i, 4)], writes=["ps%d" % bank])
        for hp in range(2):
            bank = 6 if hp == 0 else 0
            for h in (2 * hp, 2 * hp + 1):
                A("dve", lambda e, h=h, bank=bank: e.scalar_tensor_tensor(out=S[:, h, :], in0=S[:, h, :], scalar=etile[:, p, h:h + 1],
                                                                          in1=ps[bank][:, (h % 2) * 256:(h % 2 + 1) * 256], op0=ALU.mult, op1=ALU.add),
                  reads=["S%d" % h, "etile%d" % p, "ps%d" % bank], writes=["S%d" % h])

    NPRE = 7 * NT_P
    for n in range(-2, NPRE + 3):
        for st, off in ((pp_s0, 2), (pp_s1, 1), (pp_s2, 0), (pp_s5, -3), (pp_s2b, 0), (pp_s3, -1), (pp_s4, -2)):
            i = n + off
            if 0 <= i < NPRE:
                st(i)
    A("act", lambda e: e.copy(out=Sbf[0][:], in_=S[:]), reads=["S%d" % h for h in range(4)] + ["S"], writes=["Sbf0"])
    ppkeys = ["S%d" % h for h in range(4)] + ["junk", "tmpf0", "tmpf1", "wdlr", "dlrTb", "wup_bf", "utf_bf", "onesb"]
    for p_ in range(4):
        ppkeys += ["pp_kbt%d" % p_, "pp_vba%d" % p_, "pp_vbb%d" % p_]
    for p_ in range(2):
        ppkeys += ["pp_hb%d" % p_, "pp_hT%d" % p_, "pp_la%d" % p_, "pp_ec%d" % p_, "pp_kh%d" % p_, "pp_dl%d" % p_, "etile%d" % p_, "ssp%d" % p_, "rstdp%d" % p_]
    A("dve", lambda e: e.memset(dlrT[0:16, :], 0.0), reads=ppkeys,
      writes=["dlrT", "aT", "obT", "gaT", "gbT", "hT", "tmpf", "fsb0", "fsb1", "S", "ps6", "pst", "ps0", "ps1", "ps2", "ps3", "ps4", "ps5", "xg0", "xg1"])
    for g in range(4 // G):
        run_group("halo", xh, g * T, G, a0=-4 + g * G)
    run_group("sample", xs, 0, 1, out_ap=ys, kv_out=(ksn, vsn, 0))
    for i in range(2):
        A("dve", lambda e, i=i: e.memset(qz[i][:], 0.0), writes=["qz%d" % i])
    ngroups = NT_P // G
    for g in range(ngroups):
        rbase = (g * G - (NT_P - 4)) * 128
        nxt = None
        if g + 1 < ngroups:
            nxt = (lambda g=g: prep_group(xp, (g + 1) * T, G, (g + 1) % 2))
        run_group("prompt", xp, g * T, G, a0=g * G, out_ap=yp, kv_out=(kp, vp, rbase) if rbase >= 0 else None,
                  xpar=g % 2, prefetched=(g > 0), next_prep=nxt)
    A("act", lambda e: e.dma_start(out=spo[:, :], in_=S[:].rearrange("p h v -> p (h v)")), reads=["S"], dma=True, is_out=True, semkey="spo")
    P.finish()
    P.emit()
    return nc, P


def _const_masks(c):
    n = 128
    s = np.arange(n)[:, None]
    t = np.arange(n)[None, :]
    same = (s // c) == (t // c)
    UT = ((s > t) & same).astype(np.float32)
    maskT = ((s <= t) & same).astype(np.float32)
    restart = np.ones((128, 4, 128), np.float32)
    restart[:, :, ::c] = 0.0
    nch = n // c
    rm = np.zeros((128, nch), np.float32)
    for i in range(nch):
        rm[i * c:(i + 1) * c, i] = 1.0
    out = np.zeros((128, 6 * 128 + 512 + nch), np.float32)
    out[:, 0:128] = UT
    out[:, 128:640] = np.tile(maskT, (1, 4))
    out[:, 640:768] = (s > t).astype(np.float32)
    out[:, 768:1280] = restart.reshape(128, 512)
    out[:, 1280:1280 + nch] = rm
    return out


def _bias_tables(rel):
    p = np.arange(128)[:, None]
    col = np.arange(128)[None, :]
    kc, kl = p // 64, p % 64
    qc, ql = col // 64, col % 64
    idxP = np.zeros((3, 128, 128), np.int64)
    mk = np.zeros((128, 3, 128), np.float32)
    for j, d in enumerate((0, 1, 4)):
        o = 2 * d + qc - kc
        dist = 64 * o + ql - kl
        idxP[j] = np.clip(dist, -128, 128) + 128
        mk[:, j, :] = np.where((o >= 0) & (o <= 8), 0.0, NEG)
    biasP = rel[:, idxP]
    biasP = np.ascontiguousarray(biasP.transpose(2, 0, 1, 3)).reshape(128, 8 * 384)
    q32 = np.arange(32)[None, :]
    d0 = q32 + 512 - (384 + p)
    d1 = q32 - np.minimum(p, 31)
    idxS = np.stack([np.clip(d0, -128, 128) + 128, np.clip(d1, -128, 128) + 128], 0)
    biasS = rel[:, idxS]
    biasS = np.ascontiguousarray(biasS.transpose(2, 0, 1, 3)).reshape(128, 8 * 64)
    cb = np.ascontiguousarray(np.broadcast_to(rel[:, 256][None, :], (128, 8)))
    return biasP.astype(np.float32), mk.reshape(128, 384), biasS.astype(np.float32), cb.astype(np.float32)


_CACHE = {}


def kernel(x_prompt, x_sample, cache_attn_k, cache_attn_v, state_gla,
           norm_mix_pre, norm_mix_post, norm_ffn_pre, norm_ffn_post,
           w_in, w_decay_up, b_decay, rel_bias, gla_norm, w_proj_a, w_proj_b, w_out,
           w_ffn_gate, w_ffn_up, w_ffn_down):
    f = lambda a: np.ascontiguousarray(np.asarray(a, dtype=np.float32))
    x_prompt, x_sample = f(x_prompt), f(x_sample)
    if "nc" not in _CACHE:
        _CACHE["nc"] = build_program()
    nc, P = _CACHE["nc"]
    xpf = x_prompt.reshape(SEQ, D)
    wup = np.zeros((32, 512), np.float32)
    wup[0:16] = f(w_decay_up)[0]
    wup[16] = f(b_decay)[0]
    ones_row = np.zeros((16, G * 128), np.float32)
    ones_row[0] = 1.0
    bP, mkP, bS, cb = _bias_tables(f(rel_bias)[0])
    shared = {
        "w_in": f(w_in)[0], "wup": wup,
        "gpre": np.ascontiguousarray(f(norm_mix_pre)[0].reshape(8, 128).T),
        "gffn": np.ascontiguousarray(f(norm_ffn_pre)[0].reshape(8, 128).T),
        "gpost": f(norm_mix_post)[0].reshape(1, D), "gfpost": f(norm_ffn_post)[0].reshape(1, D),
        "gnorm": f(gla_norm)[0].reshape(1, 256),
        "wpa": f(w_proj_a)[0], "wpb": f(w_proj_b)[0], "wout": f(w_out)[0],
        "wg": f(w_ffn_gate)[0], "wu": f(w_ffn_up)[0], "wd": f(w_ffn_down)[0],
        "biasP": bP, "mkP": mkP, "biasS": bS, "cbias": cb,
        "cm64": _const_masks(64), "cm32": _const_masks(32), "ones_row": ones_row,
    }
    ck = f(cache_attn_k)[0].reshape(32, 512, 512)
    cv = f(cache_attn_v)[0].reshape(32, 512, 512)
    sgl = f(state_gla)[0]
    in_maps = []
    for c in range(NCORES):
        m = dict(shared)
        m["xp"] = xpf[c * TOK_CORE:(c + 1) * TOK_CORE]
        m["xh"] = xpf[c * TOK_CORE - 512:c * TOK_CORE] if c > 0 else np.zeros((512, D), np.float32)
        xpre = np.zeros((7 * TOK_CORE, D), np.float32)
        if c > 0:
            xpre[(7 - c) * TOK_CORE:] = xpf[0:c * TOK_CORE]
        m["xpre"] = xpre
        m["xs"] = x_sample[4 * c:4 * c + 4].reshape(128, D)
        m["ck"] = ck[4 * c:4 * c + 4]
        m["cv"] = cv[4 * c:4 * c + 4]
        m["sg"] = sgl[4 * c:4 * c + 4]
        m["valid"] = np.full((128, 1), 1.0 if c > 0 else 0.0, np.float32)
        in_maps.append(m)
    res = run_bass_kernel_spmd(nc, in_maps, core_ids=list(range(NCORES)))
    R = res.results
    yp = np.concatenate([R[c]["yp"] for c in range(NCORES)], 0).reshape(1, SEQ, D)
    ys = np.concatenate([R[c]["ys"] for c in range(NCORES)], 0).reshape(32, 32, D)
    kpo = R[NCORES - 1]["kp"].reshape(1, 1, 512, 8, 64)
    vpo = R[NCORES - 1]["vp"].reshape(1, 1, 512, 8, 64)
    spo = R[NCORES - 1]["spo"].reshape(128, 4, 256).transpose(1, 0, 2).reshape(1, 1, 4, 128, 256)
    kso = np.concatenate([R[c]["ksn"] for c in range(NCORES)], 0).reshape(1, 32, 32, 8, 64)
    vso = np.concatenate([R[c]["vsn"] for c in range(NCORES)], 0).reshape(1, 32, 32, 8, 64)
    sso = np.concatenate([R[c]["sso"] for c in range(NCORES)], 0).reshape(32, 128, 4, 256).transpose(0, 2, 1, 3).reshape(1, 32, 4, 128, 256)
    return (yp, ys, kpo, vpo, np.ascontiguousarray(spo), kso, vso, np.ascontiguousarray(sso))
```
